# Optimizing a Trainium2 kernel written in Bass

```python
import math
import jax
import jax.numpy as jnp
from jax import lax
import numpy as np

D_MODEL = 1024
BATCH = 8
SEQ = 2048
DEPTH = 2

D_MIX = D_MODEL
GROUP_W = D_MIX // 4
HEAD_DIM = 64
EPS = 1e-6
NEG_INF = -1e30

MLA_HEADS = GROUP_W // HEAD_DIM
MLA_NOPE = 64
MLA_ROPE = 32
MLA_V = GROUP_W // MLA_HEADS
MLA_Q_RANK = GROUP_W
MLA_KV_RANK = GROUP_W // 2
ROPE_THETA = 10000.0
Q_BLOCK = 128

S5_GROUP_CH = 16
S5_GROUPS = GROUP_W // S5_GROUP_CH
S5_STATE = 64
S5_DT_MIN = 1e-3
S5_DT_MAX = 1e-1

DIL_HEADS = GROUP_W // HEAD_DIM
DIL_PAIRS = ((128, 1), (512, 4), (2048, 16))
T5_BUCKETS = 32
T5_MAX_DIST = 2048

DN_HEADS = GROUP_W // HEAD_DIM
DN_DK = HEAD_DIM
DN_DV = HEAD_DIM
DN_CONV = 4
DN_CHUNK = 64
DN_DT_MIN = 1e-3
DN_DT_MAX = 1e-1

FFN_HIDDEN = -(-8 * D_MODEL // (3 * 256)) * 256

IN_SPLITS = (MLA_Q_RANK, MLA_KV_RANK, MLA_ROPE, GROUP_W, 3 * GROUP_W, 3 * GROUP_W, DN_HEADS, DN_HEADS, GROUP_W)
IN_COLS = MLA_Q_RANK + MLA_KV_RANK + MLA_ROPE + 8 * GROUP_W + 2 * DN_HEADS

kernel_name = 'hybrid_parallel_mixer_trunk'


def rms_norm(x, g):
    xf = x.astype(jnp.float32)
    y = xf * lax.rsqrt(jnp.mean(xf * xf, axis=-1, keepdims=True) + EPS)
    return (y * g.astype(jnp.float32)).astype(x.dtype)


def l2_norm(x):
    return x * lax.rsqrt(jnp.sum(x * x, axis=-1, keepdims=True) + EPS)


def split_cols(t, sizes):
    out, start = [], 0
    for s in sizes:
        out.append(t[..., start:start + s])
        start += s
    return out


def apply_rope(x, pos):
    half = x.shape[-1] // 2
    freqs = ROPE_THETA ** (-jnp.arange(half, dtype=jnp.float32) / half)
    ang = pos[:, None] * freqs[None, :]
    cos = jnp.cos(ang)[None, :, None, :]
    sin = jnp.sin(ang)[None, :, None, :]
    xf = x.astype(jnp.float32)
    x1, x2 = xf[..., :half], xf[..., half:]
    return jnp.concatenate([x1 * cos - x2 * sin, x1 * sin + x2 * cos], axis=-1).astype(x.dtype)


def causal_block_attention(q, k, v, scale):
    B, S, H, Dq = q.shape
    nb = S // Q_BLOCK
    qb = q.reshape(B, nb, Q_BLOCK, H, Dq).transpose(1, 0, 2, 3, 4)
    starts = jnp.arange(nb, dtype=jnp.int32) * Q_BLOCK
    kpos = jnp.arange(S, dtype=jnp.int32)

    def block(args):
        q_blk, s0 = args
        logits = jnp.einsum('bqhd,bkhd->bhqk', q_blk, k).astype(jnp.float32) * scale
        qpos = s0 + jnp.arange(Q_BLOCK, dtype=jnp.int32)
        mask = kpos[None, :] <= qpos[:, None]
        logits = jnp.where(mask[None, None], logits, NEG_INF)
        p = jax.nn.softmax(logits, axis=-1).astype(v.dtype)
        return jnp.einsum('bhqk,bkhd->bqhd', p, v)

    out = lax.map(block, (qb, starts))
    return out.transpose(1, 0, 2, 3, 4).reshape(B, S, H, v.shape[-1])


def mla_mixer(c_q, c_kv, k_rope, q_norm, kv_norm, w_uq, w_ukv, qk_q, qk_k):
    B, S, _ = c_q.shape
    H = MLA_HEADS
    dqk = MLA_NOPE + MLA_ROPE
    q = (rms_norm(c_q, q_norm) @ w_uq).reshape(B, S, H, dqk)
    kv = (rms_norm(c_kv, kv_norm) @ w_ukv).reshape(B, S, H, MLA_NOPE + MLA_V)
    k_nope, v = kv[..., :MLA_NOPE], kv[..., MLA_NOPE:]
    k = jnp.concatenate([k_nope, jnp.broadcast_to(k_rope[:, :, None, :], (B, S, H, MLA_ROPE))], axis=-1)
    q = rms_norm(q, qk_q)
    k = rms_norm(k, qk_k)
    pos = jnp.arange(S, dtype=jnp.float32)
    q = jnp.concatenate([q[..., :MLA_NOPE], apply_rope(q[..., MLA_NOPE:], pos)], axis=-1)
    k = jnp.concatenate([k[..., :MLA_NOPE], apply_rope(k[..., MLA_NOPE:], pos)], axis=-1)
    out = causal_block_attention(q, k, v, dqk ** -0.5)
    return out.reshape(B, S, H * MLA_V)


def complex_affine_combine(e1, e2):
    a1r, a1i, b1r, b1i = e1
    a2r, a2i, b2r, b2i = e2
    ar = a1r * a2r - a1i * a2i
    ai = a1r * a2i + a1i * a2r
    br = a2r * b1r - a2i * b1i + b2r
    bi = a2r * b1i + a2i * b1r + b2i
    return ar, ai, br, bi


def s5_mixer(u, lam_re, lam_im, log_dt, b_re, b_im, c_re, c_im, d_skip, w_glu):
    B, S, W = u.shape
    G, CG = S5_GROUPS, S5_GROUP_CH
    f32 = jnp.float32
    uf = u.astype(f32).reshape(B, S, G, CG)
    lr, li = lam_re.astype(f32), lam_im.astype(f32)
    dt = jnp.exp(log_dt.astype(f32))[:, None]
    mag = jnp.exp(lr * dt)
    ar, ai = mag * jnp.cos(li * dt), mag * jnp.sin(li * dt)
    den = lr * lr + li * li
    nr, ni = ar - 1.0, ai
    zr = (nr * lr + ni * li) / den
    zi = (ni * lr - nr * li) / den
    br, bi = b_re.astype(f32), b_im.astype(f32)
    bbr = zr[..., None] * br - zi[..., None] * bi
    bbi = zr[..., None] * bi + zi[..., None] * br
    xr = jnp.einsum('gpc,bsgc->bsgp', bbr, uf)
    xi = jnp.einsum('gpc,bsgc->bsgp', bbi, uf)
    ar_f = jnp.broadcast_to(ar, xr.shape)
    ai_f = jnp.broadcast_to(ai, xr.shape)
    _, _, hr, hi = lax.associative_scan(complex_affine_combine, (ar_f, ai_f, xr, xi), axis=1)
    y = jnp.einsum('gcp,bsgp->bsgc', c_re.astype(f32), hr) - jnp.einsum('gcp,bsgp->bsgc', c_im.astype(f32), hi)
    y = y.reshape(B, S, W) + d_skip.astype(f32) * u.astype(f32)
    y = y.astype(u.dtype)
    val, gate = jnp.split(y @ w_glu, 2, axis=-1)
    return val * jax.nn.sigmoid(gate)


def t5_bucket(dist):
    exact = T5_BUCKETS // 2
    df = jnp.maximum(dist, 1).astype(jnp.float32)
    large = exact + (jnp.log(df / exact) / math.log(T5_MAX_DIST / exact) * (T5_BUCKETS - exact)).astype(jnp.int32)
    large = jnp.minimum(large, T5_BUCKETS - 1)
    return jnp.where(dist < exact, dist, large)


def dilated_branch(q, k, v, bias_table, window, dilation, scale):
    B, S, H, D = q.shape
    span = window // dilation
    L = S // dilation
    nb = -(-L // span)
    Lp = nb * span

    def to_blocks(t):
        t = t.reshape(B, L, dilation, H, D).transpose(0, 2, 1, 3, 4)
        t = jnp.pad(t, ((0, 0), (0, 0), (0, Lp - L), (0, 0), (0, 0)))
        return t.reshape(B, dilation, nb, span, H, D)

    def with_prev(t):
        prev = jnp.pad(t, ((0, 0), (0, 0), (1, 0), (0, 0), (0, 0), (0, 0)))[:, :, :-1]
        return jnp.concatenate([prev, t], axis=3)

    qb = to_blocks(q)
    kk = with_prev(to_blocks(k))
    vv = with_prev(to_blocks(v))
    qi = jnp.arange(span, dtype=jnp.int32)[:, None] + span
    kj = jnp.arange(2 * span, dtype=jnp.int32)[None, :]
    delta = qi - kj
    band = (delta >= 0) & (delta <= span)
    before_start = (jnp.arange(nb)[:, None, None] == 0) & (kj < span)[None]
    valid = band[None] & (~before_start)
    bias = bias_table[t5_bucket(jnp.clip(delta, 0, span) * dilation)]
    bias = bias.transpose(2, 0, 1).astype(jnp.float32)
    logits = jnp.einsum('bgnqhd,bgnkhd->bgnhqk', qb, kk).astype(jnp.float32) * scale + bias
    logits = jnp.where(valid[None, None, :, None], logits, NEG_INF)
    m = jnp.max(logits, axis=-1)
    p = jnp.exp(logits - m[..., None])
    l = jnp.sum(p, axis=-1)
    o = jnp.einsum('bgnhqk,bgnkhd->bgnqhd', p.astype(v.dtype), vv).astype(jnp.float32)

    def from_blocks(t):
        t = t.reshape(B, dilation, Lp, *t.shape[4:])[:, :, :L]
        t = jnp.moveaxis(t, 1, 2)
        return t.reshape(B, S, *t.shape[3:])

    return from_blocks(o), from_blocks(jnp.swapaxes(m, -1, -2)), from_blocks(jnp.swapaxes(l, -1, -2))


def dilated_mixer(qkv, q_norm, k_norm, bias_table):
    B, S, _ = qkv.shape
    q, k, v = [t.reshape(B, S, DIL_HEADS, HEAD_DIM) for t in jnp.split(qkv, 3, axis=-1)]
    q = rms_norm(q, q_norm)
    k = rms_norm(k, k_norm)
    branches = [dilated_branch(q, k, v, bias_table, w, d, HEAD_DIM ** -0.5) for (w, d) in DIL_PAIRS]
    m_all = jnp.stack([br[1] for br in branches])
    l_all = jnp.stack([br[2] for br in branches])
    o_all = jnp.stack([br[0] for br in branches])
    wts = jnp.exp(m_all - jnp.max(m_all, axis=0, keepdims=True))
    num = jnp.sum(wts[..., None] * o_all, axis=0)
    den = jnp.sum(wts * l_all, axis=0)
    return (num / den[..., None]).astype(qkv.dtype).reshape(B, S, DIL_HEADS * HEAD_DIM)


def causal_depthwise_conv(x, w):
    K, C = w.shape
    return lax.conv_general_dilated(x, w[:, None, :].astype(x.dtype), window_strides=(1,),
                                    padding=[(K - 1, 0)], dimension_numbers=('NWC', 'WIO', 'NWC'),
                                    feature_group_count=C)


def chunked_gated_delta(q, k, v, g, beta):
    B, S, H, DK = q.shape
    DV = v.shape[-1]
    C = DN_CHUNK
    N = S // C

    def chunks(t):
        return jnp.moveaxis(t.reshape(B, N, C, H, *t.shape[3:]), 3, 1)

    q, k, v, g, beta = chunks(q), chunks(k), chunks(v), chunks(g), chunks(beta)
    gc = jnp.cumsum(g, axis=-1)
    causal = jnp.tril(jnp.ones((C, C), dtype=bool))
    strict = jnp.tril(jnp.ones((C, C), dtype=bool), -1)
    decay = jnp.exp(jnp.where(causal, gc[..., :, None] - gc[..., None, :], NEG_INF))
    kb = k * beta[..., None]
    lmat = jnp.where(strict, jnp.einsum('bhnid,bhnjd->bhnij', kb, k) * decay, 0.0)
    eye = jnp.eye(C, dtype=jnp.float32)
    rhs = jnp.concatenate([kb * jnp.exp(gc)[..., None], v * beta[..., None]], axis=-1)
    wu = lax.linalg.triangular_solve(eye + lmat, rhs, left_side=True, lower=True, unit_diagonal=True)
    w_c, u_c = wu[..., :DK], wu[..., DK:]
    a_qk = jnp.where(causal, jnp.einsum('bhnid,bhnjd->bhnij', q, k) * decay, 0.0)
    q_dec = q * jnp.exp(gc)[..., None]
    g_last = gc[..., -1]
    k_dec = k * jnp.exp(g_last[..., None] - gc)[..., None]
    xs = tuple(jnp.moveaxis(t, 2, 0) for t in (w_c, u_c, q_dec, a_qk, k_dec, jnp.exp(g_last)))

    def step(state, inp):
        w_i, u_i, q_i, a_i, k_i, d_i = inp
        v_new = u_i - jnp.einsum('bhck,bhkv->bhcv', w_i, state)
        o_i = jnp.einsum('bhck,bhkv->bhcv', q_i, state) + jnp.einsum('bhij,bhjv->bhiv', a_i, v_new)
        state = state * d_i[..., None, None] + jnp.einsum('bhck,bhcv->bhkv', k_i, v_new)
        return state, o_i

    s0 = jnp.zeros((B, H, DK, DV), jnp.float32)
    _, o = lax.scan(step, s0, xs)
    o = jnp.moveaxis(o, 0, 2)
    return jnp.moveaxis(o, 1, 3).reshape(B, S, H, DV)


def gated_delta_mixer(qkv, a, b, gate, conv_w, a_log, dt_bias, o_norm):
    B, S, _ = qkv.shape
    H = DN_HEADS
    f32 = jnp.float32
    qkv_c = jax.nn.silu(causal_depthwise_conv(qkv, conv_w))
    q, k, v = jnp.split(qkv_c, 3, axis=-1)
    q = l2_norm(q.reshape(B, S, H, DN_DK).astype(f32)) * (DN_DK ** -0.5)
    k = l2_norm(k.reshape(B, S, H, DN_DK).astype(f32))
    v = v.reshape(B, S, H, DN_DV).astype(f32)
    beta = jax.nn.sigmoid(b.astype(f32))
    g = -jnp.exp(a_log.astype(f32)) * jax.nn.softplus(a.astype(f32) + dt_bias.astype(f32))
    o = chunked_gated_delta(q, k, v, g, beta)
    o = rms_norm(o, o_norm) * jax.nn.silu(gate.astype(f32).reshape(B, S, H, DN_DV))
    return o.reshape(B, S, H * DN_DV).astype(qkv.dtype)


def setup_inputs(seed: int = 0) -> dict:
    key = jax.random.key(seed)
    ks = iter(jax.random.split(key, 32))
    f32 = jnp.float32
    L = DEPTH

    def nrm(shape, scale):
        return jax.random.normal(next(ks), shape, f32) * scale

    def gain(shape):
        return 1.0 + nrm(shape, 0.02)

    def unif(shape, lo, hi):
        return jax.random.uniform(next(ks), shape, f32, lo, hi)

    G, P, CG = S5_GROUPS, S5_STATE, S5_GROUP_CH
    x = nrm((BATCH, SEQ, D_MODEL), 1.0)
    attn_norm = gain((L, D_MODEL))
    w_in = nrm((L, D_MODEL, IN_COLS), D_MODEL ** -0.5)
    w_out = nrm((L, D_MIX, D_MODEL), D_MIX ** -0.5)
    mla_q_norm = gain((L, MLA_Q_RANK))
    mla_kv_norm = gain((L, MLA_KV_RANK))
    mla_w_uq = nrm((L, MLA_Q_RANK, MLA_HEADS * (MLA_NOPE + MLA_ROPE)), MLA_Q_RANK ** -0.5)
    mla_w_ukv = nrm((L, MLA_KV_RANK, MLA_HEADS * (MLA_NOPE + MLA_V)), MLA_KV_RANK ** -0.5)
    mla_qk_q = gain((L, MLA_NOPE + MLA_ROPE))
    mla_qk_k = gain((L, MLA_NOPE + MLA_ROPE))
    s5_lambda_re = -0.5 * (1.0 + nrm((L, G, P), 0.02))
    s5_lambda_im = jnp.pi * jnp.arange(P, dtype=f32)[None, None, :] + nrm((L, G, P), 0.01)
    s5_log_dt = unif((L, G), math.log(S5_DT_MIN), math.log(S5_DT_MAX))
    s5_b_re = nrm((L, G, P, CG), (2 * CG) ** -0.5)
    s5_b_im = nrm((L, G, P, CG), (2 * CG) ** -0.5)
    s5_c_re = nrm((L, G, CG, P), P ** -0.5)
    s5_c_im = nrm((L, G, CG, P), P ** -0.5)
    s5_d = nrm((L, GROUP_W), 1.0)
    s5_w_glu = nrm((L, GROUP_W, 2 * GROUP_W), GROUP_W ** -0.5)
    dil_q_norm = gain((L, HEAD_DIM))
    dil_k_norm = gain((L, HEAD_DIM))
    t5_bias = nrm((T5_BUCKETS, DIL_HEADS), 0.2)
    dn_conv = nrm((L, DN_CONV, 3 * GROUP_W), DN_CONV ** -0.5)
    dn_a_log = jnp.log(unif((L, DN_HEADS), 1.0, 16.0))
    dt = jnp.exp(unif((L, DN_HEADS), math.log(DN_DT_MIN), math.log(DN_DT_MAX)))
    dn_dt_bias = dt + jnp.log(-jnp.expm1(-dt))
    dn_o_norm = gain((L, DN_DV))
    ffn_norm = gain((L, D_MODEL))
    ffn_w1 = nrm((L, D_MODEL, FFN_HIDDEN), D_MODEL ** -0.5)
    ffn_w3 = nrm((L, D_MODEL, FFN_HIDDEN), D_MODEL ** -0.5)
    ffn_w2 = nrm((L, FFN_HIDDEN, D_MODEL), FFN_HIDDEN ** -0.5)
    return {'x': x, 'attn_norm': attn_norm, 'w_in': w_in, 'w_out': w_out,
            'mla_q_norm': mla_q_norm, 'mla_kv_norm': mla_kv_norm, 'mla_w_uq': mla_w_uq,
            'mla_w_ukv': mla_w_ukv, 'mla_qk_q': mla_qk_q, 'mla_qk_k': mla_qk_k,
            's5_lambda_re': s5_lambda_re, 's5_lambda_im': s5_lambda_im, 's5_log_dt': s5_log_dt,
            's5_b_re': s5_b_re, 's5_b_im': s5_b_im, 's5_c_re': s5_c_re, 's5_c_im': s5_c_im,
            's5_d': s5_d, 's5_w_glu': s5_w_glu, 'dil_q_norm': dil_q_norm, 'dil_k_norm': dil_k_norm,
            't5_bias': t5_bias, 'dn_conv': dn_conv, 'dn_a_log': dn_a_log, 'dn_dt_bias': dn_dt_bias,
            'dn_o_norm': dn_o_norm, 'ffn_norm': ffn_norm, 'ffn_w1': ffn_w1, 'ffn_w3': ffn_w3,
            'ffn_w2': ffn_w2}


def reference(x, attn_norm, w_in, w_out, mla_q_norm, mla_kv_norm, mla_w_uq, mla_w_ukv, mla_qk_q, mla_qk_k,
              s5_lambda_re, s5_lambda_im, s5_log_dt, s5_b_re, s5_b_im, s5_c_re, s5_c_im, s5_d, s5_w_glu,
              dil_q_norm, dil_k_norm, t5_bias, dn_conv, dn_a_log, dn_dt_bias, dn_o_norm,
              ffn_norm, ffn_w1, ffn_w3, ffn_w2):
    h = x
    for l in range(DEPTH):
        n = rms_norm(h, attn_norm[l])
        proj = n @ w_in[l]
        c_q, c_kv, k_rope, u_s5, qkv_dil, qkv_dn, a_dn, b_dn, gate_dn = split_cols(proj, IN_SPLITS)
        y_mla = mla_mixer(c_q, c_kv, k_rope, mla_q_norm[l], mla_kv_norm[l], mla_w_uq[l], mla_w_ukv[l],
                          mla_qk_q[l], mla_qk_k[l])
        y_s5 = s5_mixer(u_s5, s5_lambda_re[l], s5_lambda_im[l], s5_log_dt[l], s5_b_re[l], s5_b_im[l],
                        s5_c_re[l], s5_c_im[l], s5_d[l], s5_w_glu[l])
        y_dil = dilated_mixer(qkv_dil, dil_q_norm[l], dil_k_norm[l], t5_bias)
        y_dn = gated_delta_mixer(qkv_dn, a_dn, b_dn, gate_dn, dn_conv[l], dn_a_log[l], dn_dt_bias[l],
                                 dn_o_norm[l])
        mixed = jnp.concatenate([y_mla, y_s5, y_dil, y_dn], axis=-1)
        h = h + mixed @ w_out[l]
        n = rms_norm(h, ffn_norm[l])
        h = h + (jax.nn.silu(n @ ffn_w1[l]) * (n @ ffn_w3[l])) @ ffn_w2[l]
    return h
```

```python
import math
import numpy as np
from contextlib import ExitStack
import concourse.bass as bass
import concourse.mybir as mybir
from concourse.bass_utils import run_bass_kernel_spmd

F32 = mybir.dt.float32
BF16 = mybir.dt.bfloat16
ALU = mybir.AluOpType
AF = mybir.ActivationFunctionType

D = 1024
SEQ = 2048
NL = 2
GW = 256
IN_COLS = 2472
FFN = 2816
EPS = 1e-6
NEG = -30000.0
NTB = 4
NTT = 16
C_CQ, C_CKV, C_KR, C_U, C_DQ, C_DK, C_DV, C_NQ, C_NK, C_NV, C_A, C_B, C_G = (
    0, 256, 384, 416, 672, 928, 1184, 1440, 1696, 1952, 2208, 2212, 2216)


class Dep:
    __slots__ = ("w", "r")

    def __init__(self):
        self.w = None
        self.r = {}


class T:
    __slots__ = ("t", "d", "ex")

    def __init__(self, t, ex=False):
        self.t = t
        self.d = Dep()
        self.ex = ex

    def __getitem__(self, k):
        return self.t[k]


class Sched:
    ENG = ("pe", "act", "dve", "pool", "sp")

    def __init__(self, nc, es, n_dma_sems=8):
        self.nc = nc
        self.sem = {k: es.enter_context(nc.semaphore("s_" + k)) for k in ("pe", "act", "dve", "pool")}
        self.cnt = {k: 0 for k in ("pe", "act", "dve", "pool")}
        self.ndma = n_dma_sems
        for pre in ("q", "g"):
            for i in range(n_dma_sems):
                k = "%s%d" % (pre, i)
                self.sem[k] = es.enter_context(nc.semaphore("s_" + k))
                self.cnt[k] = 0
        self.dma_rr = {"sp": 0, "pool": 0}
        self.seen = {}
        self.ninst = 0
        self.prog = {k: [] for k in self.ENG}

    @staticmethod
    def _inc(k):
        return 16 if k[0] in "qg" else 1

    def _wait(self, e, k, c):
        s = self.seen.setdefault(e, {})
        if s.get(k, 0) >= c:
            return
        self.prog[e].append(("w", self.sem[k], c * self._inc(k)))
        s[k] = c

    @staticmethod
    def _deps(reads, writes):
        d = {}
        for t in reads:
            t = t.d if isinstance(t, T) else t
            if t.w is not None:
                k, c = t.w
                d[k] = max(d.get(k, 0), c)
        for t in writes:
            t = t.d if isinstance(t, T) else t
            if t.w is not None:
                k, c = t.w
                d[k] = max(d.get(k, 0), c)
            for k, c in t.r.items():
                d[k] = max(d.get(k, 0), c)
        return d

    def _mark(self, k, c, reads, writes):
        for t in reads:
            t = t.d if isinstance(t, T) else t
            t.r[k] = c
        for t in writes:
            t = t.d if isinstance(t, T) else t
            t.w = (k, c)
            t.r = {}

    def op(self, e, fn, reads=(), writes=()):
        exr = [t for t in reads if isinstance(t, T) and t.ex]
        if exr:
            writes = list(writes) + exr
        d = self._deps(reads, writes)
        for k, c in d.items():
            if k == e and e == "pe":
                continue
            self._wait(e, k, c)
        self.cnt[e] += 1
        self.prog[e].append(("i", fn, self.sem[e], 1))
        self._mark(e, self.cnt[e], reads, writes)
        self.ninst += 1

    def dma(self, out, in_, reads=(), writes=(), q="sp"):
        i = self.dma_rr[q]
        self.dma_rr[q] = (i + 1) % self.ndma
        k = ("q%d" if q == "sp" else "g%d") % i
        d = self._deps(reads, writes)
        if self.cnt[k] > 0:
            d[k] = max(d.get(k, 0), self.cnt[k])
        for kk, c in d.items():
            self._wait(q, kk, c)
        self.cnt[k] += 1
        self.prog[q].append(("i", (lambda e, out=out, in_=in_: e.dma_start(out=out, in_=in_)), self.sem[k], 16))
        self._mark(k, self.cnt[k], reads, writes)
        self.ninst += 1

    def barrier(self):
        for e in self.ENG:
            for k, c in self.cnt.items():
                if c > 0 and not (k == e == "pe"):
                    self._wait(e, k, c)

    def finish(self, e="sp"):
        for k, c in self.cnt.items():
            if c > 0:
                self._wait(e, k, c)

    def emit(self):
        nc = self.nc
        with nc.Block() as block:
            def mk(k):
                def body(eng):
                    for it in self.prog[k]:
                        if it[0] == "w":
                            eng.wait_ge(it[1], it[2])
                        else:
                            it[1](eng).then_inc(it[2], it[3])
                return body
            for k, reg in (("sp", block.sync), ("pe", block.tensor), ("act", block.scalar),
                           ("dve", block.vector), ("pool", block.gpsimd)):
                if self.prog[k]:
                    reg(mk(k))


class Ring:
    def __init__(self, kb, name, shape, dt, n, psum=False):
        self.tiles = [kb.alloc("%s%d" % (name, i), shape, dt, psum=psum) for i in range(n)]
        self.i = 0

    def get(self):
        t = self.tiles[self.i]
        self.i = (self.i + 1) % len(self.tiles)
        return t


class KB:
    def __init__(self, nc, es):
        self.nc = nc
        self.es = es
        self.S = Sched(nc, es)
        self.scope = es
        self.uid = 0

    def alloc(self, name, shape, dt, psum=False):
        self.uid += 1
        nm = "%s_%d" % (name, self.uid)
        if psum:
            t = self.scope.enter_context(self.nc.psum_tensor(nm, shape, dt))
        else:
            t = self.scope.enter_context(self.nc.sbuf_tensor(nm, shape, dt))
        return T(t, ex=psum)

    def mm(self, out, lhsT, rhs, start, stop, reads, writes):
        self.S.op("pe", lambda e: e.matmul(out, lhsT=lhsT, rhs=rhs, start=start, stop=stop), reads, writes)

    def tr(self, out, in_, ident, reads, writes):
        self.S.op("pe", lambda e: e.transpose(out, in_=in_, identity=ident), reads, writes)

    def act(self, out, in_, func, reads, writes, scale=None, bias=None, accum_out=None):
        kw = {}
        if scale is not None:
            kw["scale"] = scale
        if bias is not None:
            kw["bias"] = bias
        if accum_out is not None:
            kw["accum_out"] = accum_out
        self.S.op("act", lambda e: e.activation(out=out, in_=in_, func=func, **kw), reads, writes)

    def tt(self, out, in0, in1, op, reads, writes, eng="dve"):
        self.S.op(eng, lambda e: e.tensor_tensor(out=out, in0=in0, in1=in1, op=op), reads, writes)

    def ts(self, out, in0, s1, op0, reads, writes, s2=None, op1=None, eng="dve"):
        if op1 is None:
            self.S.op(eng, lambda e: e.tensor_scalar(out=out, in0=in0, scalar1=s1, scalar2=None, op0=op0), reads, writes)
        else:
            self.S.op(eng, lambda e: e.tensor_scalar(out=out, in0=in0, scalar1=s1, scalar2=s2, op0=op0, op1=op1), reads, writes)

    def stt(self, out, in0, scalar, in1, op0, op1, reads, writes):
        self.S.op("dve", lambda e: e.scalar_tensor_tensor(out=out, in0=in0, scalar=scalar, in1=in1, op0=op0, op1=op1), reads, writes)

    def copy(self, out, in_, reads, writes, eng="dve"):
        if eng == "act":
            self.S.op("act", lambda e: e.copy(out=out, in_=in_), reads, writes)
        else:
            self.S.op(eng, lambda e: e.tensor_copy(out=out, in_=in_), reads, writes)

    def recip(self, out, in_, reads, writes):
        self.S.op("dve", lambda e: e.reciprocal(out=out, in_=in_), reads, writes)

    def memset(self, ap, val, writes, eng="pool"):
        self.S.op(eng, lambda e: e.memset(ap, val), (), writes)


class _Stop(Exception):
    pass


def build(nlayers=NL, mixers=("mla", "s5", "dil", "dn"), ffn=True, dbg=None, stop=None):
    nc = bass.Bass("TRN2", target_bir_lowering=False)

    def din(name, shape):
        return nc.dram_tensor(name, list(shape), F32, kind="ExternalInput").ap()

    xT_d = din("xT", [128, 8, SEQ])
    w_in_d = din("w_in", [NL, D, IN_COLS])
    w_out_d = din("w_out", [NL, D, D])
    pv_d = din("pvec", [128, PV.n])
    cst_d = din("cst128", [128, CS.n])
    w_uq_d = din("w_uq", [NL, 256, 384])
    w_ukp_d = din("w_ukp", [NL, 128, 4 * 96])
    w_uv_d = din("w_uv", [NL, 128, 256])
    w_krp_d = din("w_krp", [NL, D, 96])
    rope_d = din("rope", [96, 2, SEQ])
    s5b_d = din("s5_b", [NL, 128, 2, 2, 512])
    s5c_d = din("s5_c", [NL, 128, 2, 8, 128])
    s5g_d = din("s5_w_glu", [NL, 256, 512])
    dnrow_d = din("dn_row", [NL, 128, 192])
    dilb_d = din("dil_bias", [128, 12, 256])
    w1_d = din("ffn_w1", [NL, D, FFN])
    w3_d = din("ffn_w3", [NL, D, FFN])
    w2_d = din("ffn_w2", [NL, FFN, D])
    yT_d = nc.dram_tensor("yT", [128, 8, SEQ], F32, kind="ExternalOutput").ap()
    dbg_d = None
    if dbg is not None:
        dbg_d = nc.dram_tensor("dbg", list(dbg[1]), F32, kind="ExternalOutput").ap()

    with ExitStack() as es:
        kb = KB(nc, es)
        S = kb.S
        hT = kb.alloc("hT", [128, 8, SEQ], F32)
        nT = kb.alloc("nT", [128, 8, SEQ], BF16)
        pv = kb.alloc("pv", [128, PV.n], F32)
        cst = kb.alloc("cst", [128, CS.n], F32)
        cstb = kb.alloc("cstb", [128, CS.n], BF16)
        ps = Ring(kb, "ps", [128, 512], F32, 6, psum=True)
        psl = Ring(kb, "psl", [128, 512], F32, 2, psum=True)

        S.dma(pv[:], pv_d, writes=[pv])
        S.dma(cst[:], cst_d, writes=[cst])
        S.dma(cstb[:], cst_d, writes=[cstb], q="pool")
        for c in range(8):
            S.dma(hT[:, c, :], xT_d[:, c, :], writes=[hT], q=("sp" if c % 2 == 0 else "pool"))

        def C(name, rows=128, bf=False):
            o, n = CS.off[name]
            return (cstb if bf else cst)[0:rows, o:o + n]

        def P(name, l=0, j=0, rows=128):
            o = PV.off[name] + l * PV.per[name] + j
            return pv[0:rows, o:o + 1]

        def stream_norm(gname, l):
            with ExitStack() as sc:
                kb.scope = sc
                sq = Ring(kb, "sq", [128, 512], BF16, 3)
                sd = Ring(kb, "sd", [128, 512], F32, 2)
                for tb in range(NTB):
                    sl = slice(tb * 512, (tb + 1) * 512)
                    st = ps.get()
                    for c in range(8):
                        q = sq.get()
                        kb.act(q[:], hT[:, c, sl], AF.Square, [hT], [q])
                        kb.mm(st[:], C("ones", bf=True), q[:], c == 0, c == 7, [q, cstb], [st])
                    s1 = sd.get()
                    kb.act(s1[:], st[:], AF.Sqrt, [st], [s1], scale=1.0 / D, bias=EPS)
                    kb.recip(s1[:], s1[:], [s1], [s1])
                    for c in range(8):
                        kb.stt(nT[:, c, sl], hT[:, c, sl], P(gname, l, c), s1[:], ALU.mult, ALU.mult,
                               [hT, pv, s1], [nT])
                S.barrier()
            kb.scope = es

        def apply_wout(l, pieces, sub=False):
            with ExitStack() as sc:
                kb.scope = sc
                wts = []
                for (_, _, row0, K) in pieces:
                    w = kb.alloc("wo", [K, D], BF16)
                    S.dma(w[:], w_out_d[l, row0:row0 + K, :], writes=[w], q="pool")
                    wts.append(w)
                for oc in range(8):
                    for tb in range(NTB):
                        sl = slice(tb * 512, (tb + 1) * 512)
                        acc = ps.get()
                        n = len(pieces)
                        for i, (pt, apf, row0, K) in enumerate(pieces):
                            kb.mm(acc[:], wts[i][:, oc * 128:(oc + 1) * 128], apf(tb), i == 0, i == n - 1,
                                  [wts[i], pt], [acc])
                        kb.tt(hT[:, oc, sl], hT[:, oc, sl], acc[:], ALU.add, [hT, acc], [hT])
                S.barrier()
            kb.scope = es

        def mla(l):
            with ExitStack() as sc:
                kb.scope = sc
                wuq = kb.alloc("wuq", [128, 2, 384], BF16)
                wuk = kb.alloc("wuk", [128, 4 * 96], BF16)
                cqn = kb.alloc("cqn", [128, 2, SEQ], BF16)
                ckvn = kb.alloc("ckvn", [128, SEQ], BF16)
                krp = kb.alloc("krp", [96, SEQ], F32)
                vaug = kb.alloc("vaug", [128, NTT, 4, 72], BF16)
                sqb = Ring(kb, "sqb", [128, 512], BF16, 3)
                sdr = Ring(kb, "sdr", [128, 512], F32, 2)
                scA = ExitStack()
                kb.scope = scA
                wq = kb.alloc("wq", [128, 8, 256], BF16)
                wkv = kb.alloc("wkv", [128, 8, 128], BF16)
                wkr = kb.alloc("wkr", [128, 8, 96], BF16)
                wuv = kb.alloc("wuv", [128, 256], BF16)
                g32 = Ring(kb, "g32", [128, 512], F32, 3)
                S.dma(wq[:], w_in_d[l, :, C_CQ:C_CQ + 256].rearrange("(c p) n -> p c n", p=128), writes=[wq], q="pool")
                S.dma(wkv[:], w_in_d[l, :, C_CKV:C_CKV + 128].rearrange("(c p) n -> p c n", p=128), writes=[wkv], q="pool")
                S.dma(wkr[:], w_krp_d[l].rearrange("(c p) n -> p c n", p=128), writes=[wkr], q="pool")
                S.dma(wuq[:], w_uq_d[l].rearrange("(c p) n -> p c n", p=128), writes=[wuq], q="pool")
                S.dma(wuk[:], w_ukp_d[l], writes=[wuk], q="pool")
                S.dma(wuv[:], w_uv_d[l], writes=[wuv], q="pool")
                kb.memset(vaug[:], 1.0, [vaug])
                if stop == "A0":
                    S.barrier()
                    scA.close()
                    kb.scope = es
                    return
                for tb in range(NTB):
                    sl = slice(tb * 512, (tb + 1) * 512)
                    st = ps.get()
                    gq = []
                    for oc in range(2):
                        pp = ps.get()
                        for c in range(8):
                            kb.mm(pp[:], wq[:, c, oc * 128:(oc + 1) * 128], nT[:, c, sl], c == 0, c == 7, [wq, nT], [pp])
                        if stop == "A1a":
                            S.barrier(); scA.close(); kb.scope = es
                            return
                        g = g32.get()
                        import os as _os
                        if _os.environ.get("VAR", "0") == "4":
                            q4 = sqb.get()
                            kb.act(q4[:], pp[:], AF.Square, [pp], [q4])
                            kb.ts(g[:], pp[:], P("mla_qn", l, oc), ALU.mult, [pp, pv, q4], [g])
                        else:
                            kb.ts(g[:], pp[:], P("mla_qn", l, oc), ALU.mult, [pp, pv], [g])
                        if stop == "A1b":
                            S.barrier(); scA.close(); kb.scope = es
                            return
                        q = sqb.get()
                        import os as _os
                        _v = _os.environ.get("VAR", "0")
                        if _v == "1":
                            kb.act(q[:], g[:], AF.Square, [g], [q])
                        elif _v == "2":
                            kb.copy(q[:], pp[:], [pp], [q], eng="act")
                        elif _v == "3":
                            kb.tt(q[:], pp[:], g[:], ALU.mult, [pp, g], [q])
                        else:
                            kb.act(q[:], pp[:], AF.Square, [pp], [q])
                        if stop == "A1c":
                            S.barrier(); scA.close(); kb.scope = es
                            return
                        kb.mm(st[:], C("ones", bf=True), q[:], oc == 0, oc == 1, [q, cstb], [st])
                        gq.append(g)
                        if stop == "A1d":
                            S.barrier(); scA.close(); kb.scope = es
                            return
                    s1 = sdr.get()
                    kb.act(s1[:], st[:], AF.Sqrt, [st], [s1], scale=1.0 / 256, bias=EPS)
                    kb.recip(s1[:], s1[:], [s1], [s1])
                    for oc in range(2):
                        kb.tt(cqn[:, oc, sl], gq[oc][:], s1[:], ALU.mult, [gq[oc], s1], [cqn])
                    if stop == "A1":
                        S.barrier()
                        scA.close()
                        kb.scope = es
                        return
                    st = ps.get()
                    pp = ps.get()
                    for c in range(8):
                        kb.mm(pp[:], wkv[:, c, :], nT[:, c, sl], c == 0, c == 7, [wkv, nT], [pp])
                    g = g32.get()
                    kb.ts(g[:], pp[:], P("mla_kvn", l, 0), ALU.mult, [pp, pv], [g])
                    q = sqb.get()
                    kb.act(q[:], pp[:], AF.Square, [pp], [q])
                    kb.mm(st[:], C("ones", bf=True), q[:], True, True, [q, cstb], [st])
                    s1 = sdr.get()
                    kb.act(s1[:], st[:], AF.Sqrt, [st], [s1], scale=1.0 / 128, bias=EPS)
                    kb.recip(s1[:], s1[:], [s1], [s1])
                    kb.tt(ckvn[:, sl], g[:], s1[:], ALU.mult, [g, s1], [ckvn])
                    if stop == "A2":
                        S.barrier()
                        scA.close()
                        kb.scope = es
                        return
                    pp = ps.get()
                    for c in range(8):
                        kb.mm(pp[0:96, :], wkr[:, c, :], nT[:, c, sl], c == 0, c == 7, [wkr, nT], [pp])
                    kb.copy(krp[:, sl], pp[0:96, :], [pp], [krp], eng="act")
                    if stop == "A3":
                        S.barrier()
                        scA.close()
                        kb.scope = es
                        return
                    for j in range(4):
                        tt_ = tb * 4 + j
                        pp = ps.get()
                        kb.mm(pp[:, 0:256], ckvn[:, tt_ * 128:(tt_ + 1) * 128], wuv[:], True, True, [ckvn, wuv], [pp])
                        kb.copy(vaug[:, tt_, :, 0:64], pp[:, 0:256].rearrange("p (h d) -> p h d", h=4), [pp], [vaug], eng="act")
                S.barrier()
                scA.close()
                if stop == "A":
                    kb.scope = es
                    return
                with ExitStack() as sc2:
                    kb.scope = sc2
                    rope = kb.alloc("rope", [96, 2, SEQ], F32)
                    S.dma(rope[:], rope_d, writes=[rope])
                    yTr = [kb.alloc("ymla%d" % h, [64, SEQ], BF16) for h in range(2)]
                    yT = [yTr[0], yTr[1], yTr[0], yTr[1]]
                    if dbg is not None and dbg[0] == "mla":
                        tmpf = kb.alloc("dbgf", [64, SEQ], F32)
                    qT = kb.alloc("qT", [96, SEQ], BF16)
                    kT = kb.alloc("kT", [96, SEQ], BF16)
                    pb = Ring(kb, "pb", [128, 512], BF16, 4)
                    osb = Ring(kb, "osb", [65, 512], F32, 2)
                    rcp = Ring(kb, "rcp", [64, 512], F32, 2)
                    qnb = Ring(kb, "qnb", [96, 512], BF16, 2)
                    g96 = Ring(kb, "g96", [96, 512], F32, 3)
                    t32 = Ring(kb, "t32", [96, 512], F32, 3)
                    scale = 96 ** -0.5

                    def normrope(src_ap, src_reads, gname, dst, sl):
                        g = g96.get()
                        kb.ts(g[:], src_ap, P(gname, l, 0, rows=96), ALU.mult, src_reads + [pv], [g])
                        q = sqb.get()
                        kb.act(q[0:96, :], src_ap, AF.Square, src_reads, [q])
                        st = ps.get()
                        kb.mm(st[0:96, :], C("ones", rows=96, bf=True)[:, 0:96], q[0:96, :], True, True, [q, cstb], [st])
                        s1 = sdr.get()
                        kb.act(s1[0:96, :], st[0:96, :], AF.Sqrt, [st], [s1], scale=1.0 / 96, bias=EPS)
                        kb.recip(s1[0:96, :], s1[0:96, :], [s1], [s1])
                        qn = qnb.get()
                        kb.tt(qn[:], g[:], s1[0:96, :], ALU.mult, [g, s1], [qn])
                        rp = ps.get()
                        kb.mm(rp[0:96, :], C("rotm", rows=96, bf=True)[:, 0:96], qn[:], True, True, [qn, cstb], [rp])
                        kb.copy(dst[0:64, sl], qn[0:64, :], [qn], [dst], eng="pool")
                        t1 = t32.get()
                        kb.tt(t1[64:96, :], rp[64:96, :], rope[64:96, 1, sl], ALU.mult, [rp, rope], [t1])
                        t2 = t32.get()
                        kb.tt(t2[64:96, :], qn[64:96, :], rope[64:96, 0, sl], ALU.mult, [qn, rope], [t2])
                        kb.tt(dst[64:96, sl], t1[64:96, :], t2[64:96, :], ALU.add, [t1, t2], [dst])

                    for h in range(4):
                        for tb in range(NTB):
                            sl = slice(tb * 512, (tb + 1) * 512)
                            pp = ps.get()
                            for oc in range(2):
                                kb.mm(pp[0:96, :], wuq[:, oc, h * 96:(h + 1) * 96], cqn[:, oc, sl], oc == 0, oc == 1, [wuq, cqn], [pp])
                            normrope(pp[0:96, :], [pp], "mla_qkq", qT, sl)
                            pp = ps.get()
                            kb.mm(pp[0:96, :], wuk[:, h * 96:(h + 1) * 96], ckvn[:, sl], True, True, [wuk, ckvn], [pp])
                            kp = t32.get()
                            kb.tt(kp[:], pp[0:96, :], krp[:, sl], ALU.add, [pp, krp], [kp])
                            normrope(kp[:], [kp], "mla_qkk", kT, sl)
                        if stop == "B0" or (stop == "B1" and h == 1):
                            S.barrier()
                            kb.scope = es
                            return
                        for qb in range(NTB):
                            o_ps = psl.get()
                            nk = 4 * qb + 4
                            for kt in range(nk):
                                r = kt - 4 * qb
                                q0 = 128 * max(r, 0)
                                n = 512 - q0
                                s_ps = ps.get()
                                kb.mm(s_ps[:, 0:n], kT[:, kt * 128:(kt + 1) * 128], qT[:, qb * 512 + q0:(qb + 1) * 512],
                                      True, True, [kT, qT], [s_ps])
                                p = pb.get()
                                kb.act(p[:, 0:n], s_ps[:, 0:n], AF.Exp, [s_ps], [p], scale=scale)
                                if r >= 0:
                                    kb.tt(p[:, 0:128], p[:, 0:128], C("tri01", bf=True), ALU.mult, [p, cstb], [p])
                                if stop == "B0a":
                                    S.barrier(); kb.scope = es
                                    return
                                kb.mm(o_ps[0:65, q0:512], vaug[:, kt, h, 0:65], p[:, 0:n], kt == 0, kt == nk - 1, [vaug, p], [o_ps])
                                if stop == "B0b":
                                    S.barrier(); kb.scope = es
                                    return
                            o = osb.get()
                            kb.copy(o[:], o_ps[0:65, :], [o_ps], [o], eng="act")
                            bc = ps.get()
                            kb.mm(bc[0:64, :], C("sel65", rows=65)[:, 0:64], o[:], True, True, [cst, o], [bc])
                            rc = rcp.get()
                            kb.recip(rc[:], bc[0:64, :], [bc], [rc])
                            kb.tt(yT[h][:, qb * 512:(qb + 1) * 512], o[0:64, :], rc[:], ALU.mult, [o, rc], [yT[h]])
                            if stop == "B0c":
                                S.barrier(); kb.scope = es
                                return
                        if dbg is not None and dbg[0] == "mla":
                            kb.copy(tmpf[:], yT[h][:], [yT[h]], [tmpf])
                            S.dma(dbg_d[h], tmpf[:], reads=[tmpf])
                        apply_wout(l, [(yT[h], (lambda tb, h=h: yT[h][:, tb * 512:(tb + 1) * 512]), 0 * GW + 64 * h, 64)], sub=True)
                        kb.scope = sc2
                    S.barrier()
            kb.scope = es


        def dilated(l):
            with ExitStack() as sc:
                kb.scope = sc
                acc = kb.alloc("dacc", [65, 4, SEQ], F32)
                kb.memset(acc[:], 0.0, [acc])
                scB = ExitStack()
                kb.scope = scB
                qkv = [kb.alloc("dqkv%d" % i, [128, 2, SEQ], F32) for i in range(3)]
                with ExitStack() as scA:
                    kb.scope = scA
                    wd = [kb.alloc("wd%d" % i, [128, 8, 256], BF16) for i in range(3)]
                    for i, c0 in enumerate((C_DQ, C_DK, C_DV)):
                        S.dma(wd[i][:], w_in_d[l, :, c0:c0 + 256].rearrange("(c p) n -> p c n", p=128), writes=[wd[i]], q="pool")
                    g32 = Ring(kb, "dg32", [128, 512], F32, 2)
                    sqb = Ring(kb, "dsqb", [128, 512], BF16, 2)
                    sdr = Ring(kb, "dsdr", [128, 512], F32, 2)
                    for i in range(3):
                        for pc in range(2):
                            for tb in range(NTB):
                                sl = slice(tb * 512, (tb + 1) * 512)
                                pp = ps.get()
                                for c in range(8):
                                    kb.mm(pp[:], wd[i][:, c, pc * 128:(pc + 1) * 128], nT[:, c, sl], c == 0, c == 7, [wd[i], nT], [pp])
                                if i == 2:
                                    kb.copy(qkv[2][:, pc, sl], pp[:], [pp], [qkv[2]], eng="act")
                                    continue
                                g = g32.get()
                                kb.ts(g[:], pp[:], P("dil_qn" if i == 0 else "dil_kn", l, 0), ALU.mult, [pp, pv], [g])
                                q = sqb.get()
                                kb.act(q[:], pp[:], AF.Square, [pp], [q])
                                st = ps.get()
                                kb.mm(st[:], C("bones", bf=True), q[:], True, True, [q, cstb], [st])
                                s1 = sdr.get()
                                kb.act(s1[:], st[:], AF.Sqrt, [st], [s1], scale=1.0 / 64, bias=EPS)
                                kb.recip(s1[:], s1[:], [s1], [s1])
                                kb.tt(qkv[i][:, pc, sl], g[:], s1[:], ALU.mult, [g, s1], [qkv[i]])
                    S.barrier()
                kb.scope = scB
                bias = kb.alloc("dbias", [128, 12, 256], F32)
                S.dma(bias[:], dilb_d, writes=[bias])
                vsb = Ring(kb, "dvsb", [128, 4, 72], BF16, 2)
                for v_ in vsb.tiles:
                    kb.memset(v_[:], 1.0, [v_])
                tmpr = Ring(kb, "dtmp", [128, 256], F32, 3)
                pbr = Ring(kb, "dpb", [128, 256], BF16, 3)
                scale = 64 ** -0.5
                for bi, (win, dil) in enumerate(((128, 1), (512, 4), (2048, 16))):
                    L = SEQ // dil
                    nb = L // 128
                    for r in range(dil):
                        for n in range(nb):
                            t0_ = r + dil * 128 * n
                            ksl = slice(t0_, t0_ + dil * 127 + 1, dil)
                            nq = 256 if n + 1 < nb else 128
                            qsl = slice(t0_, t0_ + dil * (nq - 1) + 1, dil)
                            vs = vsb.get()
                            for pc in range(2):
                                vp = ps.get()
                                kb.tr(vp[:, 0:128], qkv[2][:, pc, ksl], C("ident"), [qkv[2], cst], [vp])
                                kb.copy(vs[:, 2 * pc:2 * pc + 2, 0:64], vp[:, 0:128].rearrange("p (h d) -> p h d", h=2), [vp], [vs], eng="act")
                            for h in range(4):
                                pc, p0 = h // 2, 64 * (h % 2)
                                s_ps = ps.get()
                                kb.mm(s_ps[:, 0:nq], qkv[1][p0:p0 + 64, pc, ksl], qkv[0][p0:p0 + 64, pc, qsl], True, True, [qkv[0], qkv[1]], [s_ps])
                                tm = tmpr.get()
                                kb.stt(tm[:, 0:nq], s_ps[:, 0:nq], scale, bias[:, bi * 4 + h, 0:nq], ALU.mult, ALU.add, [s_ps, bias], [tm])
                                p = pbr.get()
                                kb.act(p[:, 0:nq], tm[:, 0:nq], AF.Exp, [tm], [p])
                                o_ps = ps.get()
                                kb.mm(o_ps[0:65, 0:nq], vs[:, h, 0:65], p[:, 0:nq], True, True, [vs, p], [o_ps])
                                kb.tt(acc[:, h, qsl], acc[:, h, qsl], o_ps[0:65, 0:nq], ALU.add, [acc, o_ps], [acc])
                S.barrier()
                scB.close()
                kb.scope = sc
                yTr = [kb.alloc("ydil%d" % h, [64, SEQ], BF16) for h in range(2)]
                rcp = Ring(kb, "drcp", [64, 512], F32, 2)
                if dbg is not None and dbg[0] == "dil":
                    tmpf = kb.alloc("dbgf", [64, SEQ], F32)
                for h in range(4):
                    yT = yTr[h % 2]
                    for tb in range(NTB):
                        sl = slice(tb * 512, (tb + 1) * 512)
                        bc = ps.get()
                        kb.mm(bc[0:64, :], C("sel65", rows=65)[:, 0:64], acc[:, h, sl], True, True, [cst, acc], [bc])
                        rc = rcp.get()
                        kb.recip(rc[:], bc[0:64, :], [bc], [rc])
                        kb.tt(yT[:, sl], acc[0:64, h, sl], rc[:], ALU.mult, [acc, rc], [yT])
                    if dbg is not None and dbg[0] == "dil":
                        kb.copy(tmpf[:], yT[:], [yT], [tmpf])
                        S.dma(dbg_d[64 * h:64 * (h + 1), :], tmpf[:], reads=[tmpf])
                    apply_wout(l, [(yT, (lambda tb, yT=yT: yT[:, tb * 512:(tb + 1) * 512]), 2 * GW + 64 * h, 64)], sub=True)
                    kb.scope = sc
                S.barrier()
            kb.scope = es


        def s5(l):
            I32 = mybir.dt.int32
            TWO_PI = 2.0 * math.pi
            with ExitStack() as sc:
                kb.scope = sc
                uT32 = kb.alloc("uT32", [128, 2, SEQ], F32)
                uTb = kb.alloc("uTb", [128, 2, SEQ], BF16)
                ypre = kb.alloc("ypre", [128, 2, SEQ], BF16)
                wu = kb.alloc("wu", [128, 8, 256], BF16)
                S.dma(wu[:], w_in_d[l, :, C_U:C_U + 256].rearrange("(c p) n -> p c n", p=128), writes=[wu], q="pool")
                bblk = kb.alloc("bblk", [128, 2, 2, 512], BF16)
                S.dma(bblk[:], s5b_d[l], writes=[bblk], q="pool")
                cblk = kb.alloc("cblk", [128, 2, 8, 128], F32)
                S.dma(cblk[:], s5c_d[l], writes=[cblk])
                wglu = kb.alloc("wglu", [128, 2, 512], BF16)
                S.dma(wglu[:], s5g_d[l].rearrange("(c p) n -> p c n", p=128), writes=[wglu], q="pool")
                for oc in range(2):
                    for tb in range(NTB):
                        sl = slice(tb * 512, (tb + 1) * 512)
                        pp = ps.get()
                        for c in range(8):
                            kb.mm(pp[:], wu[:, c, oc * 128:(oc + 1) * 128], nT[:, c, sl], c == 0, c == 7, [wu, nT], [pp])
                        kb.copy(uT32[:, oc, sl], pp[:], [pp], [uT32], eng="act")
                        kb.copy(uTb[:, oc, sl], pp[:], [pp], [uTb])
                sm = {n_: kb.alloc("s5p_" + n_, [128, 8], F32) for n_ in
                      ("dt", "rho", "th", "amag", "c", "s", "ar", "ai", "zr", "zi", "t1", "t2", "t3", "nzi")}
                kint = kb.alloc("s5kint", [128, 512], I32)
                thoff = kb.alloc("thoff", [128, 8, 4], F32)
                lr = pv[:, PV.off["s5_lr"] + l * 8: PV.off["s5_lr"] + l * 8 + 8]
                li = pv[:, PV.off["s5_li"] + l * 8: PV.off["s5_li"] + l * 8 + 8]
                ldt = pv[:, PV.off["s5_ldt"] + l * 8: PV.off["s5_ldt"] + l * 8 + 8]

                def wrap(r_, tmp_, n):
                    kb.ts(tmp_, r_, math.pi, ALU.is_gt, [r_t], [tmp_t])
                    kb.stt(r_, tmp_, -TWO_PI, r_, ALU.mult, ALU.add, [tmp_t, r_t], [r_t])
                    kb.ts(tmp_, r_, -math.pi, ALU.is_lt, [r_t], [tmp_t])
                    kb.stt(r_, tmp_, TWO_PI, r_, ALU.mult, ALU.add, [tmp_t, r_t], [r_t])

                def sincos(ang_ap, ang_t, s_ap, s_t, c_ap, c_t, r_ap, r_T, tmp_ap, tmp_T, ki_ap):
                    nonlocal r_t, tmp_t
                    r_t, tmp_t = r_T, tmp_T
                    kb.ts(tmp_ap, ang_ap, 1.0 / TWO_PI, ALU.mult, [ang_t], [tmp_T])
                    kb.copy(ki_ap, tmp_ap, [tmp_T], [kint])
                    kb.copy(tmp_ap, ki_ap, [kint], [tmp_T])
                    kb.stt(r_ap, tmp_ap, -TWO_PI, ang_ap, ALU.mult, ALU.add, [tmp_T, ang_t], [r_T])
                    wrap(r_ap, tmp_ap, 0)
                    kb.act(s_ap, r_ap, AF.Sin, [r_T], [s_t])
                    kb.ts(r_ap, r_ap, math.pi / 2, ALU.add, [r_T], [r_T])
                    wrap(r_ap, tmp_ap, 0)
                    kb.act(c_ap, r_ap, AF.Sin, [r_T], [c_t])

                r_t = tmp_t = None
                kb.act(sm["dt"][:], ldt, AF.Exp, [pv], [sm["dt"]])
                kb.tt(sm["rho"][:], lr, sm["dt"][:], ALU.mult, [pv, sm["dt"]], [sm["rho"]])
                kb.tt(sm["th"][:], li, sm["dt"][:], ALU.mult, [pv, sm["dt"]], [sm["th"]])
                kb.act(sm["amag"][:], sm["rho"][:], AF.Exp, [sm["rho"]], [sm["amag"]])
                sincos(sm["th"][:], sm["th"], sm["s"][:], sm["s"], sm["c"][:], sm["c"], sm["t1"][:], sm["t1"], sm["t2"][:], sm["t2"], kint[:, 0:8])
                kb.tt(sm["ar"][:], sm["amag"][:], sm["c"][:], ALU.mult, [sm["amag"], sm["c"]], [sm["ar"]])
                kb.tt(sm["ai"][:], sm["amag"][:], sm["s"][:], ALU.mult, [sm["amag"], sm["s"]], [sm["ai"]])
                kb.tt(sm["t1"][:], lr, lr, ALU.mult, [pv], [sm["t1"]])
                kb.tt(sm["t2"][:], li, li, ALU.mult, [pv], [sm["t2"]])
                kb.tt(sm["t1"][:], sm["t1"][:], sm["t2"][:], ALU.add, [sm["t1"], sm["t2"]], [sm["t1"]])
                kb.recip(sm["t1"][:], sm["t1"][:], [sm["t1"]], [sm["t1"]])
                kb.ts(sm["t2"][:], sm["ar"][:], -1.0, ALU.add, [sm["ar"]], [sm["t2"]])
                kb.tt(sm["zr"][:], sm["t2"][:], lr, ALU.mult, [sm["t2"], pv], [sm["zr"]])
                kb.tt(sm["t3"][:], sm["ai"][:], li, ALU.mult, [sm["ai"], pv], [sm["t3"]])
                kb.tt(sm["zr"][:], sm["zr"][:], sm["t3"][:], ALU.add, [sm["zr"], sm["t3"]], [sm["zr"]])
                kb.tt(sm["zr"][:], sm["zr"][:], sm["t1"][:], ALU.mult, [sm["zr"], sm["t1"]], [sm["zr"]])
                kb.tt(sm["zi"][:], sm["ai"][:], lr, ALU.mult, [sm["ai"], pv], [sm["zi"]])
                kb.tt(sm["t3"][:], sm["t2"][:], li, ALU.mult, [sm["t2"], pv], [sm["t3"]])
                kb.tt(sm["zi"][:], sm["zi"][:], sm["t3"][:], ALU.subtract, [sm["zi"], sm["t3"]], [sm["zi"]])
                kb.tt(sm["zi"][:], sm["zi"][:], sm["t1"][:], ALU.mult, [sm["zi"], sm["t1"]], [sm["zi"]])
                kb.ts(sm["nzi"][:], sm["zi"][:], -1.0, ALU.mult, [sm["zi"]], [sm["nzi"]])
                for tb in range(NTB):
                    kb.ts(thoff[:, :, tb], sm["th"][:], float(512 * tb), ALU.mult, [sm["th"]], [thoff])
                cz = kb.alloc("cz", [128, 2, 8, 128], F32)
                ctmp = kb.alloc("ctmp", [128, 128], F32)
                for scn in range(8):
                    kb.ts(ctmp[:], cblk[:, 1, scn, :], sm["nzi"][:, scn:scn + 1], ALU.mult, [cblk, sm["nzi"]], [ctmp])
                    kb.stt(cz[:, 0, scn, :], cblk[:, 0, scn, :], sm["zr"][:, scn:scn + 1], ctmp[:], ALU.mult, ALU.add, [cblk, sm["zr"], ctmp], [cz])
                    kb.ts(ctmp[:], cblk[:, 1, scn, :], sm["zr"][:, scn:scn + 1], ALU.mult, [cblk, sm["zr"]], [ctmp])
                    kb.stt(ctmp[:], cblk[:, 0, scn, :], sm["zi"][:, scn:scn + 1], ctmp[:], ALU.mult, ALU.add, [cblk, sm["zi"], ctmp], [ctmp])
                    kb.ts(cz[:, 1, scn, :], ctmp[:], -1.0, ALU.mult, [ctmp], [cz])
                tl = {n_: kb.alloc("s5t_" + n_, [128, 512], F32) for n_ in ("ang", "r", "tmp", "s", "c", "a1", "a2", "xr", "xi", "hr", "hi")}
                amb = kb.alloc("amb", [128, 512], F32)
                hp = [kb.alloc("s5hp%d" % i, [128, 512], F32) for i in range(2)]
                hlast = kb.alloc("s5hl", [128, 2, 8], F32)
                kb.memset(hlast[:], 0.0, [hlast])
                for tb in range(NTB):
                    sl = slice(tb * 512, (tb + 1) * 512)
                    for scn in range(8):
                        hh = scn // 4
                        if scn % 4 == 0:
                            yps = psl.get()
                        kb.ts(amb[:], C("trow"), 0.0, ALU.mult, [cst], [amb], s2=sm["amag"][:, scn:scn + 1], op1=ALU.add)
                        pr = ps.get()
                        pi_ = ps.get()
                        c0 = (scn % 4) * 128
                        kb.mm(pr[:], bblk[:, hh, 0, c0:c0 + 128], uTb[:, hh, sl], True, True, [bblk, uTb], [pr])
                        kb.mm(pi_[:], bblk[:, hh, 1, c0:c0 + 128], uTb[:, hh, sl], True, True, [bblk, uTb], [pi_])
                        kb.ts(tl["ang"][:], C("trow"), sm["th"][:, scn:scn + 1], ALU.mult, [cst, sm["th"]], [tl["ang"]],
                              s2=thoff[:, scn, tb:tb + 1], op1=ALU.add)
                        sincos(tl["ang"][:], tl["ang"], tl["s"][:], tl["s"], tl["c"][:], tl["c"], tl["r"][:], tl["r"], tl["tmp"][:], tl["tmp"], kint[:])
                        kb.tt(tl["a1"][:], pr[:], tl["c"][:], ALU.mult, [pr, tl["c"]], [tl["a1"]])
                        kb.tt(tl["a2"][:], pi_[:], tl["s"][:], ALU.mult, [pi_, tl["s"]], [tl["a2"]])
                        kb.tt(tl["xr"][:], tl["a1"][:], tl["a2"][:], ALU.add, [tl["a1"], tl["a2"]], [tl["xr"]], eng="pool")
                        kb.tt(tl["a1"][:], pi_[:], tl["c"][:], ALU.mult, [pi_, tl["c"]], [tl["a1"]])
                        kb.tt(tl["a2"][:], pr[:], tl["s"][:], ALU.mult, [pr, tl["s"]], [tl["a2"]])
                        kb.tt(tl["xi"][:], tl["a1"][:], tl["a2"][:], ALU.subtract, [tl["a1"], tl["a2"]], [tl["xi"]], eng="pool")
                        S.op("dve", lambda e, scn=scn: e.tensor_tensor_scan(out=tl["hr"][:], data0=amb[:], data1=tl["xr"][:],
                                                                          initial=hlast[:, 0, scn:scn + 1], op0=ALU.mult, op1=ALU.add),
                             [amb, tl["xr"], hlast], [tl["hr"]])
                        S.op("dve", lambda e, scn=scn: e.tensor_tensor_scan(out=tl["hi"][:], data0=amb[:], data1=tl["xi"][:],
                                                                          initial=hlast[:, 1, scn:scn + 1], op0=ALU.mult, op1=ALU.add),
                             [amb, tl["xi"], hlast], [tl["hi"]])
                        kb.copy(hlast[:, 0, scn:scn + 1], tl["hr"][:, 511:512], [tl["hr"]], [hlast])
                        kb.copy(hlast[:, 1, scn:scn + 1], tl["hi"][:, 511:512], [tl["hi"]], [hlast])
                        kb.tt(tl["a1"][:], tl["hr"][:], tl["c"][:], ALU.mult, [tl["hr"], tl["c"]], [tl["a1"]])
                        kb.tt(tl["a2"][:], tl["hi"][:], tl["s"][:], ALU.mult, [tl["hi"], tl["s"]], [tl["a2"]])
                        kb.tt(hp[0][:], tl["a1"][:], tl["a2"][:], ALU.subtract, [tl["a1"], tl["a2"]], [hp[0]], eng="pool")
                        kb.tt(tl["a1"][:], tl["hr"][:], tl["s"][:], ALU.mult, [tl["hr"], tl["s"]], [tl["a1"]])
                        kb.tt(tl["a2"][:], tl["hi"][:], tl["c"][:], ALU.mult, [tl["hi"], tl["c"]], [tl["a2"]])
                        kb.tt(hp[1][:], tl["a1"][:], tl["a2"][:], ALU.add, [tl["a1"], tl["a2"]], [hp[1]], eng="pool")
                        j = scn % 4
                        kb.mm(yps[:], cz[:, 0, scn, :], hp[0][:], j == 0, False, [cz, hp[0]], [yps])
                        kb.mm(yps[:], cz[:, 1, scn, :], hp[1][:], False, j == 3, [cz, hp[1]], [yps])
                        if j == 3:
                            kb.stt(ypre[:, hh, sl], uT32[:, hh, sl], P("s5_d", l, hh), yps[:], ALU.mult, ALU.add, [uT32, pv, yps], [ypre])
                sg = Ring(kb, "s5sg", [128, 512], F32, 2)
                yT = uTb
                for tb in range(NTB):
                    sl = slice(tb * 512, (tb + 1) * 512)
                    for oc in range(2):
                        vps = ps.get()
                        gps = ps.get()
                        for kc in range(2):
                            kb.mm(vps[:], wglu[:, kc, oc * 128:(oc + 1) * 128], ypre[:, kc, sl], kc == 0, kc == 1, [wglu, ypre], [vps])
                        for kc in range(2):
                            kb.mm(gps[:], wglu[:, kc, 256 + oc * 128:256 + (oc + 1) * 128], ypre[:, kc, sl], kc == 0, kc == 1, [wglu, ypre], [gps])
                        g_ = sg.get()
                        kb.act(g_[:], gps[:], AF.Sigmoid, [gps], [g_])
                        kb.tt(yT[:, oc, sl], vps[:], g_[:], ALU.mult, [vps, g_], [yT])
                if dbg is not None and dbg[0] == "s5":
                    for oc in range(2):
                        kb.copy(uT32[:, oc, :], yT[:, oc, :], [yT], [uT32])
                        S.dma(dbg_d[oc * 128:(oc + 1) * 128, :], uT32[:, oc, :], reads=[uT32])
                S.barrier()
                apply_wout(l, [(yT, (lambda tb, oc=oc: yT[:, oc, tb * 512:(tb + 1) * 512]), GW + 128 * oc, 128) for oc in range(2)], sub=True)
                kb.scope = sc
                S.barrier()
            kb.scope = es


        def deltanet(l):
            with ExitStack() as sc:
                kb.scope = sc
                qkvT = [kb.alloc("nqkv%d" % i, [128, 2, SEQ], F32) for i in range(3)]
                with ExitStack() as scA:
                    kb.scope = scA
                    wdn = kb.alloc("wdn", [128, 8, 768], BF16)
                    S.dma(wdn[:], w_in_d[l, :, C_NQ:C_NQ + 768].rearrange("(c p) n -> p c n", p=128), writes=[wdn], q="pool")
                    xpad = kb.alloc("xpad", [128, SEQ + 3], F32)
                    accv = kb.alloc("accv", [128, SEQ], F32)
                    kb.memset(xpad[:, 0:3], 0.0, [xpad])
                    sqb = Ring(kb, "nsqb", [128, 512], BF16, 2)
                    sdr = Ring(kb, "nsdr", [128, 512], F32, 2)
                    for fc in range(6):
                        i, pc = fc // 2, fc % 2
                        for tb in range(NTB):
                            sl = slice(tb * 512, (tb + 1) * 512)
                            pp = ps.get()
                            for c in range(8):
                                kb.mm(pp[:], wdn[:, c, fc * 128:(fc + 1) * 128], nT[:, c, sl], c == 0, c == 7, [wdn, nT], [pp])
                            kb.copy(xpad[:, 3 + tb * 512:3 + (tb + 1) * 512], pp[:], [pp], [xpad], eng="act")
                        cw = lambda k_: pv[:, PV.off["dn_conv"] + l * 24 + k_ * 6 + fc: PV.off["dn_conv"] + l * 24 + k_ * 6 + fc + 1]
                        kb.ts(accv[:], xpad[:, 0:SEQ], cw(0), ALU.mult, [xpad, pv], [accv])
                        for k_ in range(1, 4):
                            kb.stt(accv[:], xpad[:, k_:k_ + SEQ], cw(k_), accv[:], ALU.mult, ALU.add, [xpad, pv, accv], [accv])
                        if i == 2:
                            kb.act(qkvT[2][:, pc, :], accv[:], AF.Silu, [accv], [qkvT[2]])
                            continue
                        kb.act(accv[:], accv[:], AF.Silu, [accv], [accv])
                        for tb in range(NTB):
                            sl = slice(tb * 512, (tb + 1) * 512)
                            q = sqb.get()
                            kb.act(q[:], accv[:, sl], AF.Square, [accv], [q])
                            st = ps.get()
                            kb.mm(st[:], C("bones", bf=True), q[:], True, True, [q, cstb], [st])
                            s1 = sdr.get()
                            kb.act(s1[:], st[:], AF.Sqrt, [st], [s1], scale=1.0, bias=EPS)
                            kb.recip(s1[:], s1[:], [s1], [s1])
                            if i == 0:
                                kb.stt(qkvT[0][:, pc, sl], accv[:, sl], 0.125, s1[:], ALU.mult, ALU.mult, [accv, s1], [qkvT[0]])
                            else:
                                kb.tt(qkvT[1][:, pc, sl], accv[:, sl], s1[:], ALU.mult, [accv, s1], [qkvT[1]])
                    S.barrier()
                kb.scope = sc
                qT_, kT_, vT_ = qkvT
                if stop == "D1":
                    kb.scope = es
                    return
                wab = kb.alloc("wab", [128, 8, 8], BF16)
                S.dma(wab[:], w_in_d[l, :, C_A:C_A + 8].rearrange("(c p) n -> p c n", p=128), writes=[wab], q="pool")
                wg = kb.alloc("wgate", [128, 8, 256], BF16)
                S.dma(wg[:], w_in_d[l, :, C_G:C_G + 256].rearrange("(c p) n -> p c n", p=128), writes=[wg], q="pool")
                drow = kb.alloc("drow", [128, 192], F32)
                S.dma(drow[:], dnrow_d[l], writes=[drow])
                sc_ = {n_: kb.alloc("dn_" + n_, [128, 64], F32) for n_ in ("a", "b", "beta", "g", "gc", "gl", "eg", "ekd", "dl", "beg", "t")}
                for tt_ in range(NTT):
                    pp = ps.get()
                    for c in range(8):
                        kb.mm(pp[:, 0:8], nT[:, c, tt_ * 128:(tt_ + 1) * 128], wab[:, c, :], c == 0, c == 7, [nT, wab], [pp])
                    kb.copy(sc_["a"][:, tt_ * 4:(tt_ + 1) * 4], pp[:, 0:4], [pp], [sc_["a"]], eng="act")
                    kb.copy(sc_["b"][:, tt_ * 4:(tt_ + 1) * 4], pp[:, 4:8], [pp], [sc_["b"]], eng="act")
                kb.act(sc_["beta"][:], sc_["b"][:], AF.Sigmoid, [sc_["b"]], [sc_["beta"]])
                kb.tt(sc_["t"][:], sc_["a"][:], drow[:, 64:128], ALU.add, [sc_["a"], drow], [sc_["t"]])
                kb.act(sc_["t"][:], sc_["t"][:], AF.Exp, [sc_["t"]], [sc_["t"]])
                kb.act(sc_["t"][:], sc_["t"][:], AF.Ln, [sc_["t"]], [sc_["t"]], scale=1.0, bias=1.0)
                kb.act(sc_["g"][:], drow[:, 0:64], AF.Exp, [drow], [sc_["g"]])
                kb.stt(sc_["g"][:], sc_["g"][:], -1.0, sc_["t"][:], ALU.mult, ALU.mult, [sc_["g"], sc_["t"]], [sc_["g"]])
                pp = ps.get()
                kb.mm(pp[:, 0:64], C("tri"), sc_["g"][:], True, True, [cst, sc_["g"]], [pp])
                kb.copy(sc_["gc"][:], pp[:, 0:64], [pp], [sc_["gc"]])
                pp = ps.get()
                kb.mm(pp[:, 0:64], C("ones"), sc_["g"][:], True, True, [cst, sc_["g"]], [pp])
                kb.copy(sc_["gl"][:], pp[:, 0:64], [pp], [sc_["gl"]])
                kb.act(sc_["eg"][:], sc_["gc"][:], AF.Exp, [sc_["gc"]], [sc_["eg"]])
                kb.tt(sc_["t"][:], sc_["gl"][:], sc_["gc"][:], ALU.subtract, [sc_["gl"], sc_["gc"]], [sc_["t"]])
                kb.act(sc_["ekd"][:], sc_["t"][:], AF.Exp, [sc_["t"]], [sc_["ekd"]])
                kb.act(sc_["dl"][:], sc_["gl"][:], AF.Exp, [sc_["gl"]], [sc_["dl"]])
                kb.tt(sc_["beg"][:], sc_["beta"][:], sc_["eg"][:], ALU.mult, [sc_["beta"], sc_["eg"]], [sc_["beg"]])
                if stop == "D2":
                    S.barrier(); kb.scope = es
                    return
                yTdn = kb.alloc("yTdn", [128, 2, SEQ], BF16)
                scC = ExitStack()
                kb.scope = scC
                Sst = [kb.alloc("Sst%d" % i, [128, 64], F32) for i in range(4)]
                for t_ in Sst:
                    kb.memset(t_[:], 0.0, [t_])
                rwp = [kb.alloc("rwp%d" % i, [128, 128], F32) for i in range(4)]
                kdp = [kb.alloc("kdp%d" % i, [128, 128], F32) for i in range(4)]
                for t_ in rwp + kdp:
                    kb.memset(t_[:], 0.0, [t_])
                sq128 = Ring(kb, "nm", [128, 128], F32, 28)
                ringE = Ring(kb, "nme", [128, 128], F32, 8)
                ringA = Ring(kb, "nma", [128, 128], F32, 4)
                t64 = Ring(kb, "n64", [128, 64], F32, 16)
                tokr = Ring(kb, "ntok", [128, 128], F32, 4)
                otok = Ring(kb, "otok", [128, 256], F32, 2)
                w256 = Ring(kb, "w256", [128, 256], F32, 3)
                r4 = Ring(kb, "r4", [128, 4], F32, 2)
                wTt = [kb.alloc("wTt%d" % i, [128, 128], F32) for i in range(4)]
                for c_ in range(NTT):
                    tsl = slice(c_ * 128, (c_ + 1) * 128)
                    ktok, vtok = [], []
                    for pc in range(2):
                        for (src, lst) in ((kT_, ktok), (vT_, vtok)):
                            tp = ps.get()
                            kb.tr(tp[:, 0:128], src[:, pc, tsl], C("ident"), [src, cst], [tp])
                            tk = tokr.get()
                            kb.copy(tk[:], tp[:, 0:128], [tp], [tk], eng="act")
                            lst.append(tk)
                    ot = otok.get()
                    HS = range(4)
                    idx = [c_ * 4 + h for h in HS]
                    pcs = [h // 2 for h in HS]
                    p0s = [64 * (h % 2) for h in HS]
                    col = lambda t_, h: t_[:, idx[h]:idx[h] + 1]
                    kTh = [kT_[p0s[h]:p0s[h] + 64, pcs[h], tsl] for h in HS]
                    qTh = [qT_[p0s[h]:p0s[h] + 64, pcs[h], tsl] for h in HS]
                    khs = [ktok[pcs[h]][:, p0s[h]:p0s[h] + 64] for h in HS]
                    vhs = [vtok[pcs[h]][:, p0s[h]:p0s[h] + 64] for h in HS]
                    gbc, B_ps, KK_ps, x1, Lm, dtt, QK_ps, aqk, LT_ps, PT, Tt = ([None] * 4 for _ in range(11))
                    for h in HS:
                        gbc[h] = ringE.get()
                        kb.ts(gbc[h][:], C("ones"), col(sc_["g"], h), ALU.mult, [cst, sc_["g"]], [gbc[h]])
                    for h in HS:
                        B_ps[h] = ps.get()
                        kb.mm(B_ps[h][:, 0:128], gbc[h][:], C("tri"), True, True, [gbc[h], cst], [B_ps[h]])
                        kb.mm(B_ps[h][:, 128:256], kTh[h], kTh[h], True, True, [kT_], [B_ps[h]])
                        kb.mm(B_ps[h][:, 256:384], kTh[h], qTh[h], True, True, [kT_, qT_], [B_ps[h]])
                    for h in HS:
                        x1[h] = ringE.get()
                        kb.stt(x1[h][:], B_ps[h][:, 0:128], col(sc_["gc"], h), C("negS"), ALU.subtract, ALU.subtract, [B_ps[h], sc_["gc"], cst], [x1[h]])
                        dtt[h] = ringE.get()
                        kb.stt(dtt[h][:], B_ps[h][:, 0:128], col(sc_["gc"], h), C("negT"), ALU.subtract, ALU.add, [B_ps[h], sc_["gc"], cst], [dtt[h]])
                    for h in HS:
                        kb.act(x1[h][:], x1[h][:], AF.Exp, [x1[h]], [x1[h]], scale=-1.0)
                        kb.act(dtt[h][:], dtt[h][:], AF.Exp, [dtt[h]], [dtt[h]])
                    for h in HS:
                        Lm[h] = sq128.get()
                        kb.stt(Lm[h][:], B_ps[h][:, 128:256], col(sc_["beta"], h), x1[h][:], ALU.mult, ALU.mult, [B_ps[h], sc_["beta"], x1[h]], [Lm[h]])
                        aqk[h] = ringA.get()
                        kb.tt(aqk[h][:], B_ps[h][:, 256:384], dtt[h][:], ALU.mult, [B_ps[h], dtt[h]], [aqk[h]])
                    for h in HS:
                        LT_ps[h] = ps.get()
                        kb.tr(LT_ps[h][:, 0:128], Lm[h][:], C("ident"), [Lm[h], cst], [LT_ps[h]])
                    for h in HS:
                        PT[h] = sq128.get()
                        kb.copy(PT[h][:], LT_ps[h][:, 0:128], [LT_ps[h]], [PT[h]], eng="act")
                        Tt[h] = sq128.get()
                        kb.tt(Tt[h][:], C("ident"), LT_ps[h][:, 0:128], ALU.subtract, [cst, LT_ps[h]], [Tt[h]])
                    Pm = list(Lm)
                    for it in range(6):
                        pq = [None] * 4
                        for h in HS:
                            pq[h] = ps.get()
                            kb.mm(pq[h][:, 0:128], PT[h][:], Pm[h][:], True, True, [PT[h], Pm[h]], [pq[h]])
                            if it < 5:
                                kb.mm(pq[h][:, 128:256], Pm[h][:], PT[h][:], True, True, [PT[h], Pm[h]], [pq[h]])
                        P2 = [None] * 4
                        PT2 = [None] * 4
                        for h in HS:
                            P2[h] = sq128.get()
                            kb.copy(P2[h][:], pq[h][:, 0:128], [pq[h]], [P2[h]], eng="act")
                            if it < 5:
                                PT2[h] = sq128.get()
                                kb.copy(PT2[h][:], pq[h][:, 128:256], [pq[h]], [PT2[h]], eng="act" if h % 2 else "dve")
                        tq = [None] * 4
                        for h in HS:
                            tq[h] = ps.get()
                            kb.mm(tq[h][:, 0:128], P2[h][:], Tt[h][:], True, True, [P2[h], Tt[h]], [tq[h]])
                        for h in HS:
                            Ttn = sq128.get()
                            kb.tt(Ttn[:], Tt[h][:], tq[h][:, 0:128], ALU.add, [Tt[h], tq[h]], [Ttn])
                            Tt[h] = Ttn
                            Pm[h] = P2[h]
                            if it < 5:
                                PT[h] = PT2[h]
                    if stop == "D3":
                        S.barrier(); scC.close(); kb.scope = es
                        return
                    ru, us, vnew, o1 = ([None] * 4 for _ in range(4))
                    for h in HS:
                        p0 = p0s[h]
                        kb.ts(rwp[h][:, p0:p0 + 64], khs[h], col(sc_["beg"], h), ALU.mult, [ktok[pcs[h]], sc_["beg"]], [rwp[h]])
                        ru[h] = t64.get()
                        kb.ts(ru[h][:], vhs[h], col(sc_["beta"], h), ALU.mult, [vtok[pcs[h]], sc_["beta"]], [ru[h]])
                        kb.ts(kdp[h][:, p0:p0 + 64], khs[h], col(sc_["ekd"], h), ALU.mult, [ktok[pcs[h]], sc_["ekd"]], [kdp[h]])
                    wu_ps = [None] * 4
                    for h in HS:
                        wu_ps[h] = ps.get()
                        kb.mm(wu_ps[h][:, 0:128], rwp[h][:], Tt[h][:], True, True, [rwp[h], Tt[h]], [wu_ps[h]])
                        kb.mm(wu_ps[h][:, 128:192], Tt[h][:], ru[h][:], True, True, [Tt[h], ru[h]], [wu_ps[h]])
                    for h in HS:
                        p0 = p0s[h]
                        kb.copy(wTt[h][p0:p0 + 64, :], wu_ps[h][p0:p0 + 64, 0:128], [wu_ps[h]], [wTt[h]], eng="act")
                        us[h] = t64.get()
                        kb.copy(us[h][:], wu_ps[h][:, 128:192], [wu_ps[h]], [us[h]])
                    sq_ps = [None] * 4
                    for h in HS:
                        p0 = p0s[h]
                        sq_ps[h] = ps.get()
                        kb.mm(sq_ps[h][:, 0:64], wTt[h][p0:p0 + 64, :], Sst[h][p0:p0 + 64, :], True, True, [wTt[h], Sst[h]], [sq_ps[h]])
                        kb.mm(sq_ps[h][:, 64:128], qTh[h], Sst[h][p0:p0 + 64, :], True, True, [qT_, Sst[h]], [sq_ps[h]])
                    for h in HS:
                        vnew[h] = t64.get()
                        kb.tt(vnew[h][:], us[h][:], sq_ps[h][:, 0:64], ALU.subtract, [us[h], sq_ps[h]], [vnew[h]])
                        o1[h] = t64.get()
                        kb.ts(o1[h][:], sq_ps[h][:, 64:128], col(sc_["eg"], h), ALU.mult, [sq_ps[h], sc_["eg"]], [o1[h]])
                    ak_ps = [None] * 4
                    for h in HS:
                        ak_ps[h] = ps.get()
                        kb.mm(ak_ps[h][:, 0:64], aqk[h][:], vnew[h][:], True, True, [aqk[h], vnew[h]], [ak_ps[h]])
                        kb.mm(ak_ps[h][:, 64:128], kdp[h][:], vnew[h][:], True, True, [kdp[h], vnew[h]], [ak_ps[h]])
                    for h in HS:
                        p0 = p0s[h]
                        kb.tt(ot[:, h * 64:(h + 1) * 64], o1[h][:], ak_ps[h][:, 0:64], ALU.add, [o1[h], ak_ps[h]], [ot])
                        kb.stt(Sst[h][p0:p0 + 64, :], Sst[h][p0:p0 + 64, :], sc_["dl"][p0:p0 + 64, idx[h]:idx[h] + 1], ak_ps[h][p0:p0 + 64, 64:128],
                               ALU.mult, ALU.add, [Sst[h], sc_["dl"], ak_ps[h]], [Sst[h]])
                    if stop == "D4":
                        S.barrier(); scC.close(); kb.scope = es
                        return
                    g_ps = ps.get()
                    for c in range(8):
                        kb.mm(g_ps[:, 0:256], nT[:, c, tsl], wg[:, c, :], c == 0, c == 7, [nT, wg], [g_ps])
                    sgt = w256.get()
                    kb.act(sgt[:], g_ps[:, 0:256], AF.Silu, [g_ps], [sgt])
                    sq_ = w256.get()
                    kb.act(sq_[:], ot[:], AF.Square, [ot], [sq_])
                    ss = r4.get()
                    S.op("dve", lambda e, ss=ss, sq_=sq_: e.tensor_reduce(out=ss[:], in_=sq_[:].rearrange("p (h d) -> p h d", h=4),
                                                                        axis=mybir.AxisListType.X, op=ALU.add), [sq_], [ss])
                    kb.act(ss[:], ss[:], AF.Sqrt, [ss], [ss], scale=1.0 / 64, bias=EPS)
                    kb.recip(ss[:], ss[:], [ss], [ss])
                    yt = w256.get()
                    for h in range(4):
                        kb.stt(yt[:, h * 64:(h + 1) * 64], ot[:, h * 64:(h + 1) * 64], ss[:, h:h + 1], drow[:, 128:192], ALU.mult, ALU.mult,
                               [ot, ss, drow], [yt])
                    kb.tt(yt[:], yt[:], sgt[:], ALU.mult, [yt, sgt], [yt])
                    for pc in range(2):
                        tp = ps.get()
                        kb.tr(tp[:, 0:128], yt[:, pc * 128:(pc + 1) * 128], C("ident"), [yt, cst], [tp])
                        kb.copy(yTdn[:, pc, tsl], tp[:, 0:128], [tp], [yTdn], eng="act")
                    if stop == "D5" or (stop is not None and stop[0] == "T" and int(stop[1:]) == c_):
                        S.barrier(); scC.close(); kb.scope = es
                        return
                S.barrier()
                scC.close()
                kb.scope = sc
                if dbg is not None and dbg[0] == "dn":
                    for pc in range(2):
                        kb.copy(qkvT[0][:, pc, :], yTdn[:, pc, :], [yTdn], [qkvT[0]])
                        S.dma(dbg_d[pc * 128:(pc + 1) * 128, :], qkvT[0][:, pc, :], reads=[qkvT[0]])
                S.barrier()
                apply_wout(l, [(yTdn, (lambda tb, pc=pc: yTdn[:, pc, tb * 512:(tb + 1) * 512]), 3 * GW + 128 * pc, 128) for pc in range(2)], sub=True)
                kb.scope = sc
                S.barrier()
            kb.scope = es

        def ffn_block(l):
            HC = FFN // 128
            with ExitStack() as sc:
                kb.scope = sc
                w13 = Ring(kb, "w13", [128, 8, 256], BF16, 3)
                w2t = Ring(kb, "w2t", [128, D], BF16, 3)
                gT = kb.alloc("gT", [128, HC, 1024], BF16)
                sg = Ring(kb, "sg", [128, 512], F32, 3)
                for half in range(2):
                    t0 = half * 1024
                    for hc in range(HC):
                        w = w13.get()
                        S.dma(w[:, :, 0:128], w1_d[l, :, hc * 128:(hc + 1) * 128].rearrange("(c p) n -> p c n", p=128), writes=[w], q="pool")
                        S.dma(w[:, :, 128:256], w3_d[l, :, hc * 128:(hc + 1) * 128].rearrange("(c p) n -> p c n", p=128), writes=[w], q="pool")
                        for tb in range(2):
                            sl = slice(t0 + tb * 512, t0 + (tb + 1) * 512)
                            a = ps.get()
                            b = ps.get()
                            for c in range(8):
                                kb.mm(a[:], w[:, c, 0:128], nT[:, c, sl], c == 0, c == 7, [w, nT], [a])
                            for c in range(8):
                                kb.mm(b[:], w[:, c, 128:256], nT[:, c, sl], c == 0, c == 7, [w, nT], [b])
                            s = sg.get()
                            kb.act(s[:], a[:], AF.Silu, [a], [s])
                            kb.tt(gT[:, hc, tb * 512:(tb + 1) * 512], s[:], b[:], ALU.mult, [s, b], [gT])
                    for oc in range(8):
                        accs = [ps.get(), ps.get()]
                        for hc in range(HC):
                            w = w2t.get()
                            S.dma(w[:, 0:128], w2_d[l, hc * 128:(hc + 1) * 128, oc * 128:(oc + 1) * 128], writes=[w], q="pool")
                            for tb in range(2):
                                kb.mm(accs[tb][:], w[:, 0:128], gT[:, hc, tb * 512:(tb + 1) * 512], hc == 0, hc == HC - 1, [w, gT], [accs[tb]])
                        for tb in range(2):
                            sl = slice(t0 + tb * 512, t0 + (tb + 1) * 512)
                            kb.tt(hT[:, oc, sl], hT[:, oc, sl], accs[tb][:], ALU.add, [hT, accs[tb]], [hT])
                S.barrier()
            kb.scope = es

        for l in range(nlayers):
            stream_norm("attn_norm", l)
            if stop == "N":
                break
            if "mla" in mixers:
                mla(l)
                kb.scope = es
            if "s5" in mixers:
                s5(l)
            if "dil" in mixers:
                dilated(l)
            if "dn" in mixers:
                deltanet(l)
                kb.scope = es
            if ffn:
                stream_norm("ffn_norm", l)
                ffn_block(l)

        for c in range(8):
            S.dma(yT_d[:, c, :], hT[:, c, :], reads=[hT], q=("sp" if c % 2 == 0 else "pool"))
        S.finish("sp")
        S.emit()
    return nc


class _Lay:
    def __init__(self):
        self.off = {}
        self.per = {}
        self.n = 0

    def add(self, name, n, per=None):
        self.off[name] = (self.n, n) if per is None else self.n
        if per is not None:
            self.per[name] = per
        self.n += n


PV = _Lay()
for _nm, _per in (("attn_norm", 8), ("ffn_norm", 8), ("mla_qn", 2), ("mla_kvn", 1), ("mla_qkq", 1), ("mla_qkk", 1), ("dil_qn", 1), ("dil_kn", 1), ("s5_lr", 8), ("s5_li", 8), ("s5_ldt", 8), ("s5_d", 2), ("dn_conv", 24)):
    PV.add(_nm, NL * _per, per=_per)

CS = _Lay()
for _nm, _n in (("ones", 128), ("ident", 128), ("tri01", 128), ("rotm", 96), ("sel65", 64), ("bones", 128), ("trow", 512), ("tri", 128), ("negT", 128), ("negS", 128)):
    CS.add(_nm, _n)


def _consts():
    c = np.zeros((128, CS.n), np.float32)

    def put(name, m):
        o, n = CS.off[name]
        c[:m.shape[0], o:o + m.shape[1]] = m
    put("ones", np.ones((128, 128), np.float32))
    put("ident", np.eye(128, dtype=np.float32))
    k = np.arange(128)[:, None]
    q = np.arange(128)[None, :]
    put("tri01", (q >= k).astype(np.float32))
    rot = np.zeros((96, 96), np.float32)
    for m in range(64, 80):
        rot[m + 16, m] = -1.0
    for m in range(80, 96):
        rot[m - 16, m] = 1.0
    put("rotm", rot)
    sel = np.zeros((65, 64), np.float32)
    sel[64, :] = 1.0
    put("sel65", sel)
    bo = np.zeros((128, 128), np.float32)
    bo[0:64, 0:64] = 1.0
    bo[64:128, 64:128] = 1.0
    put("bones", bo)
    pi_ = np.arange(128)[:, None]
    fi_ = np.arange(128)[None, :]
    put("tri", (pi_ <= fi_).astype(np.float32))
    put("negT", np.where(fi_ >= pi_, 0.0, NEG).astype(np.float32))
    put("negS", np.where(pi_ > fi_, 0.0, NEG).astype(np.float32))
    put("trow", np.tile(np.arange(512, dtype=np.float32)[None, :], (128, 1)))
    return c


def _rope_tables():
    half = 16
    freqs = (10000.0 ** (-np.arange(half, dtype=np.float32) / half)).astype(np.float32)
    pos = np.arange(SEQ, dtype=np.float32)
    ang = pos[None, :] * freqs[:, None]
    tab = np.zeros((96, 2, SEQ), np.float32)
    tab[0:64, 0, :] = 1.0
    tab[64:80, 0, :] = np.cos(ang)
    tab[80:96, 0, :] = np.cos(ang)
    tab[64:80, 1, :] = np.sin(ang)
    tab[80:96, 1, :] = np.sin(ang)
    return tab


def _pvec(inp):
    pvv = np.zeros((128, PV.n), np.float32)

    def put(name, l, arr):
        o = PV.off[name] + l * PV.per[name]
        a = np.asarray(arr, np.float32)
        if a.shape[0] <= 128:
            pvv[:a.shape[0], o] = a
        else:
            k = a.shape[0] // 128
            pvv[:, o:o + k] = a.reshape(k, 128).T
    for l in range(NL):
        put("attn_norm", l, inp["attn_norm"][l])
        put("ffn_norm", l, inp["ffn_norm"][l])
        put("mla_qn", l, inp["mla_q_norm"][l])
        put("mla_kvn", l, inp["mla_kv_norm"][l])
        put("mla_qkq", l, inp["mla_qk_q"][l])
        put("mla_qkk", l, inp["mla_qk_k"][l])
        put("dil_qn", l, np.tile(np.asarray(inp["dil_q_norm"][l], np.float32), 2))
        put("dil_kn", l, np.tile(np.asarray(inp["dil_k_norm"][l], np.float32), 2))
        put("s5_lr", l, np.asarray(inp["s5_lambda_re"][l], np.float32).reshape(-1))
        put("s5_li", l, np.asarray(inp["s5_lambda_im"][l], np.float32).reshape(-1))
        put("s5_ldt", l, np.repeat(np.asarray(inp["s5_log_dt"][l], np.float32), 64))
        put("s5_d", l, inp["s5_d"][l])
        cw = np.asarray(inp["dn_conv"][l], np.float32)
        o_ = PV.off["dn_conv"] + l * 24
        for k_ in range(4):
            pvv[:, o_ + k_ * 6:o_ + k_ * 6 + 6] = cw[k_].reshape(6, 128).T
    return pvv


def _t5_bucket_np(dist):
    exact = 16
    df = np.maximum(dist, 1).astype(np.float32)
    large = exact + (np.log(df / np.float32(exact)) / np.float32(math.log(2048 / exact)) * np.float32(32 - exact)).astype(np.int32)
    large = np.minimum(large, 31)
    return np.where(dist < exact, dist, large)


def _dil_bias(t5):
    t5 = np.asarray(t5, np.float32)
    k = np.arange(128)[:, None]
    q = np.arange(256)[None, :]
    delta = q - k
    valid = (delta >= 0) & (delta <= 128)
    out = np.zeros((128, 12, 256), np.float32)
    for bi, dil in enumerate((1, 4, 16)):
        bucket = _t5_bucket_np(np.clip(delta, 0, 128) * dil)
        for h in range(4):
            out[:, bi * 4 + h, :] = np.where(valid, t5[bucket, h], np.float32(NEG))
    return out


def _s5_blocks(inp):
    bre, bim = np.asarray(inp["s5_b_re"], np.float32), np.asarray(inp["s5_b_im"], np.float32)
    cre, cim = np.asarray(inp["s5_c_re"], np.float32), np.asarray(inp["s5_c_im"], np.float32)
    b = np.zeros((NL, 128, 2, 2, 512), np.float32)
    c = np.zeros((NL, 128, 2, 8, 128), np.float32)
    for g in range(16):
        hh, gl = g // 8, g % 8
        for ri, (bb, cc) in enumerate(((bre, cre), (bim, cim))):
            b[:, gl * 16:(gl + 1) * 16, hh, ri, gl * 64:(gl + 1) * 64] = bb[:, g].transpose(0, 2, 1)
            c[:, (g % 2) * 64:(g % 2) * 64 + 64, ri, g // 2, gl * 16:(gl + 1) * 16] = cc[:, g].transpose(0, 2, 1)
    return b, c


def _dn_rows(inp):
    out = np.zeros((NL, 128, 192), np.float32)
    for l in range(NL):
        out[l, :, 0:64] = np.tile(np.asarray(inp["dn_a_log"][l], np.float32), 16)[None, :]
        out[l, :, 64:128] = np.tile(np.asarray(inp["dn_dt_bias"][l], np.float32), 16)[None, :]
        out[l, :, 128:192] = np.asarray(inp["dn_o_norm"][l], np.float32)[None, :]
    return out


def prep_shared(inp):
    f = lambda a: np.ascontiguousarray(np.asarray(a, np.float32))
    w_ukv = f(inp["mla_w_ukv"])
    w_ukp = np.zeros((NL, 128, 4 * 96), np.float32)
    w_uv = np.zeros((NL, 128, 256), np.float32)
    for h in range(4):
        w_ukp[:, :, h * 96:h * 96 + 64] = w_ukv[:, :, h * 128:h * 128 + 64]
        w_uv[:, :, h * 64:(h + 1) * 64] = w_ukv[:, :, h * 128 + 64:h * 128 + 128]
    w_in = f(inp["w_in"])
    w_krp = np.zeros((NL, D, 96), np.float32)
    w_krp[:, :, 64:96] = w_in[:, :, C_KR:C_KR + 32]
    return {
        "w_in": w_in, "w_out": f(inp["w_out"]), "pvec": _pvec(inp), "cst128": _consts(),
        "w_uq": f(inp["mla_w_uq"]), "w_ukp": w_ukp, "w_uv": w_uv, "w_krp": w_krp, "rope": _rope_tables(), "dil_bias": _dil_bias(inp["t5_bias"]), "dn_row": _dn_rows(inp),
        "s5_b": _s5_blocks(inp)[0], "s5_c": _s5_blocks(inp)[1], "s5_w_glu": f(inp["s5_w_glu"]),
        "ffn_w1": f(inp["ffn_w1"]), "ffn_w3": f(inp["ffn_w3"]), "ffn_w2": f(inp["ffn_w2"]),
    }


def x_to_core(xb):
    return np.ascontiguousarray(xb.T.reshape(8, 128, SEQ).transpose(1, 0, 2))


def core_to_y(yT):
    return np.ascontiguousarray(yT.transpose(1, 0, 2).reshape(D, SEQ).T)


_NC_CACHE = {}


def kernel(**inputs):
    x = np.asarray(inputs["x"], np.float32)
    shared = prep_shared(inputs)
    if "nc" not in _NC_CACHE:
        _NC_CACHE["nc"] = build()
    nc = _NC_CACHE["nc"]
    in_maps = []
    for b in range(8):
        m = dict(shared)
        m["xT"] = x_to_core(x[b])
        in_maps.append(m)
    res = run_bass_kernel_spmd(nc, in_maps, core_ids=list(range(8)))
    out = np.stack([core_to_y(np.asarray(r["yT"])) for r in res.results], axis=0)
    return out.astype(np.float32)
```

```python
import math
import numpy as np
from contextlib import ExitStack
import concourse.bass as bass
import concourse.mybir as mybir
from concourse.bass_utils import run_bass_kernel_spmd

F32 = mybir.dt.float32
BF16 = mybir.dt.bfloat16
ALU = mybir.AluOpType
AF = mybir.ActivationFunctionType

D = 1024
SEQ = 2048
NL = 2
GW = 256
IN_COLS = 2472
FFN = 2816
EPS = 1e-6
NEG = -30000.0
NTB = 4
NTT = 16
C_CQ, C_CKV, C_KR, C_U, C_DQ, C_DK, C_DV, C_NQ, C_NK, C_NV, C_A, C_B, C_G = (
    0, 256, 384, 416, 672, 928, 1184, 1440, 1696, 1952, 2208, 2212, 2216)


class Dep:
    __slots__ = ("w", "r")

    def __init__(self):
        self.w = None
        self.r = {}


class T:
    __slots__ = ("t", "d", "ex")

    def __init__(self, t, ex=False):
        self.t = t
        self.d = Dep()
        self.ex = ex

    def __getitem__(self, k):
        return self.t[k]


class Sched:
    ENG = ("pe", "act", "dve", "pool", "sp")

    def __init__(self, nc, es, n_dma_sems=8):
        self.nc = nc
        self.sem = {k: es.enter_context(nc.semaphore("s_" + k)) for k in ("pe", "act", "dve", "pool")}
        self.cnt = {k: 0 for k in ("pe", "act", "dve", "pool")}
        self.ndma = n_dma_sems
        for pre in ("q", "g"):
            for i in range(n_dma_sems):
                k = "%s%d" % (pre, i)
                self.sem[k] = es.enter_context(nc.semaphore("s_" + k))
                self.cnt[k] = 0
        self.dma_rr = {"sp": 0, "pool": 0}
        self.seen = {}
        self.ninst = 0
        self.prog = {k: [] for k in self.ENG}

    @staticmethod
    def _inc(k):
        return 16 if k[0] in "qg" else 1

    def _wait(self, e, k, c):
        s = self.seen.setdefault(e, {})
        if s.get(k, 0) >= c:
            return
        self.prog[e].append(("w", self.sem[k], c * self._inc(k)))
        s[k] = c

    @staticmethod
    def _deps(reads, writes):
        d = {}
        for t in reads:
            t = t.d if isinstance(t, T) else t
            if t.w is not None:
                k, c = t.w
                d[k] = max(d.get(k, 0), c)
        for t in writes:
            t = t.d if isinstance(t, T) else t
            if t.w is not None:
                k, c = t.w
                d[k] = max(d.get(k, 0), c)
            for k, c in t.r.items():
                d[k] = max(d.get(k, 0), c)
        return d

    def _mark(self, k, c, reads, writes):
        for t in reads:
            t = t.d if isinstance(t, T) else t
            t.r[k] = c
        for t in writes:
            t = t.d if isinstance(t, T) else t
            t.w = (k, c)
            t.r = {}

    def op(self, e, fn, reads=(), writes=()):
        exr = [t for t in reads if isinstance(t, T) and t.ex]
        if exr:
            writes = list(writes) + exr
        d = self._deps(reads, writes)
        for k, c in d.items():
            if k == e and e == "pe":
                continue
            self._wait(e, k, c)
        self.cnt[e] += 1
        self.prog[e].append(("i", fn, self.sem[e], 1))
        self._mark(e, self.cnt[e], reads, writes)
        self.ninst += 1

    def dma(self, out, in_, reads=(), writes=(), q="sp"):
        i = self.dma_rr[q]
        self.dma_rr[q] = (i + 1) % self.ndma
        k = ("q%d" if q == "sp" else "g%d") % i
        d = self._deps(reads, writes)
        if self.cnt[k] > 0:
            d[k] = max(d.get(k, 0), self.cnt[k])
        for kk, c in d.items():
            self._wait(q, kk, c)
        self.cnt[k] += 1
        self.prog[q].append(("i", (lambda e, out=out, in_=in_: e.dma_start(out=out, in_=in_)), self.sem[k], 16))
        self._mark(k, self.cnt[k], reads, writes)
        self.ninst += 1

    def barrier(self):
        for e in self.ENG:
            for k, c in self.cnt.items():
                if c > 0 and not (k == e == "pe"):
                    self._wait(e, k, c)

    def finish(self, e="sp"):
        for k, c in self.cnt.items():
            if c > 0:
                self._wait(e, k, c)

    def emit(self):
        nc = self.nc
        with nc.Block() as block:
            def mk(k):
                def body(eng):
                    for it in self.prog[k]:
                        if it[0] == "w":
                            eng.wait_ge(it[1], it[2])
                        else:
                            it[1](eng).then_inc(it[2], it[3])
                return body
            for k, reg in (("sp", block.sync), ("pe", block.tensor), ("act", block.scalar),
                           ("dve", block.vector), ("pool", block.gpsimd)):
                if self.prog[k]:
                    reg(mk(k))


class Ring:
    def __init__(self, kb, name, shape, dt, n, psum=False):
        self.tiles = [kb.alloc("%s%d" % (name, i), shape, dt, psum=psum) for i in range(n)]
        self.i = 0

    def get(self):
        t = self.tiles[self.i]
        self.i = (self.i + 1) % len(self.tiles)
        return t


class KB:
    def __init__(self, nc, es):
        self.nc = nc
        self.es = es
        self.S = Sched(nc, es)
        self.scope = es
        self.uid = 0

    def alloc(self, name, shape, dt, psum=False):
        self.uid += 1
        nm = "%s_%d" % (name, self.uid)
        if psum:
            t = self.scope.enter_context(self.nc.psum_tensor(nm, shape, dt))
        else:
            t = self.scope.enter_context(self.nc.sbuf_tensor(nm, shape, dt))
        return T(t, ex=psum)

    def mm(self, out, lhsT, rhs, start, stop, reads, writes):
        self.S.op("pe", lambda e: e.matmul(out, lhsT=lhsT, rhs=rhs, start=start, stop=stop), reads, writes)

    def tr(self, out, in_, ident, reads, writes):
        self.S.op("pe", lambda e: e.transpose(out, in_=in_, identity=ident), reads, writes)

    def act(self, out, in_, func, reads, writes, scale=None, bias=None, accum_out=None):
        kw = {}
        if scale is not None:
            kw["scale"] = scale
        if bias is not None:
            kw["bias"] = bias
        if accum_out is not None:
            kw["accum_out"] = accum_out
        self.S.op("act", lambda e: e.activation(out=out, in_=in_, func=func, **kw), reads, writes)

    def tt(self, out, in0, in1, op, reads, writes, eng="dve"):
        self.S.op(eng, lambda e: e.tensor_tensor(out=out, in0=in0, in1=in1, op=op), reads, writes)

    def ts(self, out, in0, s1, op0, reads, writes, s2=None, op1=None, eng="dve"):
        if op1 is None:
            self.S.op(eng, lambda e: e.tensor_scalar(out=out, in0=in0, scalar1=s1, scalar2=None, op0=op0), reads, writes)
        else:
            self.S.op(eng, lambda e: e.tensor_scalar(out=out, in0=in0, scalar1=s1, scalar2=s2, op0=op0, op1=op1), reads, writes)

    def stt(self, out, in0, scalar, in1, op0, op1, reads, writes):
        self.S.op("dve", lambda e: e.scalar_tensor_tensor(out=out, in0=in0, scalar=scalar, in1=in1, op0=op0, op1=op1), reads, writes)

    def copy(self, out, in_, reads, writes, eng="dve"):
        if eng == "act":
            self.S.op("act", lambda e: e.copy(out=out, in_=in_), reads, writes)
        else:
            self.S.op(eng, lambda e: e.tensor_copy(out=out, in_=in_), reads, writes)

    def recip(self, out, in_, reads, writes):
        self.S.op("dve", lambda e: e.reciprocal(out=out, in_=in_), reads, writes)

    def memset(self, ap, val, writes, eng="pool"):
        self.S.op(eng, lambda e: e.memset(ap, val), (), writes)


class _Stop(Exception):
    pass


def build(nlayers=NL, mixers=("mla", "s5", "dil", "dn"), ffn=True, dbg=None, stop=None):
    nc = bass.Bass("TRN2", target_bir_lowering=False)

    def din(name, shape):
        return nc.dram_tensor(name, list(shape), F32, kind="ExternalInput").ap()

    xT_d = din("xT", [128, 8, SEQ])
    w_in_d = din("w_in", [NL, D, IN_COLS])
    w_out_d = din("w_out", [NL, D, D])
    pv_d = din("pvec", [128, PV.n])
    cst_d = din("cst128", [128, CS.n])
    w_uq_d = din("w_uq", [NL, 256, 384])
    w_ukp_d = din("w_ukp", [NL, 128, 4 * 96])
    w_uv_d = din("w_uv", [NL, 128, 256])
    w_krp_d = din("w_krp", [NL, D, 96])
    rope_d = din("rope", [96, 2, SEQ])
    s5b_d = din("s5_b", [NL, 128, 2, 2, 512])
    s5c_d = din("s5_c", [NL, 128, 2, 8, 128])
    s5g_d = din("s5_w_glu", [NL, 256, 512])
    dnrow_d = din("dn_row", [NL, 128, 192])
    dilb_d = din("dil_bias", [128, 12, 256])
    w1_d = din("ffn_w1", [NL, D, FFN])
    w3_d = din("ffn_w3", [NL, D, FFN])
    w2_d = din("ffn_w2", [NL, FFN, D])
    yT_d = nc.dram_tensor("yT", [128, 8, SEQ], F32, kind="ExternalOutput").ap()
    dbg_d = None
    if dbg is not None:
        dbg_d = nc.dram_tensor("dbg", list(dbg[1]), F32, kind="ExternalOutput").ap()

    with ExitStack() as es:
        kb = KB(nc, es)
        S = kb.S
        hT = kb.alloc("hT", [128, 8, SEQ], F32)
        nT = kb.alloc("nT", [128, 8, SEQ], BF16)
        pv = kb.alloc("pv", [128, PV.n], F32)
        cst = kb.alloc("cst", [128, CS.n], F32)
        cstb = kb.alloc("cstb", [128, CS.n], BF16)
        ps = Ring(kb, "ps", [128, 512], F32, 6, psum=True)
        psl = Ring(kb, "psl", [128, 512], F32, 2, psum=True)

        S.dma(pv[:], pv_d, writes=[pv])
        S.dma(cst[:], cst_d, writes=[cst])
        S.dma(cstb[:], cst_d, writes=[cstb], q="pool")
        for c in range(8):
            S.dma(hT[:, c, :], xT_d[:, c, :], writes=[hT], q=("sp" if c % 2 == 0 else "pool"))

        def C(name, rows=128, bf=False):
            o, n = CS.off[name]
            return (cstb if bf else cst)[0:rows, o:o + n]

        def P(name, l=0, j=0, rows=128):
            o = PV.off[name] + l * PV.per[name] + j
            return pv[0:rows, o:o + 1]

        def stream_norm(gname, l):
            with ExitStack() as sc:
                kb.scope = sc
                sq = Ring(kb, "sq", [128, 512], BF16, 3)
                sd = Ring(kb, "sd", [128, 512], F32, 2)
                for tb in range(NTB):
                    sl = slice(tb * 512, (tb + 1) * 512)
                    st = ps.get()
                    for c in range(8):
                        q = sq.get()
                        kb.act(q[:], hT[:, c, sl], AF.Square, [hT], [q])
                        kb.mm(st[:], C("ones", bf=True), q[:], c == 0, c == 7, [q, cstb], [st])
                    s1 = sd.get()
                    kb.act(s1[:], st[:], AF.Sqrt, [st], [s1], scale=1.0 / D, bias=EPS)
                    kb.recip(s1[:], s1[:], [s1], [s1])
                    for c in range(8):
                        kb.stt(nT[:, c, sl], hT[:, c, sl], P(gname, l, c), s1[:], ALU.mult, ALU.mult,
                               [hT, pv, s1], [nT])
                S.barrier()
            kb.scope = es

        def apply_wout(l, pieces, sub=False):
            with ExitStack() as sc:
                kb.scope = sc
                wts = []
                for (_, _, row0, K) in pieces:
                    w = kb.alloc("wo", [K, D], BF16)
                    S.dma(w[:], w_out_d[l, row0:row0 + K, :], writes=[w], q="pool")
                    wts.append(w)
                for oc in range(8):
                    for tb in range(NTB):
                        sl = slice(tb * 512, (tb + 1) * 512)
                        acc = ps.get()
                        n = len(pieces)
                        for i, (pt, apf, row0, K) in enumerate(pieces):
                            kb.mm(acc[:], wts[i][:, oc * 128:(oc + 1) * 128], apf(tb), i == 0, i == n - 1,
                                  [wts[i], pt], [acc])
                        kb.tt(hT[:, oc, sl], hT[:, oc, sl], acc[:], ALU.add, [hT, acc], [hT])
                S.barrier()
            kb.scope = es

        def mla(l):
            with ExitStack() as sc:
                kb.scope = sc
                wuq = kb.alloc("wuq", [128, 2, 384], BF16)
                wuk = kb.alloc("wuk", [128, 4 * 96], BF16)
                cqn = kb.alloc("cqn", [128, 2, SEQ], BF16)
                ckvn = kb.alloc("ckvn", [128, SEQ], BF16)
                krp = kb.alloc("krp", [96, SEQ], F32)
                vaug = kb.alloc("vaug", [128, NTT, 4, 72], BF16)
                sqb = Ring(kb, "sqb", [128, 512], BF16, 3)
                sdr = Ring(kb, "sdr", [128, 512], F32, 2)
                scA = ExitStack()
                kb.scope = scA
                wq = kb.alloc("wq", [128, 8, 256], BF16)
                wkv = kb.alloc("wkv", [128, 8, 128], BF16)
                wkr = kb.alloc("wkr", [128, 8, 96], BF16)
                wuv = kb.alloc("wuv", [128, 256], BF16)
                g32 = Ring(kb, "g32", [128, 512], F32, 3)
                S.dma(wq[:], w_in_d[l, :, C_CQ:C_CQ + 256].rearrange("(c p) n -> p c n", p=128), writes=[wq], q="pool")
                S.dma(wkv[:], w_in_d[l, :, C_CKV:C_CKV + 128].rearrange("(c p) n -> p c n", p=128), writes=[wkv], q="pool")
                S.dma(wkr[:], w_krp_d[l].rearrange("(c p) n -> p c n", p=128), writes=[wkr], q="pool")
                S.dma(wuq[:], w_uq_d[l].rearrange("(c p) n -> p c n", p=128), writes=[wuq], q="pool")
                S.dma(wuk[:], w_ukp_d[l], writes=[wuk], q="pool")
                S.dma(wuv[:], w_uv_d[l], writes=[wuv], q="pool")
                kb.memset(vaug[:], 1.0, [vaug])
                if stop == "A0":
                    S.barrier()
                    scA.close()
                    kb.scope = es
                    return
                for tb in range(NTB):
                    sl = slice(tb * 512, (tb + 1) * 512)
                    st = ps.get()
                    gq = []
                    for oc in range(2):
                        pp = ps.get()
                        for c in range(8):
                            kb.mm(pp[:], wq[:, c, oc * 128:(oc + 1) * 128], nT[:, c, sl], c == 0, c == 7, [wq, nT], [pp])
                        if stop == "A1a":
                            S.barrier(); scA.close(); kb.scope = es
                            return
                        g = g32.get()
                        import os as _os
                        if _os.environ.get("VAR", "0") == "4":
                            q4 = sqb.get()
                            kb.act(q4[:], pp[:], AF.Square, [pp], [q4])
                            kb.ts(g[:], pp[:], P("mla_qn", l, oc), ALU.mult, [pp, pv, q4], [g])
                        else:
                            kb.ts(g[:], pp[:], P("mla_qn", l, oc), ALU.mult, [pp, pv], [g])
                        if stop == "A1b":
                            S.barrier(); scA.close(); kb.scope = es
                            return
                        q = sqb.get()
                        import os as _os
                        _v = _os.environ.get("VAR", "0")
                        if _v == "1":
                            kb.act(q[:], g[:], AF.Square, [g], [q])
                        elif _v == "2":
                            kb.copy(q[:], pp[:], [pp], [q], eng="act")
                        elif _v == "3":
                            kb.tt(q[:], pp[:], g[:], ALU.mult, [pp, g], [q])
                        else:
                            kb.act(q[:], pp[:], AF.Square, [pp], [q])
                        if stop == "A1c":
                            S.barrier(); scA.close(); kb.scope = es
                            return
                        kb.mm(st[:], C("ones", bf=True), q[:], oc == 0, oc == 1, [q, cstb], [st])
                        gq.append(g)
                        if stop == "A1d":
                            S.barrier(); scA.close(); kb.scope = es
                            return
                    s1 = sdr.get()
                    kb.act(s1[:], st[:], AF.Sqrt, [st], [s1], scale=1.0 / 256, bias=EPS)
                    kb.recip(s1[:], s1[:], [s1], [s1])
                    for oc in range(2):
                        kb.tt(cqn[:, oc, sl], gq[oc][:], s1[:], ALU.mult, [gq[oc], s1], [cqn])
                    if stop == "A1":
                        S.barrier()
                        scA.close()
                        kb.scope = es
                        return
                    st = ps.get()
                    pp = ps.get()
                    for c in range(8):
                        kb.mm(pp[:], wkv[:, c, :], nT[:, c, sl], c == 0, c == 7, [wkv, nT], [pp])
                    g = g32.get()
                    kb.ts(g[:], pp[:], P("mla_kvn", l, 0), ALU.mult, [pp, pv], [g])
                    q = sqb.get()
                    kb.act(q[:], pp[:], AF.Square, [pp], [q])
                    kb.mm(st[:], C("ones", bf=True), q[:], True, True, [q, cstb], [st])
                    s1 = sdr.get()
                    kb.act(s1[:], st[:], AF.Sqrt, [st], [s1], scale=1.0 / 128, bias=EPS)
                    kb.recip(s1[:], s1[:], [s1], [s1])
                    kb.tt(ckvn[:, sl], g[:], s1[:], ALU.mult, [g, s1], [ckvn])
                    if stop == "A2":
                        S.barrier()
                        scA.close()
                        kb.scope = es
                        return
                    pp = ps.get()
                    for c in range(8):
                        kb.mm(pp[0:96, :], wkr[:, c, :], nT[:, c, sl], c == 0, c == 7, [wkr, nT], [pp])
                    kb.copy(krp[:, sl], pp[0:96, :], [pp], [krp], eng="act")
                    if stop == "A3":
                        S.barrier()
                        scA.close()
                        kb.scope = es
                        return
                    for j in range(4):
                        tt_ = tb * 4 + j
                        pp = ps.get()
                        kb.mm(pp[:, 0:256], ckvn[:, tt_ * 128:(tt_ + 1) * 128], wuv[:], True, True, [ckvn, wuv], [pp])
                        kb.copy(vaug[:, tt_, :, 0:64], pp[:, 0:256].rearrange("p (h d) -> p h d", h=4), [pp], [vaug], eng="act")
                S.barrier()
                scA.close()
                if stop == "A":
                    kb.scope = es
                    return
                with ExitStack() as sc2:
                    kb.scope = sc2
                    rope = kb.alloc("rope", [96, 2, SEQ], F32)
                    S.dma(rope[:], rope_d, writes=[rope])
                    yTr = [kb.alloc("ymla%d" % h, [64, SEQ], BF16) for h in range(2)]
                    yT = [yTr[0], yTr[1], yTr[0], yTr[1]]
                    if dbg is not None and dbg[0] == "mla":
                        tmpf = kb.alloc("dbgf", [64, SEQ], F32)
                    qT = kb.alloc("qT", [96, SEQ], BF16)
                    kT = kb.alloc("kT", [96, SEQ], BF16)
                    pb = Ring(kb, "pb", [128, 512], BF16, 4)
                    osb = Ring(kb, "osb", [65, 512], F32, 2)
                    rcp = Ring(kb, "rcp", [64, 512], F32, 2)
                    qnb = Ring(kb, "qnb", [96, 512], BF16, 2)
                    g96 = Ring(kb, "g96", [96, 512], F32, 3)
                    t32 = Ring(kb, "t32", [96, 512], F32, 3)
                    scale = 96 ** -0.5

                    def normrope(src_ap, src_reads, gname, dst, sl):
                        g = g96.get()
                        kb.ts(g[:], src_ap, P(gname, l, 0, rows=96), ALU.mult, src_reads + [pv], [g])
                        q = sqb.get()
                        kb.act(q[0:96, :], src_ap, AF.Square, src_reads, [q])
                        st = ps.get()
                        kb.mm(st[0:96, :], C("ones", rows=96, bf=True)[:, 0:96], q[0:96, :], True, True, [q, cstb], [st])
                        s1 = sdr.get()
                        kb.act(s1[0:96, :], st[0:96, :], AF.Sqrt, [st], [s1], scale=1.0 / 96, bias=EPS)
                        kb.recip(s1[0:96, :], s1[0:96, :], [s1], [s1])
                        qn = qnb.get()
                        kb.tt(qn[:], g[:], s1[0:96, :], ALU.mult, [g, s1], [qn])
                        rp = ps.get()
                        kb.mm(rp[0:96, :], C("rotm", rows=96, bf=True)[:, 0:96], qn[:], True, True, [qn, cstb], [rp])
                        kb.copy(dst[0:64, sl], qn[0:64, :], [qn], [dst], eng="pool")
                        t1 = t32.get()
                        kb.tt(t1[64:96, :], rp[64:96, :], rope[64:96, 1, sl], ALU.mult, [rp, rope], [t1])
                        t2 = t32.get()
                        kb.tt(t2[64:96, :], qn[64:96, :], rope[64:96, 0, sl], ALU.mult, [qn, rope], [t2])
                        kb.tt(dst[64:96, sl], t1[64:96, :], t2[64:96, :], ALU.add, [t1, t2], [dst])

                    for h in range(4):
                        for tb in range(NTB):
                            sl = slice(tb * 512, (tb + 1) * 512)
                            pp = ps.get()
                            for oc in range(2):
                                kb.mm(pp[0:96, :], wuq[:, oc, h * 96:(h + 1) * 96], cqn[:, oc, sl], oc == 0, oc == 1, [wuq, cqn], [pp])
                            normrope(pp[0:96, :], [pp], "mla_qkq", qT, sl)
                            pp = ps.get()
                            kb.mm(pp[0:96, :], wuk[:, h * 96:(h + 1) * 96], ckvn[:, sl], True, True, [wuk, ckvn], [pp])
                            kp = t32.get()
                            kb.tt(kp[:], pp[0:96, :], krp[:, sl], ALU.add, [pp, krp], [kp])
                            normrope(kp[:], [kp], "mla_qkk", kT, sl)
                        if stop == "B0" or (stop == "B1" and h == 1):
                            S.barrier()
                            kb.scope = es
                            return
                        for qb in range(NTB):
                            o_ps = psl.get()
                            nk = 4 * qb + 4
                            def score(kt):
                                r = kt - 4 * qb
                                q0 = 128 * max(r, 0)
                                n = 512 - q0
                                s_ps = ps.get()
                                kb.mm(s_ps[:, 0:n], kT[:, kt * 128:(kt + 1) * 128], qT[:, qb * 512 + q0:(qb + 1) * 512],
                                      True, True, [kT, qT], [s_ps])
                                return s_ps
                            s_next = score(0)
                            for kt in range(nk):
                                r = kt - 4 * qb
                                q0 = 128 * max(r, 0)
                                n = 512 - q0
                                s_ps = s_next
                                if kt + 1 < nk:
                                    s_next = score(kt + 1)
                                p = pb.get()
                                kb.act(p[:, 0:n], s_ps[:, 0:n], AF.Exp, [s_ps], [p], scale=scale)
                                if r >= 0:
                                    kb.tt(p[:, 0:128], p[:, 0:128], C("tri01", bf=True), ALU.mult, [p, cstb], [p])
                                kb.mm(o_ps[0:65, q0:512], vaug[:, kt, h, 0:65], p[:, 0:n], kt == 0, kt == nk - 1, [vaug, p], [o_ps])
                            o = osb.get()
                            kb.copy(o[:], o_ps[0:65, :], [o_ps], [o], eng="act")
                            bc = ps.get()
                            kb.mm(bc[0:64, :], C("sel65", rows=65)[:, 0:64], o[:], True, True, [cst, o], [bc])
                            rc = rcp.get()
                            kb.recip(rc[:], bc[0:64, :], [bc], [rc])
                            kb.tt(yT[h][:, qb * 512:(qb + 1) * 512], o[0:64, :], rc[:], ALU.mult, [o, rc], [yT[h]])
                            if stop == "B0c":
                                S.barrier(); kb.scope = es
                                return
                        if dbg is not None and dbg[0] == "mla":
                            kb.copy(tmpf[:], yT[h][:], [yT[h]], [tmpf])
                            S.dma(dbg_d[h], tmpf[:], reads=[tmpf])
                        apply_wout(l, [(yT[h], (lambda tb, h=h: yT[h][:, tb * 512:(tb + 1) * 512]), 0 * GW + 64 * h, 64)], sub=True)
                        kb.scope = sc2
                    S.barrier()
            kb.scope = es


        def dilated(l):
            with ExitStack() as sc:
                kb.scope = sc
                acc = kb.alloc("dacc", [65, 4, SEQ], F32)
                kb.memset(acc[:], 0.0, [acc])
                scB = ExitStack()
                kb.scope = scB
                qkv = [kb.alloc("dqkv%d" % i, [128, 2, SEQ], F32) for i in range(3)]
                with ExitStack() as scA:
                    kb.scope = scA
                    wd = [kb.alloc("wd%d" % i, [128, 8, 256], BF16) for i in range(3)]
                    for i, c0 in enumerate((C_DQ, C_DK, C_DV)):
                        S.dma(wd[i][:], w_in_d[l, :, c0:c0 + 256].rearrange("(c p) n -> p c n", p=128), writes=[wd[i]], q="pool")
                    g32 = Ring(kb, "dg32", [128, 512], F32, 2)
                    sqb = Ring(kb, "dsqb", [128, 512], BF16, 2)
                    sdr = Ring(kb, "dsdr", [128, 512], F32, 2)
                    for i in range(3):
                        for pc in range(2):
                            for tb in range(NTB):
                                sl = slice(tb * 512, (tb + 1) * 512)
                                pp = ps.get()
                                for c in range(8):
                                    kb.mm(pp[:], wd[i][:, c, pc * 128:(pc + 1) * 128], nT[:, c, sl], c == 0, c == 7, [wd[i], nT], [pp])
                                if i == 2:
                                    kb.copy(qkv[2][:, pc, sl], pp[:], [pp], [qkv[2]], eng="act")
                                    continue
                                g = g32.get()
                                kb.ts(g[:], pp[:], P("dil_qn" if i == 0 else "dil_kn", l, 0), ALU.mult, [pp, pv], [g])
                                q = sqb.get()
                                kb.act(q[:], pp[:], AF.Square, [pp], [q])
                                st = ps.get()
                                kb.mm(st[:], C("bones", bf=True), q[:], True, True, [q, cstb], [st])
                                s1 = sdr.get()
                                kb.act(s1[:], st[:], AF.Sqrt, [st], [s1], scale=1.0 / 64, bias=EPS)
                                kb.recip(s1[:], s1[:], [s1], [s1])
                                kb.tt(qkv[i][:, pc, sl], g[:], s1[:], ALU.mult, [g, s1], [qkv[i]])
                    S.barrier()
                kb.scope = scB
                bias = kb.alloc("dbias", [128, 12, 256], F32)
                S.dma(bias[:], dilb_d, writes=[bias])
                vsb = Ring(kb, "dvsb", [128, 4, 72], BF16, 2)
                for v_ in vsb.tiles:
                    kb.memset(v_[:], 1.0, [v_])
                tmpr = Ring(kb, "dtmp", [128, 256], F32, 4)
                pbr = Ring(kb, "dpb", [128, 256], BF16, 4)
                scale = 64 ** -0.5
                for bi, (win, dil) in enumerate(((128, 1), (512, 4), (2048, 16))):
                    L = SEQ // dil
                    nb = L // 128
                    for r in range(dil):
                        for n in range(nb):
                            t0_ = r + dil * 128 * n
                            ksl = slice(t0_, t0_ + dil * 127 + 1, dil)
                            nq = 256 if n + 1 < nb else 128
                            qsl = slice(t0_, t0_ + dil * (nq - 1) + 1, dil)
                            vs = vsb.get()
                            for pc in range(2):
                                vp = ps.get()
                                kb.tr(vp[:, 0:128], qkv[2][:, pc, ksl], C("ident"), [qkv[2], cst], [vp])
                                kb.copy(vs[:, 2 * pc:2 * pc + 2, 0:64], vp[:, 0:128].rearrange("p (h d) -> p h d", h=2), [vp], [vs], eng="act")
                            HS = range(4)
                            sps, tms, pbs, ops_ = [None] * 4, [None] * 4, [None] * 4, [None] * 4
                            for h in HS:
                                pc, p0 = h // 2, 64 * (h % 2)
                                sps[h] = ps.get()
                                kb.mm(sps[h][:, 0:nq], qkv[1][p0:p0 + 64, pc, ksl], qkv[0][p0:p0 + 64, pc, qsl], True, True, [qkv[0], qkv[1]], [sps[h]])
                            for h in HS:
                                tms[h] = tmpr.get()
                                kb.stt(tms[h][:, 0:nq], sps[h][:, 0:nq], scale, bias[:, bi * 4 + h, 0:nq], ALU.mult, ALU.add, [sps[h], bias], [tms[h]])
                            for h in HS:
                                pbs[h] = pbr.get()
                                kb.act(pbs[h][:, 0:nq], tms[h][:, 0:nq], AF.Exp, [tms[h]], [pbs[h]])
                            for h in HS:
                                ops_[h] = ps.get()
                                kb.mm(ops_[h][0:65, 0:nq], vs[:, h, 0:65], pbs[h][:, 0:nq], True, True, [vs, pbs[h]], [ops_[h]])
                            for h in HS:
                                kb.tt(acc[:, h, qsl], acc[:, h, qsl], ops_[h][0:65, 0:nq], ALU.add, [acc, ops_[h]], [acc])
                S.barrier()
                scB.close()
                kb.scope = sc
                yTr = [kb.alloc("ydil%d" % h, [64, SEQ], BF16) for h in range(2)]
                rcp = Ring(kb, "drcp", [64, 512], F32, 2)
                if dbg is not None and dbg[0] == "dil":
                    tmpf = kb.alloc("dbgf", [64, SEQ], F32)
                for h in range(4):
                    yT = yTr[h % 2]
                    for tb in range(NTB):
                        sl = slice(tb * 512, (tb + 1) * 512)
                        bc = ps.get()
                        kb.mm(bc[0:64, :], C("sel65", rows=65)[:, 0:64], acc[:, h, sl], True, True, [cst, acc], [bc])
                        rc = rcp.get()
                        kb.recip(rc[:], bc[0:64, :], [bc], [rc])
                        kb.tt(yT[:, sl], acc[0:64, h, sl], rc[:], ALU.mult, [acc, rc], [yT])
                    if dbg is not None and dbg[0] == "dil":
                        kb.copy(tmpf[:], yT[:], [yT], [tmpf])
                        S.dma(dbg_d[64 * h:64 * (h + 1), :], tmpf[:], reads=[tmpf])
                    apply_wout(l, [(yT, (lambda tb, yT=yT: yT[:, tb * 512:(tb + 1) * 512]), 2 * GW + 64 * h, 64)], sub=True)
                    kb.scope = sc
                S.barrier()
            kb.scope = es


        def s5(l):
            I32 = mybir.dt.int32
            TWO_PI = 2.0 * math.pi
            with ExitStack() as sc:
                kb.scope = sc
                uT32 = kb.alloc("uT32", [128, 2, SEQ], F32)
                uTb = kb.alloc("uTb", [128, 2, SEQ], BF16)
                ypre = kb.alloc("ypre", [128, 2, SEQ], BF16)
                wu = kb.alloc("wu", [128, 8, 256], BF16)
                S.dma(wu[:], w_in_d[l, :, C_U:C_U + 256].rearrange("(c p) n -> p c n", p=128), writes=[wu], q="pool")
                bblk = kb.alloc("bblk", [128, 2, 2, 512], BF16)
                S.dma(bblk[:], s5b_d[l], writes=[bblk], q="pool")
                cblk = kb.alloc("cblk", [128, 2, 8, 128], F32)
                S.dma(cblk[:], s5c_d[l], writes=[cblk])
                wglu = kb.alloc("wglu", [128, 2, 512], BF16)
                S.dma(wglu[:], s5g_d[l].rearrange("(c p) n -> p c n", p=128), writes=[wglu], q="pool")
                for oc in range(2):
                    for tb in range(NTB):
                        sl = slice(tb * 512, (tb + 1) * 512)
                        pp = ps.get()
                        for c in range(8):
                            kb.mm(pp[:], wu[:, c, oc * 128:(oc + 1) * 128], nT[:, c, sl], c == 0, c == 7, [wu, nT], [pp])
                        kb.copy(uT32[:, oc, sl], pp[:], [pp], [uT32], eng="act")
                        kb.copy(uTb[:, oc, sl], pp[:], [pp], [uTb])
                sm = {n_: kb.alloc("s5p_" + n_, [128, 8], F32) for n_ in
                      ("dt", "rho", "th", "amag", "c", "s", "ar", "ai", "zr", "zi", "t1", "t2", "t3", "nzi")}
                kint = kb.alloc("s5kint", [128, 512], I32)
                thoff = kb.alloc("thoff", [128, 8, 4], F32)
                lr = pv[:, PV.off["s5_lr"] + l * 8: PV.off["s5_lr"] + l * 8 + 8]
                li = pv[:, PV.off["s5_li"] + l * 8: PV.off["s5_li"] + l * 8 + 8]
                ldt = pv[:, PV.off["s5_ldt"] + l * 8: PV.off["s5_ldt"] + l * 8 + 8]

                def wrap(r_, tmp_, n):
                    kb.ts(tmp_, r_, math.pi, ALU.is_gt, [r_t], [tmp_t])
                    kb.stt(r_, tmp_, -TWO_PI, r_, ALU.mult, ALU.add, [tmp_t, r_t], [r_t])
                    kb.ts(tmp_, r_, -math.pi, ALU.is_lt, [r_t], [tmp_t])
                    kb.stt(r_, tmp_, TWO_PI, r_, ALU.mult, ALU.add, [tmp_t, r_t], [r_t])

                def sincos(ang_ap, ang_t, s_ap, s_t, c_ap, c_t, r_ap, r_T, tmp_ap, tmp_T, ki_ap):
                    nonlocal r_t, tmp_t
                    r_t, tmp_t = r_T, tmp_T
                    kb.ts(tmp_ap, ang_ap, 1.0 / TWO_PI, ALU.mult, [ang_t], [tmp_T])
                    kb.copy(ki_ap, tmp_ap, [tmp_T], [kint])
                    kb.copy(tmp_ap, ki_ap, [kint], [tmp_T])
                    kb.stt(r_ap, tmp_ap, -TWO_PI, ang_ap, ALU.mult, ALU.add, [tmp_T, ang_t], [r_T])
                    wrap(r_ap, tmp_ap, 0)
                    kb.act(s_ap, r_ap, AF.Sin, [r_T], [s_t])
                    kb.ts(r_ap, r_ap, math.pi / 2, ALU.add, [r_T], [r_T])
                    wrap(r_ap, tmp_ap, 0)
                    kb.act(c_ap, r_ap, AF.Sin, [r_T], [c_t])

                r_t = tmp_t = None
                kb.act(sm["dt"][:], ldt, AF.Exp, [pv], [sm["dt"]])
                kb.tt(sm["rho"][:], lr, sm["dt"][:], ALU.mult, [pv, sm["dt"]], [sm["rho"]])
                kb.tt(sm["th"][:], li, sm["dt"][:], ALU.mult, [pv, sm["dt"]], [sm["th"]])
                kb.act(sm["amag"][:], sm["rho"][:], AF.Exp, [sm["rho"]], [sm["amag"]])
                sincos(sm["th"][:], sm["th"], sm["s"][:], sm["s"], sm["c"][:], sm["c"], sm["t1"][:], sm["t1"], sm["t2"][:], sm["t2"], kint[:, 0:8])
                kb.tt(sm["ar"][:], sm["amag"][:], sm["c"][:], ALU.mult, [sm["amag"], sm["c"]], [sm["ar"]])
                kb.tt(sm["ai"][:], sm["amag"][:], sm["s"][:], ALU.mult, [sm["amag"], sm["s"]], [sm["ai"]])
                kb.tt(sm["t1"][:], lr, lr, ALU.mult, [pv], [sm["t1"]])
                kb.tt(sm["t2"][:], li, li, ALU.mult, [pv], [sm["t2"]])
                kb.tt(sm["t1"][:], sm["t1"][:], sm["t2"][:], ALU.add, [sm["t1"], sm["t2"]], [sm["t1"]])
                kb.recip(sm["t1"][:], sm["t1"][:], [sm["t1"]], [sm["t1"]])
                kb.ts(sm["t2"][:], sm["ar"][:], -1.0, ALU.add, [sm["ar"]], [sm["t2"]])
                kb.tt(sm["zr"][:], sm["t2"][:], lr, ALU.mult, [sm["t2"], pv], [sm["zr"]])
                kb.tt(sm["t3"][:], sm["ai"][:], li, ALU.mult, [sm["ai"], pv], [sm["t3"]])
                kb.tt(sm["zr"][:], sm["zr"][:], sm["t3"][:], ALU.add, [sm["zr"], sm["t3"]], [sm["zr"]])
                kb.tt(sm["zr"][:], sm["zr"][:], sm["t1"][:], ALU.mult, [sm["zr"], sm["t1"]], [sm["zr"]])
                kb.tt(sm["zi"][:], sm["ai"][:], lr, ALU.mult, [sm["ai"], pv], [sm["zi"]])
                kb.tt(sm["t3"][:], sm["t2"][:], li, ALU.mult, [sm["t2"], pv], [sm["t3"]])
                kb.tt(sm["zi"][:], sm["zi"][:], sm["t3"][:], ALU.subtract, [sm["zi"], sm["t3"]], [sm["zi"]])
                kb.tt(sm["zi"][:], sm["zi"][:], sm["t1"][:], ALU.mult, [sm["zi"], sm["t1"]], [sm["zi"]])
                kb.ts(sm["nzi"][:], sm["zi"][:], -1.0, ALU.mult, [sm["zi"]], [sm["nzi"]])
                for tb in range(NTB):
                    kb.ts(thoff[:, :, tb], sm["th"][:], float(512 * tb), ALU.mult, [sm["th"]], [thoff])
                cz = kb.alloc("cz", [128, 2, 8, 128], F32)
                ctmp = kb.alloc("ctmp", [128, 128], F32)
                for scn in range(8):
                    kb.ts(ctmp[:], cblk[:, 1, scn, :], sm["nzi"][:, scn:scn + 1], ALU.mult, [cblk, sm["nzi"]], [ctmp])
                    kb.stt(cz[:, 0, scn, :], cblk[:, 0, scn, :], sm["zr"][:, scn:scn + 1], ctmp[:], ALU.mult, ALU.add, [cblk, sm["zr"], ctmp], [cz])
                    kb.ts(ctmp[:], cblk[:, 1, scn, :], sm["zr"][:, scn:scn + 1], ALU.mult, [cblk, sm["zr"]], [ctmp])
                    kb.stt(ctmp[:], cblk[:, 0, scn, :], sm["zi"][:, scn:scn + 1], ctmp[:], ALU.mult, ALU.add, [cblk, sm["zi"], ctmp], [ctmp])
                    kb.ts(cz[:, 1, scn, :], ctmp[:], -1.0, ALU.mult, [ctmp], [cz])
                tl = {n_: kb.alloc("s5t_" + n_, [128, 512], F32) for n_ in ("ang", "r", "tmp", "s", "c", "a1", "a2", "xr", "xi", "hr", "hi")}
                amb = kb.alloc("amb", [128, 512], F32)
                hp = [kb.alloc("s5hp%d" % i, [128, 512], F32) for i in range(2)]
                hlast = kb.alloc("s5hl", [128, 2, 8], F32)
                kb.memset(hlast[:], 0.0, [hlast])
                for tb in range(NTB):
                    sl = slice(tb * 512, (tb + 1) * 512)
                    for scn in range(8):
                        hh = scn // 4
                        if scn % 4 == 0:
                            yps = psl.get()
                        kb.ts(amb[:], C("trow"), 0.0, ALU.mult, [cst], [amb], s2=sm["amag"][:, scn:scn + 1], op1=ALU.add)
                        pr = ps.get()
                        pi_ = ps.get()
                        c0 = (scn % 4) * 128
                        kb.mm(pr[:], bblk[:, hh, 0, c0:c0 + 128], uTb[:, hh, sl], True, True, [bblk, uTb], [pr])
                        kb.mm(pi_[:], bblk[:, hh, 1, c0:c0 + 128], uTb[:, hh, sl], True, True, [bblk, uTb], [pi_])
                        kb.ts(tl["ang"][:], C("trow"), sm["th"][:, scn:scn + 1], ALU.mult, [cst, sm["th"]], [tl["ang"]],
                              s2=thoff[:, scn, tb:tb + 1], op1=ALU.add)
                        sincos(tl["ang"][:], tl["ang"], tl["s"][:], tl["s"], tl["c"][:], tl["c"], tl["r"][:], tl["r"], tl["tmp"][:], tl["tmp"], kint[:])
                        kb.tt(tl["a1"][:], pr[:], tl["c"][:], ALU.mult, [pr, tl["c"]], [tl["a1"]])
                        kb.tt(tl["a2"][:], pi_[:], tl["s"][:], ALU.mult, [pi_, tl["s"]], [tl["a2"]])
                        kb.tt(tl["xr"][:], tl["a1"][:], tl["a2"][:], ALU.add, [tl["a1"], tl["a2"]], [tl["xr"]], eng="pool")
                        kb.tt(tl["a1"][:], pi_[:], tl["c"][:], ALU.mult, [pi_, tl["c"]], [tl["a1"]])
                        kb.tt(tl["a2"][:], pr[:], tl["s"][:], ALU.mult, [pr, tl["s"]], [tl["a2"]])
                        kb.tt(tl["xi"][:], tl["a1"][:], tl["a2"][:], ALU.subtract, [tl["a1"], tl["a2"]], [tl["xi"]], eng="pool")
                        S.op("dve", lambda e, scn=scn: e.tensor_tensor_scan(out=tl["hr"][:], data0=amb[:], data1=tl["xr"][:],
                                                                          initial=hlast[:, 0, scn:scn + 1], op0=ALU.mult, op1=ALU.add),
                             [amb, tl["xr"], hlast], [tl["hr"]])
                        S.op("dve", lambda e, scn=scn: e.tensor_tensor_scan(out=tl["hi"][:], data0=amb[:], data1=tl["xi"][:],
                                                                          initial=hlast[:, 1, scn:scn + 1], op0=ALU.mult, op1=ALU.add),
                             [amb, tl["xi"], hlast], [tl["hi"]])
                        kb.copy(hlast[:, 0, scn:scn + 1], tl["hr"][:, 511:512], [tl["hr"]], [hlast])
                        kb.copy(hlast[:, 1, scn:scn + 1], tl["hi"][:, 511:512], [tl["hi"]], [hlast])
                        kb.tt(tl["a1"][:], tl["hr"][:], tl["c"][:], ALU.mult, [tl["hr"], tl["c"]], [tl["a1"]])
                        kb.tt(tl["a2"][:], tl["hi"][:], tl["s"][:], ALU.mult, [tl["hi"], tl["s"]], [tl["a2"]])
                        kb.tt(hp[0][:], tl["a1"][:], tl["a2"][:], ALU.subtract, [tl["a1"], tl["a2"]], [hp[0]], eng="pool")
                        kb.tt(tl["a1"][:], tl["hr"][:], tl["s"][:], ALU.mult, [tl["hr"], tl["s"]], [tl["a1"]])
                        kb.tt(tl["a2"][:], tl["hi"][:], tl["c"][:], ALU.mult, [tl["hi"], tl["c"]], [tl["a2"]])
                        kb.tt(hp[1][:], tl["a1"][:], tl["a2"][:], ALU.add, [tl["a1"], tl["a2"]], [hp[1]], eng="pool")
                        j = scn % 4
                        kb.mm(yps[:], cz[:, 0, scn, :], hp[0][:], j == 0, False, [cz, hp[0]], [yps])
                        kb.mm(yps[:], cz[:, 1, scn, :], hp[1][:], False, j == 3, [cz, hp[1]], [yps])
                        if j == 3:
                            kb.stt(ypre[:, hh, sl], uT32[:, hh, sl], P("s5_d", l, hh), yps[:], ALU.mult, ALU.add, [uT32, pv, yps], [ypre])
                sg = Ring(kb, "s5sg", [128, 512], F32, 2)
                yT = uTb
                for tb in range(NTB):
                    sl = slice(tb * 512, (tb + 1) * 512)
                    for oc in range(2):
                        vps = ps.get()
                        gps = ps.get()
                        for kc in range(2):
                            kb.mm(vps[:], wglu[:, kc, oc * 128:(oc + 1) * 128], ypre[:, kc, sl], kc == 0, kc == 1, [wglu, ypre], [vps])
                        for kc in range(2):
                            kb.mm(gps[:], wglu[:, kc, 256 + oc * 128:256 + (oc + 1) * 128], ypre[:, kc, sl], kc == 0, kc == 1, [wglu, ypre], [gps])
                        g_ = sg.get()
                        kb.act(g_[:], gps[:], AF.Sigmoid, [gps], [g_])
                        kb.tt(yT[:, oc, sl], vps[:], g_[:], ALU.mult, [vps, g_], [yT])
                if dbg is not None and dbg[0] == "s5":
                    for oc in range(2):
                        kb.copy(uT32[:, oc, :], yT[:, oc, :], [yT], [uT32])
                        S.dma(dbg_d[oc * 128:(oc + 1) * 128, :], uT32[:, oc, :], reads=[uT32])
                S.barrier()
                apply_wout(l, [(yT, (lambda tb, oc=oc: yT[:, oc, tb * 512:(tb + 1) * 512]), GW + 128 * oc, 128) for oc in range(2)], sub=True)
                kb.scope = sc
                S.barrier()
            kb.scope = es


        def deltanet(l):
            with ExitStack() as sc:
                kb.scope = sc
                qkvT = [kb.alloc("nqkv%d" % i, [128, 2, SEQ], F32) for i in range(3)]
                with ExitStack() as scA:
                    kb.scope = scA
                    wdn = kb.alloc("wdn", [128, 8, 768], BF16)
                    S.dma(wdn[:], w_in_d[l, :, C_NQ:C_NQ + 768].rearrange("(c p) n -> p c n", p=128), writes=[wdn], q="pool")
                    xpad = kb.alloc("xpad", [128, SEQ + 3], F32)
                    accv = kb.alloc("accv", [128, SEQ], F32)
                    kb.memset(xpad[:, 0:3], 0.0, [xpad])
                    sqb = Ring(kb, "nsqb", [128, 512], BF16, 2)
                    sdr = Ring(kb, "nsdr", [128, 512], F32, 2)
                    for fc in range(6):
                        i, pc = fc // 2, fc % 2
                        for tb in range(NTB):
                            sl = slice(tb * 512, (tb + 1) * 512)
                            pp = ps.get()
                            for c in range(8):
                                kb.mm(pp[:], wdn[:, c, fc * 128:(fc + 1) * 128], nT[:, c, sl], c == 0, c == 7, [wdn, nT], [pp])
                            kb.copy(xpad[:, 3 + tb * 512:3 + (tb + 1) * 512], pp[:], [pp], [xpad], eng="act")
                        cw = lambda k_: pv[:, PV.off["dn_conv"] + l * 24 + k_ * 6 + fc: PV.off["dn_conv"] + l * 24 + k_ * 6 + fc + 1]
                        kb.ts(accv[:], xpad[:, 0:SEQ], cw(0), ALU.mult, [xpad, pv], [accv])
                        for k_ in range(1, 4):
                            kb.stt(accv[:], xpad[:, k_:k_ + SEQ], cw(k_), accv[:], ALU.mult, ALU.add, [xpad, pv, accv], [accv])
                        if i == 2:
                            kb.act(qkvT[2][:, pc, :], accv[:], AF.Silu, [accv], [qkvT[2]])
                            continue
                        kb.act(accv[:], accv[:], AF.Silu, [accv], [accv])
                        for tb in range(NTB):
                            sl = slice(tb * 512, (tb + 1) * 512)
                            q = sqb.get()
                            kb.act(q[:], accv[:, sl], AF.Square, [accv], [q])
                            st = ps.get()
                            kb.mm(st[:], C("bones", bf=True), q[:], True, True, [q, cstb], [st])
                            s1 = sdr.get()
                            kb.act(s1[:], st[:], AF.Sqrt, [st], [s1], scale=1.0, bias=EPS)
                            kb.recip(s1[:], s1[:], [s1], [s1])
                            if i == 0:
                                kb.stt(qkvT[0][:, pc, sl], accv[:, sl], 0.125, s1[:], ALU.mult, ALU.mult, [accv, s1], [qkvT[0]])
                            else:
                                kb.tt(qkvT[1][:, pc, sl], accv[:, sl], s1[:], ALU.mult, [accv, s1], [qkvT[1]])
                    S.barrier()
                kb.scope = sc
                qT_, kT_, vT_ = qkvT
                if stop == "D1":
                    kb.scope = es
                    return
                wab = kb.alloc("wab", [128, 8, 8], BF16)
                S.dma(wab[:], w_in_d[l, :, C_A:C_A + 8].rearrange("(c p) n -> p c n", p=128), writes=[wab], q="pool")
                wg = kb.alloc("wgate", [128, 8, 256], BF16)
                S.dma(wg[:], w_in_d[l, :, C_G:C_G + 256].rearrange("(c p) n -> p c n", p=128), writes=[wg], q="pool")
                drow = kb.alloc("drow", [128, 192], F32)
                S.dma(drow[:], dnrow_d[l], writes=[drow])
                sc_ = {n_: kb.alloc("dn_" + n_, [128, 64], F32) for n_ in ("a", "b", "beta", "g", "gc", "gl", "eg", "ekd", "dl", "beg", "t")}
                for tt_ in range(NTT):
                    pp = ps.get()
                    for c in range(8):
                        kb.mm(pp[:, 0:8], nT[:, c, tt_ * 128:(tt_ + 1) * 128], wab[:, c, :], c == 0, c == 7, [nT, wab], [pp])
                    kb.copy(sc_["a"][:, tt_ * 4:(tt_ + 1) * 4], pp[:, 0:4], [pp], [sc_["a"]], eng="act")
                    kb.copy(sc_["b"][:, tt_ * 4:(tt_ + 1) * 4], pp[:, 4:8], [pp], [sc_["b"]], eng="act")
                kb.act(sc_["beta"][:], sc_["b"][:], AF.Sigmoid, [sc_["b"]], [sc_["beta"]])
                kb.tt(sc_["t"][:], sc_["a"][:], drow[:, 64:128], ALU.add, [sc_["a"], drow], [sc_["t"]])
                kb.act(sc_["t"][:], sc_["t"][:], AF.Exp, [sc_["t"]], [sc_["t"]])
                kb.act(sc_["t"][:], sc_["t"][:], AF.Ln, [sc_["t"]], [sc_["t"]], scale=1.0, bias=1.0)
                kb.act(sc_["g"][:], drow[:, 0:64], AF.Exp, [drow], [sc_["g"]])
                kb.stt(sc_["g"][:], sc_["g"][:], -1.0, sc_["t"][:], ALU.mult, ALU.mult, [sc_["g"], sc_["t"]], [sc_["g"]])
                pp = ps.get()
                kb.mm(pp[:, 0:64], C("tri"), sc_["g"][:], True, True, [cst, sc_["g"]], [pp])
                kb.copy(sc_["gc"][:], pp[:, 0:64], [pp], [sc_["gc"]])
                pp = ps.get()
                kb.mm(pp[:, 0:64], C("ones"), sc_["g"][:], True, True, [cst, sc_["g"]], [pp])
                kb.copy(sc_["gl"][:], pp[:, 0:64], [pp], [sc_["gl"]])
                kb.act(sc_["eg"][:], sc_["gc"][:], AF.Exp, [sc_["gc"]], [sc_["eg"]])
                kb.tt(sc_["t"][:], sc_["gl"][:], sc_["gc"][:], ALU.subtract, [sc_["gl"], sc_["gc"]], [sc_["t"]])
                kb.act(sc_["ekd"][:], sc_["t"][:], AF.Exp, [sc_["t"]], [sc_["ekd"]])
                kb.act(sc_["dl"][:], sc_["gl"][:], AF.Exp, [sc_["gl"]], [sc_["dl"]])
                kb.tt(sc_["beg"][:], sc_["beta"][:], sc_["eg"][:], ALU.mult, [sc_["beta"], sc_["eg"]], [sc_["beg"]])
                if stop == "D2":
                    S.barrier(); kb.scope = es
                    return
                yTdn = kb.alloc("yTdn", [128, 2, SEQ], BF16)
                scC = ExitStack()
                kb.scope = scC
                Sst = [kb.alloc("Sst%d" % i, [128, 64], F32) for i in range(4)]
                for t_ in Sst:
                    kb.memset(t_[:], 0.0, [t_])
                rwp = [kb.alloc("rwp%d" % i, [128, 128], F32) for i in range(4)]
                kdp = [kb.alloc("kdp%d" % i, [128, 128], F32) for i in range(4)]
                for t_ in rwp + kdp:
                    kb.memset(t_[:], 0.0, [t_])
                sq128 = Ring(kb, "nm", [128, 128], F32, 28)
                ringE = Ring(kb, "nme", [128, 128], F32, 8)
                ringA = Ring(kb, "nma", [128, 128], F32, 4)
                t64 = Ring(kb, "n64", [128, 64], F32, 16)
                tokr = Ring(kb, "ntok", [128, 128], F32, 4)
                otok = Ring(kb, "otok", [128, 256], F32, 2)
                w256 = Ring(kb, "w256", [128, 256], F32, 3)
                r4 = Ring(kb, "r4", [128, 4], F32, 2)
                wTt = [kb.alloc("wTt%d" % i, [128, 128], F32) for i in range(4)]
                for c_ in range(NTT):
                    tsl = slice(c_ * 128, (c_ + 1) * 128)
                    ktok, vtok = [], []
                    for pc in range(2):
                        for (src, lst) in ((kT_, ktok), (vT_, vtok)):
                            tp = ps.get()
                            kb.tr(tp[:, 0:128], src[:, pc, tsl], C("ident"), [src, cst], [tp])
                            tk = tokr.get()
                            kb.copy(tk[:], tp[:, 0:128], [tp], [tk], eng="act")
                            lst.append(tk)
                    ot = otok.get()
                    HS = range(4)
                    idx = [c_ * 4 + h for h in HS]
                    pcs = [h // 2 for h in HS]
                    p0s = [64 * (h % 2) for h in HS]
                    col = lambda t_, h: t_[:, idx[h]:idx[h] + 1]
                    kTh = [kT_[p0s[h]:p0s[h] + 64, pcs[h], tsl] for h in HS]
                    qTh = [qT_[p0s[h]:p0s[h] + 64, pcs[h], tsl] for h in HS]
                    khs = [ktok[pcs[h]][:, p0s[h]:p0s[h] + 64] for h in HS]
                    vhs = [vtok[pcs[h]][:, p0s[h]:p0s[h] + 64] for h in HS]
                    gbc, B_ps, KK_ps, x1, Lm, dtt, QK_ps, aqk, LT_ps, PT, Tt = ([None] * 4 for _ in range(11))
                    for h in HS:
                        gbc[h] = ringE.get()
                        kb.ts(gbc[h][:], C("ones"), col(sc_["g"], h), ALU.mult, [cst, sc_["g"]], [gbc[h]])
                    for h in HS:
                        B_ps[h] = ps.get()
                        kb.mm(B_ps[h][:, 0:128], gbc[h][:], C("tri"), True, True, [gbc[h], cst], [B_ps[h]])
                        kb.mm(B_ps[h][:, 128:256], kTh[h], kTh[h], True, True, [kT_], [B_ps[h]])
                        kb.mm(B_ps[h][:, 256:384], kTh[h], qTh[h], True, True, [kT_, qT_], [B_ps[h]])
                    for h in HS:
                        x1[h] = ringE.get()
                        kb.stt(x1[h][:], B_ps[h][:, 0:128], col(sc_["gc"], h), C("negS"), ALU.subtract, ALU.subtract, [B_ps[h], sc_["gc"], cst], [x1[h]])
                        dtt[h] = ringE.get()
                        kb.stt(dtt[h][:], B_ps[h][:, 0:128], col(sc_["gc"], h), C("negT"), ALU.subtract, ALU.add, [B_ps[h], sc_["gc"], cst], [dtt[h]])
                    for h in HS:
                        kb.act(x1[h][:], x1[h][:], AF.Exp, [x1[h]], [x1[h]], scale=-1.0)
                        kb.act(dtt[h][:], dtt[h][:], AF.Exp, [dtt[h]], [dtt[h]])
                    for h in HS:
                        Lm[h] = sq128.get()
                        kb.stt(Lm[h][:], B_ps[h][:, 128:256], col(sc_["beta"], h), x1[h][:], ALU.mult, ALU.mult, [B_ps[h], sc_["beta"], x1[h]], [Lm[h]])
                        aqk[h] = ringA.get()
                        kb.tt(aqk[h][:], B_ps[h][:, 256:384], dtt[h][:], ALU.mult, [B_ps[h], dtt[h]], [aqk[h]])
                    for h in HS:
                        LT_ps[h] = ps.get()
                        kb.tr(LT_ps[h][:, 0:128], Lm[h][:], C("ident"), [Lm[h], cst], [LT_ps[h]])
                    for h in HS:
                        PT[h] = sq128.get()
                        kb.copy(PT[h][:], LT_ps[h][:, 0:128], [LT_ps[h]], [PT[h]], eng="act")
                        Tt[h] = sq128.get()
                        kb.tt(Tt[h][:], C("ident"), LT_ps[h][:, 0:128], ALU.subtract, [cst, LT_ps[h]], [Tt[h]])
                    Pm = list(Lm)
                    for it in range(6):
                        pq = [None] * 4
                        for h in HS:
                            pq[h] = ps.get()
                            kb.mm(pq[h][:, 0:128], PT[h][:], Pm[h][:], True, True, [PT[h], Pm[h]], [pq[h]])
                            if it < 5:
                                kb.mm(pq[h][:, 128:256], Pm[h][:], PT[h][:], True, True, [PT[h], Pm[h]], [pq[h]])
                        P2 = [None] * 4
                        PT2 = [None] * 4
                        for h in HS:
                            P2[h] = sq128.get()
                            kb.copy(P2[h][:], pq[h][:, 0:128], [pq[h]], [P2[h]], eng="act")
                            if it < 5:
                                PT2[h] = sq128.get()
                                kb.copy(PT2[h][:], pq[h][:, 128:256], [pq[h]], [PT2[h]], eng="act" if h % 2 else "dve")
                        tq = [None] * 4
                        for h in HS:
                            tq[h] = ps.get()
                            kb.mm(tq[h][:, 0:128], P2[h][:], Tt[h][:], True, True, [P2[h], Tt[h]], [tq[h]])
                        for h in HS:
                            Ttn = sq128.get()
                            kb.tt(Ttn[:], Tt[h][:], tq[h][:, 0:128], ALU.add, [Tt[h], tq[h]], [Ttn])
                            Tt[h] = Ttn
                            Pm[h] = P2[h]
                            if it < 5:
                                PT[h] = PT2[h]
                    if stop == "D3":
                        S.barrier(); scC.close(); kb.scope = es
                        return
                    ru, us, vnew, o1 = ([None] * 4 for _ in range(4))
                    for h in HS:
                        p0 = p0s[h]
                        kb.ts(rwp[h][:, p0:p0 + 64], khs[h], col(sc_["beg"], h), ALU.mult, [ktok[pcs[h]], sc_["beg"]], [rwp[h]])
                        ru[h] = t64.get()
                        kb.ts(ru[h][:], vhs[h], col(sc_["beta"], h), ALU.mult, [vtok[pcs[h]], sc_["beta"]], [ru[h]])
                        kb.ts(kdp[h][:, p0:p0 + 64], khs[h], col(sc_["ekd"], h), ALU.mult, [ktok[pcs[h]], sc_["ekd"]], [kdp[h]])
                    wu_ps = [None] * 4
                    for h in HS:
                        wu_ps[h] = ps.get()
                        kb.mm(wu_ps[h][:, 0:128], rwp[h][:], Tt[h][:], True, True, [rwp[h], Tt[h]], [wu_ps[h]])
                        kb.mm(wu_ps[h][:, 128:192], Tt[h][:], ru[h][:], True, True, [Tt[h], ru[h]], [wu_ps[h]])
                    for h in HS:
                        p0 = p0s[h]
                        kb.copy(wTt[h][p0:p0 + 64, :], wu_ps[h][p0:p0 + 64, 0:128], [wu_ps[h]], [wTt[h]], eng="act")
                        us[h] = t64.get()
                        kb.copy(us[h][:], wu_ps[h][:, 128:192], [wu_ps[h]], [us[h]])
                    sq_ps = [None] * 4
                    for h in HS:
                        p0 = p0s[h]
                        sq_ps[h] = ps.get()
                        kb.mm(sq_ps[h][:, 0:64], wTt[h][p0:p0 + 64, :], Sst[h][p0:p0 + 64, :], True, True, [wTt[h], Sst[h]], [sq_ps[h]])
                        kb.mm(sq_ps[h][:, 64:128], qTh[h], Sst[h][p0:p0 + 64, :], True, True, [qT_, Sst[h]], [sq_ps[h]])
                    for h in HS:
                        vnew[h] = t64.get()
                        kb.tt(vnew[h][:], us[h][:], sq_ps[h][:, 0:64], ALU.subtract, [us[h], sq_ps[h]], [vnew[h]])
                        o1[h] = t64.get()
                        kb.ts(o1[h][:], sq_ps[h][:, 64:128], col(sc_["eg"], h), ALU.mult, [sq_ps[h], sc_["eg"]], [o1[h]])
                    ak_ps = [None] * 4
                    for h in HS:
                        ak_ps[h] = ps.get()
                        kb.mm(ak_ps[h][:, 0:64], aqk[h][:], vnew[h][:], True, True, [aqk[h], vnew[h]], [ak_ps[h]])
                        kb.mm(ak_ps[h][:, 64:128], kdp[h][:], vnew[h][:], True, True, [kdp[h], vnew[h]], [ak_ps[h]])
                    for h in HS:
                        p0 = p0s[h]
                        kb.tt(ot[:, h * 64:(h + 1) * 64], o1[h][:], ak_ps[h][:, 0:64], ALU.add, [o1[h], ak_ps[h]], [ot])
                        kb.stt(Sst[h][p0:p0 + 64, :], Sst[h][p0:p0 + 64, :], sc_["dl"][p0:p0 + 64, idx[h]:idx[h] + 1], ak_ps[h][p0:p0 + 64, 64:128],
                               ALU.mult, ALU.add, [Sst[h], sc_["dl"], ak_ps[h]], [Sst[h]])
                    if stop == "D4":
                        S.barrier(); scC.close(); kb.scope = es
                        return
                    g_ps = ps.get()
                    for c in range(8):
                        kb.mm(g_ps[:, 0:256], nT[:, c, tsl], wg[:, c, :], c == 0, c == 7, [nT, wg], [g_ps])
                    sgt = w256.get()
                    kb.act(sgt[:], g_ps[:, 0:256], AF.Silu, [g_ps], [sgt])
                    sq_ = w256.get()
                    kb.act(sq_[:], ot[:], AF.Square, [ot], [sq_])
                    ss = r4.get()
                    S.op("dve", lambda e, ss=ss, sq_=sq_: e.tensor_reduce(out=ss[:], in_=sq_[:].rearrange("p (h d) -> p h d", h=4),
                                                                        axis=mybir.AxisListType.X, op=ALU.add), [sq_], [ss])
                    kb.act(ss[:], ss[:], AF.Sqrt, [ss], [ss], scale=1.0 / 64, bias=EPS)
                    kb.recip(ss[:], ss[:], [ss], [ss])
                    yt = w256.get()
                    for h in range(4):
                        kb.stt(yt[:, h * 64:(h + 1) * 64], ot[:, h * 64:(h + 1) * 64], ss[:, h:h + 1], drow[:, 128:192], ALU.mult, ALU.mult,
                               [ot, ss, drow], [yt])
                    kb.tt(yt[:], yt[:], sgt[:], ALU.mult, [yt, sgt], [yt])
                    for pc in range(2):
                        tp = ps.get()
                        kb.tr(tp[:, 0:128], yt[:, pc * 128:(pc + 1) * 128], C("ident"), [yt, cst], [tp])
                        kb.copy(yTdn[:, pc, tsl], tp[:, 0:128], [tp], [yTdn], eng="act")
                    if stop == "D5" or (stop is not None and stop[0] == "T" and int(stop[1:]) == c_):
                        S.barrier(); scC.close(); kb.scope = es
                        return
                S.barrier()
                scC.close()
                kb.scope = sc
                if dbg is not None and dbg[0] == "dn":
                    for pc in range(2):
                        kb.copy(qkvT[0][:, pc, :], yTdn[:, pc, :], [yTdn], [qkvT[0]])
                        S.dma(dbg_d[pc * 128:(pc + 1) * 128, :], qkvT[0][:, pc, :], reads=[qkvT[0]])
                S.barrier()
                apply_wout(l, [(yTdn, (lambda tb, pc=pc: yTdn[:, pc, tb * 512:(tb + 1) * 512]), 3 * GW + 128 * pc, 128) for pc in range(2)], sub=True)
                kb.scope = sc
                S.barrier()
            kb.scope = es

        def ffn_block(l):
            HC = FFN // 128
            with ExitStack() as sc:
                kb.scope = sc
                w13 = Ring(kb, "w13", [128, 8, 256], BF16, 3)
                w2t = Ring(kb, "w2t", [128, D], BF16, 3)
                gT = kb.alloc("gT", [128, HC, 1024], BF16)
                sg = Ring(kb, "sg", [128, 512], F32, 3)
                for half in range(2):
                    t0 = half * 1024
                    for hc in range(HC):
                        w = w13.get()
                        S.dma(w[:, :, 0:128], w1_d[l, :, hc * 128:(hc + 1) * 128].rearrange("(c p) n -> p c n", p=128), writes=[w], q="pool")
                        S.dma(w[:, :, 128:256], w3_d[l, :, hc * 128:(hc + 1) * 128].rearrange("(c p) n -> p c n", p=128), writes=[w], q="pool")
                        for tb in range(2):
                            sl = slice(t0 + tb * 512, t0 + (tb + 1) * 512)
                            a = ps.get()
                            b = ps.get()
                            for c in range(8):
                                kb.mm(a[:], w[:, c, 0:128], nT[:, c, sl], c == 0, c == 7, [w, nT], [a])
                            for c in range(8):
                                kb.mm(b[:], w[:, c, 128:256], nT[:, c, sl], c == 0, c == 7, [w, nT], [b])
                            s = sg.get()
                            kb.act(s[:], a[:], AF.Silu, [a], [s])
                            kb.tt(gT[:, hc, tb * 512:(tb + 1) * 512], s[:], b[:], ALU.mult, [s, b], [gT])
                    for oc in range(8):
                        accs = [ps.get(), ps.get()]
                        for hc in range(HC):
                            w = w2t.get()
                            S.dma(w[:, 0:128], w2_d[l, hc * 128:(hc + 1) * 128, oc * 128:(oc + 1) * 128], writes=[w], q="pool")
                            for tb in range(2):
                                kb.mm(accs[tb][:], w[:, 0:128], gT[:, hc, tb * 512:(tb + 1) * 512], hc == 0, hc == HC - 1, [w, gT], [accs[tb]])
                        for tb in range(2):
                            sl = slice(t0 + tb * 512, t0 + (tb + 1) * 512)
                            kb.tt(hT[:, oc, sl], hT[:, oc, sl], accs[tb][:], ALU.add, [hT, accs[tb]], [hT])
                S.barrier()
            kb.scope = es

        for l in range(nlayers):
            stream_norm("attn_norm", l)
            if stop == "N":
                break
            if "mla" in mixers:
                mla(l)
                kb.scope = es
            if "s5" in mixers:
                s5(l)
            if "dil" in mixers:
                dilated(l)
            if "dn" in mixers:
                deltanet(l)
                kb.scope = es
            if ffn:
                stream_norm("ffn_norm", l)
                ffn_block(l)

        for c in range(8):
            S.dma(yT_d[:, c, :], hT[:, c, :], reads=[hT], q=("sp" if c % 2 == 0 else "pool"))
        S.finish("sp")
        S.emit()
    return nc


class _Lay:
    def __init__(self):
        self.off = {}
        self.per = {}
        self.n = 0

    def add(self, name, n, per=None):
        self.off[name] = (self.n, n) if per is None else self.n
        if per is not None:
            self.per[name] = per
        self.n += n


PV = _Lay()
for _nm, _per in (("attn_norm", 8), ("ffn_norm", 8), ("mla_qn", 2), ("mla_kvn", 1), ("mla_qkq", 1), ("mla_qkk", 1), ("dil_qn", 1), ("dil_kn", 1), ("s5_lr", 8), ("s5_li", 8), ("s5_ldt", 8), ("s5_d", 2), ("dn_conv", 24)):
    PV.add(_nm, NL * _per, per=_per)

CS = _Lay()
for _nm, _n in (("ones", 128), ("ident", 128), ("tri01", 128), ("rotm", 96), ("sel65", 64), ("bones", 128), ("trow", 512), ("tri", 128), ("negT", 128), ("negS", 128)):
    CS.add(_nm, _n)


def _consts():
    c = np.zeros((128, CS.n), np.float32)

    def put(name, m):
        o, n = CS.off[name]
        c[:m.shape[0], o:o + m.shape[1]] = m
    put("ones", np.ones((128, 128), np.float32))
    put("ident", np.eye(128, dtype=np.float32))
    k = np.arange(128)[:, None]
    q = np.arange(128)[None, :]
    put("tri01", (q >= k).astype(np.float32))
    rot = np.zeros((96, 96), np.float32)
    for m in range(64, 80):
        rot[m + 16, m] = -1.0
    for m in range(80, 96):
        rot[m - 16, m] = 1.0
    put("rotm", rot)
    sel = np.zeros((65, 64), np.float32)
    sel[64, :] = 1.0
    put("sel65", sel)
    bo = np.zeros((128, 128), np.float32)
    bo[0:64, 0:64] = 1.0
    bo[64:128, 64:128] = 1.0
    put("bones", bo)
    pi_ = np.arange(128)[:, None]
    fi_ = np.arange(128)[None, :]
    put("tri", (pi_ <= fi_).astype(np.float32))
    put("negT", np.where(fi_ >= pi_, 0.0, NEG).astype(np.float32))
    put("negS", np.where(pi_ > fi_, 0.0, NEG).astype(np.float32))
    put("trow", np.tile(np.arange(512, dtype=np.float32)[None, :], (128, 1)))
    return c


def _rope_tables():
    half = 16
    freqs = (10000.0 ** (-np.arange(half, dtype=np.float32) / half)).astype(np.float32)
    pos = np.arange(SEQ, dtype=np.float32)
    ang = pos[None, :] * freqs[:, None]
    tab = np.zeros((96, 2, SEQ), np.float32)
    tab[0:64, 0, :] = 1.0
    tab[64:80, 0, :] = np.cos(ang)
    tab[80:96, 0, :] = np.cos(ang)
    tab[64:80, 1, :] = np.sin(ang)
    tab[80:96, 1, :] = np.sin(ang)
    return tab


def _pvec(inp):
    pvv = np.zeros((128, PV.n), np.float32)

    def put(name, l, arr):
        o = PV.off[name] + l * PV.per[name]
        a = np.asarray(arr, np.float32)
        if a.shape[0] <= 128:
            pvv[:a.shape[0], o] = a
        else:
            k = a.shape[0] // 128
            pvv[:, o:o + k] = a.reshape(k, 128).T
    for l in range(NL):
        put("attn_norm", l, inp["attn_norm"][l])
        put("ffn_norm", l, inp["ffn_norm"][l])
        put("mla_qn", l, inp["mla_q_norm"][l])
        put("mla_kvn", l, inp["mla_kv_norm"][l])
        put("mla_qkq", l, inp["mla_qk_q"][l])
        put("mla_qkk", l, inp["mla_qk_k"][l])
        put("dil_qn", l, np.tile(np.asarray(inp["dil_q_norm"][l], np.float32), 2))
        put("dil_kn", l, np.tile(np.asarray(inp["dil_k_norm"][l], np.float32), 2))
        put("s5_lr", l, np.asarray(inp["s5_lambda_re"][l], np.float32).reshape(-1))
        put("s5_li", l, np.asarray(inp["s5_lambda_im"][l], np.float32).reshape(-1))
        put("s5_ldt", l, np.repeat(np.asarray(inp["s5_log_dt"][l], np.float32), 64))
        put("s5_d", l, inp["s5_d"][l])
        cw = np.asarray(inp["dn_conv"][l], np.float32)
        o_ = PV.off["dn_conv"] + l * 24
        for k_ in range(4):
            pvv[:, o_ + k_ * 6:o_ + k_ * 6 + 6] = cw[k_].reshape(6, 128).T
    return pvv


def _t5_bucket_np(dist):
    exact = 16
    df = np.maximum(dist, 1).astype(np.float32)
    large = exact + (np.log(df / np.float32(exact)) / np.float32(math.log(2048 / exact)) * np.float32(32 - exact)).astype(np.int32)
    large = np.minimum(large, 31)
    return np.where(dist < exact, dist, large)


def _dil_bias(t5):
    t5 = np.asarray(t5, np.float32)
    k = np.arange(128)[:, None]
    q = np.arange(256)[None, :]
    delta = q - k
    valid = (delta >= 0) & (delta <= 128)
    out = np.zeros((128, 12, 256), np.float32)
    for bi, dil in enumerate((1, 4, 16)):
        bucket = _t5_bucket_np(np.clip(delta, 0, 128) * dil)
        for h in range(4):
            out[:, bi * 4 + h, :] = np.where(valid, t5[bucket, h], np.float32(NEG))
    return out


def _s5_blocks(inp):
    bre, bim = np.asarray(inp["s5_b_re"], np.float32), np.asarray(inp["s5_b_im"], np.float32)
    cre, cim = np.asarray(inp["s5_c_re"], np.float32), np.asarray(inp["s5_c_im"], np.float32)
    b = np.zeros((NL, 128, 2, 2, 512), np.float32)
    c = np.zeros((NL, 128, 2, 8, 128), np.float32)
    for g in range(16):
        hh, gl = g // 8, g % 8
        for ri, (bb, cc) in enumerate(((bre, cre), (bim, cim))):
            b[:, gl * 16:(gl + 1) * 16, hh, ri, gl * 64:(gl + 1) * 64] = bb[:, g].transpose(0, 2, 1)
            c[:, (g % 2) * 64:(g % 2) * 64 + 64, ri, g // 2, gl * 16:(gl + 1) * 16] = cc[:, g].transpose(0, 2, 1)
    return b, c


def _dn_rows(inp):
    out = np.zeros((NL, 128, 192), np.float32)
    for l in range(NL):
        out[l, :, 0:64] = np.tile(np.asarray(inp["dn_a_log"][l], np.float32), 16)[None, :]
        out[l, :, 64:128] = np.tile(np.asarray(inp["dn_dt_bias"][l], np.float32), 16)[None, :]
        out[l, :, 128:192] = np.asarray(inp["dn_o_norm"][l], np.float32)[None, :]
    return out


def prep_shared(inp):
    f = lambda a: np.ascontiguousarray(np.asarray(a, np.float32))
    w_ukv = f(inp["mla_w_ukv"])
    w_ukp = np.zeros((NL, 128, 4 * 96), np.float32)
    w_uv = np.zeros((NL, 128, 256), np.float32)
    for h in range(4):
        w_ukp[:, :, h * 96:h * 96 + 64] = w_ukv[:, :, h * 128:h * 128 + 64]
        w_uv[:, :, h * 64:(h + 1) * 64] = w_ukv[:, :, h * 128 + 64:h * 128 + 128]
    w_in = f(inp["w_in"])
    w_krp = np.zeros((NL, D, 96), np.float32)
    w_krp[:, :, 64:96] = w_in[:, :, C_KR:C_KR + 32]
    return {
        "w_in": w_in, "w_out": f(inp["w_out"]), "pvec": _pvec(inp), "cst128": _consts(),
        "w_uq": f(inp["mla_w_uq"]), "w_ukp": w_ukp, "w_uv": w_uv, "w_krp": w_krp, "rope": _rope_tables(), "dil_bias": _dil_bias(inp["t5_bias"]), "dn_row": _dn_rows(inp),
        "s5_b": _s5_blocks(inp)[0], "s5_c": _s5_blocks(inp)[1], "s5_w_glu": f(inp["s5_w_glu"]),
        "ffn_w1": f(inp["ffn_w1"]), "ffn_w3": f(inp["ffn_w3"]), "ffn_w2": f(inp["ffn_w2"]),
    }


def x_to_core(xb):
    return np.ascontiguousarray(xb.T.reshape(8, 128, SEQ).transpose(1, 0, 2))


def core_to_y(yT):
    return np.ascontiguousarray(yT.transpose(1, 0, 2).reshape(D, SEQ).T)


_NC_CACHE = {}


def kernel(**inputs):
    x = np.asarray(inputs["x"], np.float32)
    shared = prep_shared(inputs)
    if "nc" not in _NC_CACHE:
        _NC_CACHE["nc"] = build()
    nc = _NC_CACHE["nc"]
    in_maps = []
    for b in range(8):
        m = dict(shared)
        m["xT"] = x_to_core(x[b])
        in_maps.append(m)
    res = run_bass_kernel_spmd(nc, in_maps, core_ids=list(range(8)))
    out = np.stack([core_to_y(np.asarray(r["yT"])) for r in res.results], axis=0)
    return out.astype(np.float32)
```

```python
import math
import numpy as np
from contextlib import ExitStack
import concourse.bass as bass
import concourse.mybir as mybir
from concourse.bass_utils import run_bass_kernel_spmd

F32 = mybir.dt.float32
BF16 = mybir.dt.bfloat16
ALU = mybir.AluOpType
AF = mybir.ActivationFunctionType

D = 1024
SEQ = 2048
NL = 2
GW = 256
IN_COLS = 2472
FFN = 2816
EPS = 1e-6
NEG = -30000.0
NTB = 4
NTT = 16
C_CQ, C_CKV, C_KR, C_U, C_DQ, C_DK, C_DV, C_NQ, C_NK, C_NV, C_A, C_B, C_G = (
    0, 256, 384, 416, 672, 928, 1184, 1440, 1696, 1952, 2208, 2212, 2216)


class Dep:
    __slots__ = ("w", "r")

    def __init__(self):
        self.w = None
        self.r = {}


class T:
    __slots__ = ("t", "d", "ex")

    def __init__(self, t, ex=False):
        self.t = t
        self.d = Dep()
        self.ex = ex

    def __getitem__(self, k):
        return self.t[k]


class Sched:
    ENG = ("pe", "act", "dve", "pool", "sp")

    def __init__(self, nc, es, n_dma_sems=8):
        self.nc = nc
        self.sem = {k: es.enter_context(nc.semaphore("s_" + k)) for k in ("pe", "act", "dve", "pool")}
        self.cnt = {k: 0 for k in ("pe", "act", "dve", "pool")}
        self.ndma = n_dma_sems
        for pre in ("q", "g"):
            for i in range(n_dma_sems):
                k = "%s%d" % (pre, i)
                self.sem[k] = es.enter_context(nc.semaphore("s_" + k))
                self.cnt[k] = 0
        self.dma_rr = {"sp": 0, "pool": 0}
        self.seen = {}
        self.ninst = 0
        self.prog = {k: [] for k in self.ENG}

    @staticmethod
    def _inc(k):
        return 16 if k[0] in "qg" else 1

    def _wait(self, e, k, c):
        s = self.seen.setdefault(e, {})
        if s.get(k, 0) >= c:
            return
        self.prog[e].append(("w", self.sem[k], c * self._inc(k)))
        s[k] = c

    @staticmethod
    def _deps(reads, writes):
        d = {}
        for t in reads:
            t = t.d if isinstance(t, T) else t
            if t.w is not None:
                k, c = t.w
                d[k] = max(d.get(k, 0), c)
        for t in writes:
            t = t.d if isinstance(t, T) else t
            if t.w is not None:
                k, c = t.w
                d[k] = max(d.get(k, 0), c)
            for k, c in t.r.items():
                d[k] = max(d.get(k, 0), c)
        return d

    def _mark(self, k, c, reads, writes):
        for t in reads:
            t = t.d if isinstance(t, T) else t
            t.r[k] = c
        for t in writes:
            t = t.d if isinstance(t, T) else t
            t.w = (k, c)
            t.r = {}

    def op(self, e, fn, reads=(), writes=()):
        exr = [t for t in reads if isinstance(t, T) and t.ex]
        if exr:
            writes = list(writes) + exr
        d = self._deps(reads, writes)
        for k, c in d.items():
            if k == e and e == "pe":
                continue
            self._wait(e, k, c)
        self.cnt[e] += 1
        self.prog[e].append(("i", fn, self.sem[e], 1))
        self._mark(e, self.cnt[e], reads, writes)
        self.ninst += 1

    def dma(self, out, in_, reads=(), writes=(), q="sp"):
        i = self.dma_rr[q]
        self.dma_rr[q] = (i + 1) % self.ndma
        k = ("q%d" if q == "sp" else "g%d") % i
        d = self._deps(reads, writes)
        if self.cnt[k] > 0:
            d[k] = max(d.get(k, 0), self.cnt[k])
        for kk, c in d.items():
            self._wait(q, kk, c)
        self.cnt[k] += 1
        self.prog[q].append(("i", (lambda e, out=out, in_=in_: e.dma_start(out=out, in_=in_)), self.sem[k], 16))
        self._mark(k, self.cnt[k], reads, writes)
        self.ninst += 1

    def barrier(self):
        for e in self.ENG:
            for k, c in self.cnt.items():
                if c > 0 and not (k == e == "pe"):
                    self._wait(e, k, c)

    def finish(self, e="sp"):
        for k, c in self.cnt.items():
            if c > 0:
                self._wait(e, k, c)

    def emit(self):
        nc = self.nc
        with nc.Block() as block:
            def mk(k):
                def body(eng):
                    for it in self.prog[k]:
                        if it[0] == "w":
                            eng.wait_ge(it[1], it[2])
                        else:
                            it[1](eng).then_inc(it[2], it[3])
                return body
            for k, reg in (("sp", block.sync), ("pe", block.tensor), ("act", block.scalar),
                           ("dve", block.vector), ("pool", block.gpsimd)):
                if self.prog[k]:
                    reg(mk(k))


class Ring:
    def __init__(self, kb, name, shape, dt, n, psum=False):
        self.tiles = [kb.alloc("%s%d" % (name, i), shape, dt, psum=psum) for i in range(n)]
        self.i = 0

    def get(self):
        t = self.tiles[self.i]
        self.i = (self.i + 1) % len(self.tiles)
        return t


class KB:
    def __init__(self, nc, es):
        self.nc = nc
        self.es = es
        self.S = Sched(nc, es)
        self.scope = es
        self.uid = 0

    def alloc(self, name, shape, dt, psum=False):
        self.uid += 1
        nm = "%s_%d" % (name, self.uid)
        if psum:
            t = self.scope.enter_context(self.nc.psum_tensor(nm, shape, dt))
        else:
            t = self.scope.enter_context(self.nc.sbuf_tensor(nm, shape, dt))
        return T(t, ex=psum)

    def mm(self, out, lhsT, rhs, start, stop, reads, writes):
        self.S.op("pe", lambda e: e.matmul(out, lhsT=lhsT, rhs=rhs, start=start, stop=stop), reads, writes)

    def tr(self, out, in_, ident, reads, writes):
        self.S.op("pe", lambda e: e.transpose(out, in_=in_, identity=ident), reads, writes)

    def act(self, out, in_, func, reads, writes, scale=None, bias=None, accum_out=None):
        kw = {}
        if scale is not None:
            kw["scale"] = scale
        if bias is not None:
            kw["bias"] = bias
        if accum_out is not None:
            kw["accum_out"] = accum_out
        self.S.op("act", lambda e: e.activation(out=out, in_=in_, func=func, **kw), reads, writes)

    def tt(self, out, in0, in1, op, reads, writes, eng="dve"):
        self.S.op(eng, lambda e: e.tensor_tensor(out=out, in0=in0, in1=in1, op=op), reads, writes)

    def ts(self, out, in0, s1, op0, reads, writes, s2=None, op1=None, eng="dve"):
        if op1 is None:
            self.S.op(eng, lambda e: e.tensor_scalar(out=out, in0=in0, scalar1=s1, scalar2=None, op0=op0), reads, writes)
        else:
            self.S.op(eng, lambda e: e.tensor_scalar(out=out, in0=in0, scalar1=s1, scalar2=s2, op0=op0, op1=op1), reads, writes)

    def stt(self, out, in0, scalar, in1, op0, op1, reads, writes):
        self.S.op("dve", lambda e: e.scalar_tensor_tensor(out=out, in0=in0, scalar=scalar, in1=in1, op0=op0, op1=op1), reads, writes)

    def copy(self, out, in_, reads, writes, eng="dve"):
        if eng == "act":
            self.S.op("act", lambda e: e.copy(out=out, in_=in_), reads, writes)
        else:
            self.S.op(eng, lambda e: e.tensor_copy(out=out, in_=in_), reads, writes)

    def recip(self, out, in_, reads, writes):
        self.S.op("dve", lambda e: e.reciprocal(out=out, in_=in_), reads, writes)

    def memset(self, ap, val, writes, eng="pool"):
        self.S.op(eng, lambda e: e.memset(ap, val), (), writes)


class _Stop(Exception):
    pass


def build(nlayers=NL, mixers=("mla", "s5", "dil", "dn"), ffn=True, dbg=None, stop=None):
    nc = bass.Bass("TRN2", target_bir_lowering=False)

    def din(name, shape):
        return nc.dram_tensor(name, list(shape), F32, kind="ExternalInput").ap()

    xT_d = din("xT", [128, 8, SEQ])
    w_in_d = din("w_in", [NL, D, IN_COLS])
    w_out_d = din("w_out", [NL, D, D])
    pv_d = din("pvec", [128, PV.n])
    cst_d = din("cst128", [128, CS.n])
    w_uq_d = din("w_uq", [NL, 256, 384])
    w_ukp_d = din("w_ukp", [NL, 128, 4 * 96])
    w_uv_d = din("w_uv", [NL, 128, 256])
    w_krp_d = din("w_krp", [NL, D, 96])
    rope_d = din("rope", [96, 2, SEQ])
    s5b_d = din("s5_b", [NL, 128, 2, 2, 512])
    s5c_d = din("s5_c", [NL, 128, 2, 8, 128])
    s5g_d = din("s5_w_glu", [NL, 256, 512])
    dnrow_d = din("dn_row", [NL, 128, 192])
    dilb_d = din("dil_bias", [128, 12, 256])
    w1_d = din("ffn_w1", [NL, FFN // 128, 128, 8, 128])
    w3_d = din("ffn_w3", [NL, FFN // 128, 128, 8, 128])
    w2_d = din("ffn_w2", [NL, 8, 128, FFN // 128, 128])
    yT_d = nc.dram_tensor("yT", [128, 8, SEQ], F32, kind="ExternalOutput").ap()
    dbg_d = None
    if dbg is not None:
        dbg_d = nc.dram_tensor("dbg", list(dbg[1]), F32, kind="ExternalOutput").ap()

    with ExitStack() as es:
        kb = KB(nc, es)
        S = kb.S
        hT = kb.alloc("hT", [128, 8, SEQ], F32)
        nT = kb.alloc("nT", [128, 8, SEQ], BF16)
        pv = kb.alloc("pv", [128, PV.n], F32)
        cst = kb.alloc("cst", [128, CS.n], F32)
        cstb = kb.alloc("cstb", [128, CS.n], BF16)
        ps = Ring(kb, "ps", [128, 512], F32, 6, psum=True)
        psl = Ring(kb, "psl", [128, 512], F32, 2, psum=True)

        S.dma(pv[:], pv_d, writes=[pv])
        S.dma(cst[:], cst_d, writes=[cst])
        S.dma(cstb[:], cst_d, writes=[cstb], q="pool")
        for c in range(8):
            S.dma(hT[:, c, :], xT_d[:, c, :], writes=[hT], q=("sp" if c % 2 == 0 else "pool"))

        def C(name, rows=128, bf=False):
            o, n = CS.off[name]
            return (cstb if bf else cst)[0:rows, o:o + n]

        def P(name, l=0, j=0, rows=128):
            o = PV.off[name] + l * PV.per[name] + j
            return pv[0:rows, o:o + 1]

        def stream_norm(gname, l):
            with ExitStack() as sc:
                kb.scope = sc
                sq = Ring(kb, "sq", [128, 512], BF16, 3)
                sd = Ring(kb, "sd", [128, 512], F32, 2)
                for tb in range(NTB):
                    sl = slice(tb * 512, (tb + 1) * 512)
                    st = ps.get()
                    for c in range(8):
                        q = sq.get()
                        kb.act(q[:], hT[:, c, sl], AF.Square, [hT], [q])
                        kb.mm(st[:], C("ones", bf=True), q[:], c == 0, c == 7, [q, cstb], [st])
                    s1 = sd.get()
                    kb.act(s1[:], st[:], AF.Sqrt, [st], [s1], scale=1.0 / D, bias=EPS)
                    kb.recip(s1[:], s1[:], [s1], [s1])
                    for c in range(8):
                        kb.stt(nT[:, c, sl], hT[:, c, sl], P(gname, l, c), s1[:], ALU.mult, ALU.mult,
                               [hT, pv, s1], [nT])
                S.barrier()
            kb.scope = es

        def apply_wout(l, pieces, sub=False):
            with ExitStack() as sc:
                kb.scope = sc
                wts = []
                for (_, _, row0, K) in pieces:
                    w = kb.alloc("wo", [K, D], BF16)
                    S.dma(w[:], w_out_d[l, row0:row0 + K, :], writes=[w], q="pool")
                    wts.append(w)
                for oc in range(8):
                    for tb in range(NTB):
                        sl = slice(tb * 512, (tb + 1) * 512)
                        acc = ps.get()
                        n = len(pieces)
                        for i, (pt, apf, row0, K) in enumerate(pieces):
                            kb.mm(acc[:], wts[i][:, oc * 128:(oc + 1) * 128], apf(tb), i == 0, i == n - 1,
                                  [wts[i], pt], [acc])
                        kb.tt(hT[:, oc, sl], hT[:, oc, sl], acc[:], ALU.add, [hT, acc], [hT])
                S.barrier()
            kb.scope = es

        def mla(l):
            with ExitStack() as sc:
                kb.scope = sc
                wuq = kb.alloc("wuq", [128, 2, 384], BF16)
                wuk = kb.alloc("wuk", [128, 4 * 96], BF16)
                cqn = kb.alloc("cqn", [128, 2, SEQ], BF16)
                ckvn = kb.alloc("ckvn", [128, SEQ], BF16)
                krp = kb.alloc("krp", [96, SEQ], F32)
                vaug = kb.alloc("vaug", [128, NTT, 4, 72], BF16)
                sqb = Ring(kb, "sqb", [128, 512], BF16, 3)
                sdr = Ring(kb, "sdr", [128, 512], F32, 2)
                scA = ExitStack()
                kb.scope = scA
                wq = kb.alloc("wq", [128, 8, 256], BF16)
                wkv = kb.alloc("wkv", [128, 8, 128], BF16)
                wkr = kb.alloc("wkr", [128, 8, 96], BF16)
                wuv = kb.alloc("wuv", [128, 256], BF16)
                g32 = Ring(kb, "g32", [128, 512], F32, 3)
                S.dma(wq[:], w_in_d[l, :, C_CQ:C_CQ + 256].rearrange("(c p) n -> p c n", p=128), writes=[wq], q="pool")
                S.dma(wkv[:], w_in_d[l, :, C_CKV:C_CKV + 128].rearrange("(c p) n -> p c n", p=128), writes=[wkv], q="pool")
                S.dma(wkr[:], w_krp_d[l].rearrange("(c p) n -> p c n", p=128), writes=[wkr], q="pool")
                S.dma(wuq[:], w_uq_d[l].rearrange("(c p) n -> p c n", p=128), writes=[wuq], q="pool")
                S.dma(wuk[:], w_ukp_d[l], writes=[wuk], q="pool")
                S.dma(wuv[:], w_uv_d[l], writes=[wuv], q="pool")
                kb.memset(vaug[:], 1.0, [vaug])
                if stop == "A0":
                    S.barrier()
                    scA.close()
                    kb.scope = es
                    return
                for tb in range(NTB):
                    sl = slice(tb * 512, (tb + 1) * 512)
                    st = ps.get()
                    gq = []
                    for oc in range(2):
                        pp = ps.get()
                        for c in range(8):
                            kb.mm(pp[:], wq[:, c, oc * 128:(oc + 1) * 128], nT[:, c, sl], c == 0, c == 7, [wq, nT], [pp])
                        if stop == "A1a":
                            S.barrier(); scA.close(); kb.scope = es
                            return
                        g = g32.get()
                        import os as _os
                        if _os.environ.get("VAR", "0") == "4":
                            q4 = sqb.get()
                            kb.act(q4[:], pp[:], AF.Square, [pp], [q4])
                            kb.ts(g[:], pp[:], P("mla_qn", l, oc), ALU.mult, [pp, pv, q4], [g])
                        else:
                            kb.ts(g[:], pp[:], P("mla_qn", l, oc), ALU.mult, [pp, pv], [g])
                        if stop == "A1b":
                            S.barrier(); scA.close(); kb.scope = es
                            return
                        q = sqb.get()
                        import os as _os
                        _v = _os.environ.get("VAR", "0")
                        if _v == "1":
                            kb.act(q[:], g[:], AF.Square, [g], [q])
                        elif _v == "2":
                            kb.copy(q[:], pp[:], [pp], [q], eng="act")
                        elif _v == "3":
                            kb.tt(q[:], pp[:], g[:], ALU.mult, [pp, g], [q])
                        else:
                            kb.act(q[:], pp[:], AF.Square, [pp], [q])
                        if stop == "A1c":
                            S.barrier(); scA.close(); kb.scope = es
                            return
                        kb.mm(st[:], C("ones", bf=True), q[:], oc == 0, oc == 1, [q, cstb], [st])
                        gq.append(g)
                        if stop == "A1d":
                            S.barrier(); scA.close(); kb.scope = es
                            return
                    s1 = sdr.get()
                    kb.act(s1[:], st[:], AF.Sqrt, [st], [s1], scale=1.0 / 256, bias=EPS)
                    kb.recip(s1[:], s1[:], [s1], [s1])
                    for oc in range(2):
                        kb.tt(cqn[:, oc, sl], gq[oc][:], s1[:], ALU.mult, [gq[oc], s1], [cqn])
                    if stop == "A1":
                        S.barrier()
                        scA.close()
                        kb.scope = es
                        return
                    st = ps.get()
                    pp = ps.get()
                    for c in range(8):
                        kb.mm(pp[:], wkv[:, c, :], nT[:, c, sl], c == 0, c == 7, [wkv, nT], [pp])
                    g = g32.get()
                    kb.ts(g[:], pp[:], P("mla_kvn", l, 0), ALU.mult, [pp, pv], [g])
                    q = sqb.get()
                    kb.act(q[:], pp[:], AF.Square, [pp], [q])
                    kb.mm(st[:], C("ones", bf=True), q[:], True, True, [q, cstb], [st])
                    s1 = sdr.get()
                    kb.act(s1[:], st[:], AF.Sqrt, [st], [s1], scale=1.0 / 128, bias=EPS)
                    kb.recip(s1[:], s1[:], [s1], [s1])
                    kb.tt(ckvn[:, sl], g[:], s1[:], ALU.mult, [g, s1], [ckvn])
                    if stop == "A2":
                        S.barrier()
                        scA.close()
                        kb.scope = es
                        return
                    pp = ps.get()
                    for c in range(8):
                        kb.mm(pp[0:96, :], wkr[:, c, :], nT[:, c, sl], c == 0, c == 7, [wkr, nT], [pp])
                    kb.copy(krp[:, sl], pp[0:96, :], [pp], [krp], eng="act")
                    if stop == "A3":
                        S.barrier()
                        scA.close()
                        kb.scope = es
                        return
                    for j in range(4):
                        tt_ = tb * 4 + j
                        pp = ps.get()
                        kb.mm(pp[:, 0:256], ckvn[:, tt_ * 128:(tt_ + 1) * 128], wuv[:], True, True, [ckvn, wuv], [pp])
                        kb.copy(vaug[:, tt_, :, 0:64], pp[:, 0:256].rearrange("p (h d) -> p h d", h=4), [pp], [vaug], eng="act")
                S.barrier()
                scA.close()
                if stop == "A":
                    kb.scope = es
                    return
                with ExitStack() as sc2:
                    kb.scope = sc2
                    rope = kb.alloc("rope", [96, 2, SEQ], F32)
                    S.dma(rope[:], rope_d, writes=[rope])
                    yTr = [kb.alloc("ymla%d" % h, [64, SEQ], BF16) for h in range(2)]
                    yT = [yTr[0], yTr[1], yTr[0], yTr[1]]
                    if dbg is not None and dbg[0] == "mla":
                        tmpf = kb.alloc("dbgf", [64, SEQ], F32)
                    qT = kb.alloc("qT", [96, SEQ], BF16)
                    kT = kb.alloc("kT", [96, SEQ], BF16)
                    pb = Ring(kb, "pb", [128, 512], BF16, 4)
                    osb = Ring(kb, "osb", [65, 512], F32, 2)
                    rcp = Ring(kb, "rcp", [64, 512], F32, 2)
                    qnb = Ring(kb, "qnb", [96, 512], BF16, 2)
                    g96 = Ring(kb, "g96", [96, 512], F32, 3)
                    t32 = Ring(kb, "t32", [96, 512], F32, 3)
                    scale = 96 ** -0.5

                    def normrope(src_ap, src_reads, gname, dst, sl):
                        g = g96.get()
                        kb.ts(g[:], src_ap, P(gname, l, 0, rows=96), ALU.mult, src_reads + [pv], [g])
                        q = sqb.get()
                        kb.act(q[0:96, :], src_ap, AF.Square, src_reads, [q])
                        st = ps.get()
                        kb.mm(st[0:96, :], C("ones", rows=96, bf=True)[:, 0:96], q[0:96, :], True, True, [q, cstb], [st])
                        s1 = sdr.get()
                        kb.act(s1[0:96, :], st[0:96, :], AF.Sqrt, [st], [s1], scale=1.0 / 96, bias=EPS)
                        kb.recip(s1[0:96, :], s1[0:96, :], [s1], [s1])
                        qn = qnb.get()
                        kb.tt(qn[:], g[:], s1[0:96, :], ALU.mult, [g, s1], [qn])
                        rp = ps.get()
                        kb.mm(rp[0:96, :], C("rotm", rows=96, bf=True)[:, 0:96], qn[:], True, True, [qn, cstb], [rp])
                        kb.copy(dst[0:64, sl], qn[0:64, :], [qn], [dst], eng="pool")
                        t1 = t32.get()
                        kb.tt(t1[64:96, :], rp[64:96, :], rope[64:96, 1, sl], ALU.mult, [rp, rope], [t1])
                        t2 = t32.get()
                        kb.tt(t2[64:96, :], qn[64:96, :], rope[64:96, 0, sl], ALU.mult, [qn, rope], [t2])
                        kb.tt(dst[64:96, sl], t1[64:96, :], t2[64:96, :], ALU.add, [t1, t2], [dst])

                    for h in range(4):
                        for tb in range(NTB):
                            sl = slice(tb * 512, (tb + 1) * 512)
                            pp = ps.get()
                            for oc in range(2):
                                kb.mm(pp[0:96, :], wuq[:, oc, h * 96:(h + 1) * 96], cqn[:, oc, sl], oc == 0, oc == 1, [wuq, cqn], [pp])
                            normrope(pp[0:96, :], [pp], "mla_qkq", qT, sl)
                            pp = ps.get()
                            kb.mm(pp[0:96, :], wuk[:, h * 96:(h + 1) * 96], ckvn[:, sl], True, True, [wuk, ckvn], [pp])
                            kp = t32.get()
                            kb.tt(kp[:], pp[0:96, :], krp[:, sl], ALU.add, [pp, krp], [kp])
                            normrope(kp[:], [kp], "mla_qkk", kT, sl)
                        if stop == "B0" or (stop == "B1" and h == 1):
                            S.barrier()
                            kb.scope = es
                            return
                        for qb in range(NTB):
                            o_ps = psl.get()
                            nk = 4 * qb + 4
                            def score(kt):
                                r = kt - 4 * qb
                                q0 = 128 * max(r, 0)
                                n = 512 - q0
                                s_ps = ps.get()
                                kb.mm(s_ps[:, 0:n], kT[:, kt * 128:(kt + 1) * 128], qT[:, qb * 512 + q0:(qb + 1) * 512],
                                      True, True, [kT, qT], [s_ps])
                                return s_ps
                            s_next = score(0)
                            for kt in range(nk):
                                r = kt - 4 * qb
                                q0 = 128 * max(r, 0)
                                n = 512 - q0
                                s_ps = s_next
                                if kt + 1 < nk:
                                    s_next = score(kt + 1)
                                p = pb.get()
                                kb.act(p[:, 0:n], s_ps[:, 0:n], AF.Exp, [s_ps], [p], scale=scale)
                                if r >= 0:
                                    kb.tt(p[:, 0:128], p[:, 0:128], C("tri01", bf=True), ALU.mult, [p, cstb], [p])
                                kb.mm(o_ps[0:65, q0:512], vaug[:, kt, h, 0:65], p[:, 0:n], kt == 0, kt == nk - 1, [vaug, p], [o_ps])
                            o = osb.get()
                            kb.copy(o[:], o_ps[0:65, :], [o_ps], [o], eng="act")
                            bc = ps.get()
                            kb.mm(bc[0:64, :], C("sel65", rows=65)[:, 0:64], o[:], True, True, [cst, o], [bc])
                            rc = rcp.get()
                            kb.recip(rc[:], bc[0:64, :], [bc], [rc])
                            kb.tt(yT[h][:, qb * 512:(qb + 1) * 512], o[0:64, :], rc[:], ALU.mult, [o, rc], [yT[h]])
                            if stop == "B0c":
                                S.barrier(); kb.scope = es
                                return
                        if dbg is not None and dbg[0] == "mla":
                            kb.copy(tmpf[:], yT[h][:], [yT[h]], [tmpf])
                            S.dma(dbg_d[h], tmpf[:], reads=[tmpf])
                        apply_wout(l, [(yT[h], (lambda tb, h=h: yT[h][:, tb * 512:(tb + 1) * 512]), 0 * GW + 64 * h, 64)], sub=True)
                        kb.scope = sc2
                    S.barrier()
            kb.scope = es


        def dilated(l):
            with ExitStack() as sc:
                kb.scope = sc
                acc = kb.alloc("dacc", [65, 4, SEQ], F32)
                kb.memset(acc[:], 0.0, [acc])
                scB = ExitStack()
                kb.scope = scB
                qkv = [kb.alloc("dqkv%d" % i, [128, 2, SEQ], F32) for i in range(3)]
                with ExitStack() as scA:
                    kb.scope = scA
                    wd = [kb.alloc("wd%d" % i, [128, 8, 256], BF16) for i in range(3)]
                    for i, c0 in enumerate((C_DQ, C_DK, C_DV)):
                        S.dma(wd[i][:], w_in_d[l, :, c0:c0 + 256].rearrange("(c p) n -> p c n", p=128), writes=[wd[i]], q="pool")
                    g32 = Ring(kb, "dg32", [128, 512], F32, 2)
                    sqb = Ring(kb, "dsqb", [128, 512], BF16, 2)
                    sdr = Ring(kb, "dsdr", [128, 512], F32, 2)
                    for i in range(3):
                        for pc in range(2):
                            for tb in range(NTB):
                                sl = slice(tb * 512, (tb + 1) * 512)
                                pp = ps.get()
                                for c in range(8):
                                    kb.mm(pp[:], wd[i][:, c, pc * 128:(pc + 1) * 128], nT[:, c, sl], c == 0, c == 7, [wd[i], nT], [pp])
                                if i == 2:
                                    kb.copy(qkv[2][:, pc, sl], pp[:], [pp], [qkv[2]], eng="act")
                                    continue
                                g = g32.get()
                                kb.ts(g[:], pp[:], P("dil_qn" if i == 0 else "dil_kn", l, 0), ALU.mult, [pp, pv], [g])
                                q = sqb.get()
                                kb.act(q[:], pp[:], AF.Square, [pp], [q])
                                st = ps.get()
                                kb.mm(st[:], C("bones", bf=True), q[:], True, True, [q, cstb], [st])
                                s1 = sdr.get()
                                kb.act(s1[:], st[:], AF.Sqrt, [st], [s1], scale=1.0 / 64, bias=EPS)
                                kb.recip(s1[:], s1[:], [s1], [s1])
                                kb.tt(qkv[i][:, pc, sl], g[:], s1[:], ALU.mult, [g, s1], [qkv[i]])
                    S.barrier()
                kb.scope = scB
                bias = kb.alloc("dbias", [128, 12, 256], F32)
                S.dma(bias[:], dilb_d, writes=[bias])
                vsb = Ring(kb, "dvsb", [128, 4, 72], BF16, 2)
                for v_ in vsb.tiles:
                    kb.memset(v_[:], 1.0, [v_])
                tmpr = Ring(kb, "dtmp", [128, 256], F32, 4)
                pbr = Ring(kb, "dpb", [128, 256], BF16, 4)
                scale = 64 ** -0.5
                for bi, (win, dil) in enumerate(((128, 1), (512, 4), (2048, 16))):
                    L = SEQ // dil
                    nb = L // 128
                    for r in range(dil):
                        for n in range(nb):
                            t0_ = r + dil * 128 * n
                            ksl = slice(t0_, t0_ + dil * 127 + 1, dil)
                            nq = 256 if n + 1 < nb else 128
                            qsl = slice(t0_, t0_ + dil * (nq - 1) + 1, dil)
                            vs = vsb.get()
                            for pc in range(2):
                                vp = ps.get()
                                kb.tr(vp[:, 0:128], qkv[2][:, pc, ksl], C("ident"), [qkv[2], cst], [vp])
                                kb.copy(vs[:, 2 * pc:2 * pc + 2, 0:64], vp[:, 0:128].rearrange("p (h d) -> p h d", h=2), [vp], [vs], eng="act")
                            HS = range(4)
                            sps, tms, pbs, ops_ = [None] * 4, [None] * 4, [None] * 4, [None] * 4
                            for h in HS:
                                pc, p0 = h // 2, 64 * (h % 2)
                                sps[h] = ps.get()
                                kb.mm(sps[h][:, 0:nq], qkv[1][p0:p0 + 64, pc, ksl], qkv[0][p0:p0 + 64, pc, qsl], True, True, [qkv[0], qkv[1]], [sps[h]])
                            for h in HS:
                                tms[h] = tmpr.get()
                                kb.stt(tms[h][:, 0:nq], sps[h][:, 0:nq], scale, bias[:, bi * 4 + h, 0:nq], ALU.mult, ALU.add, [sps[h], bias], [tms[h]])
                            for h in HS:
                                pbs[h] = pbr.get()
                                kb.act(pbs[h][:, 0:nq], tms[h][:, 0:nq], AF.Exp, [tms[h]], [pbs[h]])
                            for h in HS:
                                ops_[h] = ps.get()
                                kb.mm(ops_[h][0:65, 0:nq], vs[:, h, 0:65], pbs[h][:, 0:nq], True, True, [vs, pbs[h]], [ops_[h]])
                            for h in HS:
                                kb.tt(acc[:, h, qsl], acc[:, h, qsl], ops_[h][0:65, 0:nq], ALU.add, [acc, ops_[h]], [acc])
                S.barrier()
                scB.close()
                kb.scope = sc
                yTr = [kb.alloc("ydil%d" % h, [64, SEQ], BF16) for h in range(2)]
                rcp = Ring(kb, "drcp", [64, 512], F32, 2)
                if dbg is not None and dbg[0] == "dil":
                    tmpf = kb.alloc("dbgf", [64, SEQ], F32)
                for h in range(4):
                    yT = yTr[h % 2]
                    for tb in range(NTB):
                        sl = slice(tb * 512, (tb + 1) * 512)
                        bc = ps.get()
                        kb.mm(bc[0:64, :], C("sel65", rows=65)[:, 0:64], acc[:, h, sl], True, True, [cst, acc], [bc])
                        rc = rcp.get()
                        kb.recip(rc[:], bc[0:64, :], [bc], [rc])
                        kb.tt(yT[:, sl], acc[0:64, h, sl], rc[:], ALU.mult, [acc, rc], [yT])
                    if dbg is not None and dbg[0] == "dil":
                        kb.copy(tmpf[:], yT[:], [yT], [tmpf])
                        S.dma(dbg_d[64 * h:64 * (h + 1), :], tmpf[:], reads=[tmpf])
                    apply_wout(l, [(yT, (lambda tb, yT=yT: yT[:, tb * 512:(tb + 1) * 512]), 2 * GW + 64 * h, 64)], sub=True)
                    kb.scope = sc
                S.barrier()
            kb.scope = es


        def s5(l):
            I32 = mybir.dt.int32
            TWO_PI = 2.0 * math.pi
            with ExitStack() as sc:
                kb.scope = sc
                uT32 = kb.alloc("uT32", [128, 2, SEQ], F32)
                uTb = kb.alloc("uTb", [128, 2, SEQ], BF16)
                ypre = kb.alloc("ypre", [128, 2, SEQ], BF16)
                wu = kb.alloc("wu", [128, 8, 256], BF16)
                S.dma(wu[:], w_in_d[l, :, C_U:C_U + 256].rearrange("(c p) n -> p c n", p=128), writes=[wu], q="pool")
                bblk = kb.alloc("bblk", [128, 2, 2, 512], BF16)
                S.dma(bblk[:], s5b_d[l], writes=[bblk], q="pool")
                cblk = kb.alloc("cblk", [128, 2, 8, 128], F32)
                S.dma(cblk[:], s5c_d[l], writes=[cblk])
                wglu = kb.alloc("wglu", [128, 2, 512], BF16)
                S.dma(wglu[:], s5g_d[l].rearrange("(c p) n -> p c n", p=128), writes=[wglu], q="pool")
                for oc in range(2):
                    for tb in range(NTB):
                        sl = slice(tb * 512, (tb + 1) * 512)
                        pp = ps.get()
                        for c in range(8):
                            kb.mm(pp[:], wu[:, c, oc * 128:(oc + 1) * 128], nT[:, c, sl], c == 0, c == 7, [wu, nT], [pp])
                        kb.copy(uT32[:, oc, sl], pp[:], [pp], [uT32], eng="act")
                        kb.copy(uTb[:, oc, sl], pp[:], [pp], [uTb])
                sm = {n_: kb.alloc("s5p_" + n_, [128, 8], F32) for n_ in
                      ("dt", "rho", "th", "amag", "c", "s", "ar", "ai", "zr", "zi", "t1", "t2", "t3", "nzi")}
                kint = kb.alloc("s5kint", [128, 512], I32)
                thoff = kb.alloc("thoff", [128, 8, 4], F32)
                lr = pv[:, PV.off["s5_lr"] + l * 8: PV.off["s5_lr"] + l * 8 + 8]
                li = pv[:, PV.off["s5_li"] + l * 8: PV.off["s5_li"] + l * 8 + 8]
                ldt = pv[:, PV.off["s5_ldt"] + l * 8: PV.off["s5_ldt"] + l * 8 + 8]

                def wrap(r_, tmp_, n):
                    kb.ts(tmp_, r_, math.pi, ALU.is_gt, [r_t], [tmp_t])
                    kb.stt(r_, tmp_, -TWO_PI, r_, ALU.mult, ALU.add, [tmp_t, r_t], [r_t])
                    kb.ts(tmp_, r_, -math.pi, ALU.is_lt, [r_t], [tmp_t])
                    kb.stt(r_, tmp_, TWO_PI, r_, ALU.mult, ALU.add, [tmp_t, r_t], [r_t])

                def sincos(ang_ap, ang_t, s_ap, s_t, c_ap, c_t, r_ap, r_T, tmp_ap, tmp_T, ki_ap):
                    nonlocal r_t, tmp_t
                    r_t, tmp_t = r_T, tmp_T
                    kb.ts(tmp_ap, ang_ap, 1.0 / TWO_PI, ALU.mult, [ang_t], [tmp_T])
                    kb.copy(ki_ap, tmp_ap, [tmp_T], [kint])
                    kb.copy(tmp_ap, ki_ap, [kint], [tmp_T])
                    kb.stt(r_ap, tmp_ap, -TWO_PI, ang_ap, ALU.mult, ALU.add, [tmp_T, ang_t], [r_T])
                    wrap(r_ap, tmp_ap, 0)
                    kb.act(s_ap, r_ap, AF.Sin, [r_T], [s_t])
                    kb.ts(r_ap, r_ap, math.pi / 2, ALU.add, [r_T], [r_T])
                    wrap(r_ap, tmp_ap, 0)
                    kb.act(c_ap, r_ap, AF.Sin, [r_T], [c_t])

                r_t = tmp_t = None
                kb.act(sm["dt"][:], ldt, AF.Exp, [pv], [sm["dt"]])
                kb.tt(sm["rho"][:], lr, sm["dt"][:], ALU.mult, [pv, sm["dt"]], [sm["rho"]])
                kb.tt(sm["th"][:], li, sm["dt"][:], ALU.mult, [pv, sm["dt"]], [sm["th"]])
                kb.act(sm["amag"][:], sm["rho"][:], AF.Exp, [sm["rho"]], [sm["amag"]])
                sincos(sm["th"][:], sm["th"], sm["s"][:], sm["s"], sm["c"][:], sm["c"], sm["t1"][:], sm["t1"], sm["t2"][:], sm["t2"], kint[:, 0:8])
                kb.tt(sm["ar"][:], sm["amag"][:], sm["c"][:], ALU.mult, [sm["amag"], sm["c"]], [sm["ar"]])
                kb.tt(sm["ai"][:], sm["amag"][:], sm["s"][:], ALU.mult, [sm["amag"], sm["s"]], [sm["ai"]])
                kb.tt(sm["t1"][:], lr, lr, ALU.mult, [pv], [sm["t1"]])
                kb.tt(sm["t2"][:], li, li, ALU.mult, [pv], [sm["t2"]])
                kb.tt(sm["t1"][:], sm["t1"][:], sm["t2"][:], ALU.add, [sm["t1"], sm["t2"]], [sm["t1"]])
                kb.recip(sm["t1"][:], sm["t1"][:], [sm["t1"]], [sm["t1"]])
                kb.ts(sm["t2"][:], sm["ar"][:], -1.0, ALU.add, [sm["ar"]], [sm["t2"]])
                kb.tt(sm["zr"][:], sm["t2"][:], lr, ALU.mult, [sm["t2"], pv], [sm["zr"]])
                kb.tt(sm["t3"][:], sm["ai"][:], li, ALU.mult, [sm["ai"], pv], [sm["t3"]])
                kb.tt(sm["zr"][:], sm["zr"][:], sm["t3"][:], ALU.add, [sm["zr"], sm["t3"]], [sm["zr"]])
                kb.tt(sm["zr"][:], sm["zr"][:], sm["t1"][:], ALU.mult, [sm["zr"], sm["t1"]], [sm["zr"]])
                kb.tt(sm["zi"][:], sm["ai"][:], lr, ALU.mult, [sm["ai"], pv], [sm["zi"]])
                kb.tt(sm["t3"][:], sm["t2"][:], li, ALU.mult, [sm["t2"], pv], [sm["t3"]])
                kb.tt(sm["zi"][:], sm["zi"][:], sm["t3"][:], ALU.subtract, [sm["zi"], sm["t3"]], [sm["zi"]])
                kb.tt(sm["zi"][:], sm["zi"][:], sm["t1"][:], ALU.mult, [sm["zi"], sm["t1"]], [sm["zi"]])
                kb.ts(sm["nzi"][:], sm["zi"][:], -1.0, ALU.mult, [sm["zi"]], [sm["nzi"]])
                for tb in range(NTB):
                    kb.ts(thoff[:, :, tb], sm["th"][:], float(512 * tb), ALU.mult, [sm["th"]], [thoff])
                cz = kb.alloc("cz", [128, 2, 8, 128], F32)
                ctmp = kb.alloc("ctmp", [128, 128], F32)
                for scn in range(8):
                    kb.ts(ctmp[:], cblk[:, 1, scn, :], sm["nzi"][:, scn:scn + 1], ALU.mult, [cblk, sm["nzi"]], [ctmp])
                    kb.stt(cz[:, 0, scn, :], cblk[:, 0, scn, :], sm["zr"][:, scn:scn + 1], ctmp[:], ALU.mult, ALU.add, [cblk, sm["zr"], ctmp], [cz])
                    kb.ts(ctmp[:], cblk[:, 1, scn, :], sm["zr"][:, scn:scn + 1], ALU.mult, [cblk, sm["zr"]], [ctmp])
                    kb.stt(ctmp[:], cblk[:, 0, scn, :], sm["zi"][:, scn:scn + 1], ctmp[:], ALU.mult, ALU.add, [cblk, sm["zi"], ctmp], [ctmp])
                    kb.ts(cz[:, 1, scn, :], ctmp[:], -1.0, ALU.mult, [ctmp], [cz])
                tl = {n_: kb.alloc("s5t_" + n_, [128, 512], F32) for n_ in ("ang", "r", "tmp", "s", "c", "a1", "a2", "xr", "xi", "hr", "hi")}
                amb = kb.alloc("amb", [128, 512], F32)
                hp = [kb.alloc("s5hp%d" % i, [128, 512], F32) for i in range(2)]
                hlast = kb.alloc("s5hl", [128, 2, 8], F32)
                kb.memset(hlast[:], 0.0, [hlast])
                for tb in range(NTB):
                    sl = slice(tb * 512, (tb + 1) * 512)
                    for scn in range(8):
                        hh = scn // 4
                        if scn % 4 == 0:
                            yps = psl.get()
                        kb.ts(amb[:], C("trow"), 0.0, ALU.mult, [cst], [amb], s2=sm["amag"][:, scn:scn + 1], op1=ALU.add)
                        pr = ps.get()
                        pi_ = ps.get()
                        c0 = (scn % 4) * 128
                        kb.mm(pr[:], bblk[:, hh, 0, c0:c0 + 128], uTb[:, hh, sl], True, True, [bblk, uTb], [pr])
                        kb.mm(pi_[:], bblk[:, hh, 1, c0:c0 + 128], uTb[:, hh, sl], True, True, [bblk, uTb], [pi_])
                        kb.ts(tl["ang"][:], C("trow"), sm["th"][:, scn:scn + 1], ALU.mult, [cst, sm["th"]], [tl["ang"]],
                              s2=thoff[:, scn, tb:tb + 1], op1=ALU.add)
                        sincos(tl["ang"][:], tl["ang"], tl["s"][:], tl["s"], tl["c"][:], tl["c"], tl["r"][:], tl["r"], tl["tmp"][:], tl["tmp"], kint[:])
                        kb.tt(tl["a1"][:], pr[:], tl["c"][:], ALU.mult, [pr, tl["c"]], [tl["a1"]])
                        kb.tt(tl["a2"][:], pi_[:], tl["s"][:], ALU.mult, [pi_, tl["s"]], [tl["a2"]])
                        kb.tt(tl["xr"][:], tl["a1"][:], tl["a2"][:], ALU.add, [tl["a1"], tl["a2"]], [tl["xr"]], eng="pool")
                        kb.tt(tl["a1"][:], pi_[:], tl["c"][:], ALU.mult, [pi_, tl["c"]], [tl["a1"]])
                        kb.tt(tl["a2"][:], pr[:], tl["s"][:], ALU.mult, [pr, tl["s"]], [tl["a2"]])
                        kb.tt(tl["xi"][:], tl["a1"][:], tl["a2"][:], ALU.subtract, [tl["a1"], tl["a2"]], [tl["xi"]], eng="pool")
                        S.op("dve", lambda e, scn=scn: e.tensor_tensor_scan(out=tl["hr"][:], data0=amb[:], data1=tl["xr"][:],
                                                                          initial=hlast[:, 0, scn:scn + 1], op0=ALU.mult, op1=ALU.add),
                             [amb, tl["xr"], hlast], [tl["hr"]])
                        S.op("dve", lambda e, scn=scn: e.tensor_tensor_scan(out=tl["hi"][:], data0=amb[:], data1=tl["xi"][:],
                                                                          initial=hlast[:, 1, scn:scn + 1], op0=ALU.mult, op1=ALU.add),
                             [amb, tl["xi"], hlast], [tl["hi"]])
                        kb.copy(hlast[:, 0, scn:scn + 1], tl["hr"][:, 511:512], [tl["hr"]], [hlast])
                        kb.copy(hlast[:, 1, scn:scn + 1], tl["hi"][:, 511:512], [tl["hi"]], [hlast])
                        kb.tt(tl["a1"][:], tl["hr"][:], tl["c"][:], ALU.mult, [tl["hr"], tl["c"]], [tl["a1"]], eng="pool")
                        kb.tt(tl["a2"][:], tl["hi"][:], tl["s"][:], ALU.mult, [tl["hi"], tl["s"]], [tl["a2"]], eng="pool")
                        kb.tt(hp[0][:], tl["a1"][:], tl["a2"][:], ALU.subtract, [tl["a1"], tl["a2"]], [hp[0]], eng="pool")
                        kb.tt(tl["a1"][:], tl["hr"][:], tl["s"][:], ALU.mult, [tl["hr"], tl["s"]], [tl["a1"]], eng="pool")
                        kb.tt(tl["a2"][:], tl["hi"][:], tl["c"][:], ALU.mult, [tl["hi"], tl["c"]], [tl["a2"]], eng="pool")
                        kb.tt(hp[1][:], tl["a1"][:], tl["a2"][:], ALU.add, [tl["a1"], tl["a2"]], [hp[1]], eng="pool")
                        j = scn % 4
                        kb.mm(yps[:], cz[:, 0, scn, :], hp[0][:], j == 0, False, [cz, hp[0]], [yps])
                        kb.mm(yps[:], cz[:, 1, scn, :], hp[1][:], False, j == 3, [cz, hp[1]], [yps])
                        if j == 3:
                            kb.stt(ypre[:, hh, sl], uT32[:, hh, sl], P("s5_d", l, hh), yps[:], ALU.mult, ALU.add, [uT32, pv, yps], [ypre])
                sg = Ring(kb, "s5sg", [128, 512], F32, 2)
                yT = uTb
                for tb in range(NTB):
                    sl = slice(tb * 512, (tb + 1) * 512)
                    for oc in range(2):
                        vps = ps.get()
                        gps = ps.get()
                        for kc in range(2):
                            kb.mm(vps[:], wglu[:, kc, oc * 128:(oc + 1) * 128], ypre[:, kc, sl], kc == 0, kc == 1, [wglu, ypre], [vps])
                        for kc in range(2):
                            kb.mm(gps[:], wglu[:, kc, 256 + oc * 128:256 + (oc + 1) * 128], ypre[:, kc, sl], kc == 0, kc == 1, [wglu, ypre], [gps])
                        g_ = sg.get()
                        kb.act(g_[:], gps[:], AF.Sigmoid, [gps], [g_])
                        kb.tt(yT[:, oc, sl], vps[:], g_[:], ALU.mult, [vps, g_], [yT])
                if dbg is not None and dbg[0] == "s5":
                    for oc in range(2):
                        kb.copy(uT32[:, oc, :], yT[:, oc, :], [yT], [uT32])
                        S.dma(dbg_d[oc * 128:(oc + 1) * 128, :], uT32[:, oc, :], reads=[uT32])
                S.barrier()
                apply_wout(l, [(yT, (lambda tb, oc=oc: yT[:, oc, tb * 512:(tb + 1) * 512]), GW + 128 * oc, 128) for oc in range(2)], sub=True)
                kb.scope = sc
                S.barrier()
            kb.scope = es


        def deltanet(l):
            with ExitStack() as sc:
                kb.scope = sc
                qkvT = [kb.alloc("nqkv%d" % i, [128, 2, SEQ], F32) for i in range(3)]
                with ExitStack() as scA:
                    kb.scope = scA
                    wdn = kb.alloc("wdn", [128, 8, 768], BF16)
                    S.dma(wdn[:], w_in_d[l, :, C_NQ:C_NQ + 768].rearrange("(c p) n -> p c n", p=128), writes=[wdn], q="pool")
                    xpad = kb.alloc("xpad", [128, SEQ + 3], F32)
                    accv = kb.alloc("accv", [128, SEQ], F32)
                    kb.memset(xpad[:, 0:3], 0.0, [xpad])
                    sqb = Ring(kb, "nsqb", [128, 512], BF16, 2)
                    sdr = Ring(kb, "nsdr", [128, 512], F32, 2)
                    for fc in range(6):
                        i, pc = fc // 2, fc % 2
                        for tb in range(NTB):
                            sl = slice(tb * 512, (tb + 1) * 512)
                            pp = ps.get()
                            for c in range(8):
                                kb.mm(pp[:], wdn[:, c, fc * 128:(fc + 1) * 128], nT[:, c, sl], c == 0, c == 7, [wdn, nT], [pp])
                            kb.copy(xpad[:, 3 + tb * 512:3 + (tb + 1) * 512], pp[:], [pp], [xpad], eng="act")
                        cw = lambda k_: pv[:, PV.off["dn_conv"] + l * 24 + k_ * 6 + fc: PV.off["dn_conv"] + l * 24 + k_ * 6 + fc + 1]
                        kb.ts(accv[:], xpad[:, 0:SEQ], cw(0), ALU.mult, [xpad, pv], [accv])
                        for k_ in range(1, 4):
                            kb.stt(accv[:], xpad[:, k_:k_ + SEQ], cw(k_), accv[:], ALU.mult, ALU.add, [xpad, pv, accv], [accv])
                        if i == 2:
                            kb.act(qkvT[2][:, pc, :], accv[:], AF.Silu, [accv], [qkvT[2]])
                            continue
                        kb.act(accv[:], accv[:], AF.Silu, [accv], [accv])
                        for tb in range(NTB):
                            sl = slice(tb * 512, (tb + 1) * 512)
                            q = sqb.get()
                            kb.act(q[:], accv[:, sl], AF.Square, [accv], [q])
                            st = ps.get()
                            kb.mm(st[:], C("bones", bf=True), q[:], True, True, [q, cstb], [st])
                            s1 = sdr.get()
                            kb.act(s1[:], st[:], AF.Sqrt, [st], [s1], scale=1.0, bias=EPS)
                            kb.recip(s1[:], s1[:], [s1], [s1])
                            if i == 0:
                                kb.stt(qkvT[0][:, pc, sl], accv[:, sl], 0.125, s1[:], ALU.mult, ALU.mult, [accv, s1], [qkvT[0]])
                            else:
                                kb.tt(qkvT[1][:, pc, sl], accv[:, sl], s1[:], ALU.mult, [accv, s1], [qkvT[1]])
                    S.barrier()
                kb.scope = sc
                qT_, kT_, vT_ = qkvT
                if stop == "D1":
                    kb.scope = es
                    return
                wab = kb.alloc("wab", [128, 8, 8], BF16)
                S.dma(wab[:], w_in_d[l, :, C_A:C_A + 8].rearrange("(c p) n -> p c n", p=128), writes=[wab], q="pool")
                wg = kb.alloc("wgate", [128, 8, 256], BF16)
                S.dma(wg[:], w_in_d[l, :, C_G:C_G + 256].rearrange("(c p) n -> p c n", p=128), writes=[wg], q="pool")
                drow = kb.alloc("drow", [128, 192], F32)
                S.dma(drow[:], dnrow_d[l], writes=[drow])
                sc_ = {n_: kb.alloc("dn_" + n_, [128, 64], F32) for n_ in ("a", "b", "beta", "g", "gc", "gl", "eg", "ekd", "dl", "beg", "t")}
                for tt_ in range(NTT):
                    pp = ps.get()
                    for c in range(8):
                        kb.mm(pp[:, 0:8], nT[:, c, tt_ * 128:(tt_ + 1) * 128], wab[:, c, :], c == 0, c == 7, [nT, wab], [pp])
                    kb.copy(sc_["a"][:, tt_ * 4:(tt_ + 1) * 4], pp[:, 0:4], [pp], [sc_["a"]], eng="act")
                    kb.copy(sc_["b"][:, tt_ * 4:(tt_ + 1) * 4], pp[:, 4:8], [pp], [sc_["b"]], eng="act")
                kb.act(sc_["beta"][:], sc_["b"][:], AF.Sigmoid, [sc_["b"]], [sc_["beta"]])
                kb.tt(sc_["t"][:], sc_["a"][:], drow[:, 64:128], ALU.add, [sc_["a"], drow], [sc_["t"]])
                kb.act(sc_["t"][:], sc_["t"][:], AF.Exp, [sc_["t"]], [sc_["t"]])
                kb.act(sc_["t"][:], sc_["t"][:], AF.Ln, [sc_["t"]], [sc_["t"]], scale=1.0, bias=1.0)
                kb.act(sc_["g"][:], drow[:, 0:64], AF.Exp, [drow], [sc_["g"]])
                kb.stt(sc_["g"][:], sc_["g"][:], -1.0, sc_["t"][:], ALU.mult, ALU.mult, [sc_["g"], sc_["t"]], [sc_["g"]])
                pp = ps.get()
                kb.mm(pp[:, 0:64], C("tri"), sc_["g"][:], True, True, [cst, sc_["g"]], [pp])
                kb.copy(sc_["gc"][:], pp[:, 0:64], [pp], [sc_["gc"]])
                pp = ps.get()
                kb.mm(pp[:, 0:64], C("ones"), sc_["g"][:], True, True, [cst, sc_["g"]], [pp])
                kb.copy(sc_["gl"][:], pp[:, 0:64], [pp], [sc_["gl"]])
                kb.act(sc_["eg"][:], sc_["gc"][:], AF.Exp, [sc_["gc"]], [sc_["eg"]])
                kb.tt(sc_["t"][:], sc_["gl"][:], sc_["gc"][:], ALU.subtract, [sc_["gl"], sc_["gc"]], [sc_["t"]])
                kb.act(sc_["ekd"][:], sc_["t"][:], AF.Exp, [sc_["t"]], [sc_["ekd"]])
                kb.act(sc_["dl"][:], sc_["gl"][:], AF.Exp, [sc_["gl"]], [sc_["dl"]])
                kb.tt(sc_["beg"][:], sc_["beta"][:], sc_["eg"][:], ALU.mult, [sc_["beta"], sc_["eg"]], [sc_["beg"]])
                if stop == "D2":
                    S.barrier(); kb.scope = es
                    return
                yTdn = kb.alloc("yTdn", [128, 2, SEQ], BF16)
                scC = ExitStack()
                kb.scope = scC
                Sst = [kb.alloc("Sst%d" % i, [128, 64], F32) for i in range(4)]
                for t_ in Sst:
                    kb.memset(t_[:], 0.0, [t_])
                rwp = [kb.alloc("rwp%d" % i, [128, 128], F32) for i in range(4)]
                kdp = [kb.alloc("kdp%d" % i, [128, 128], F32) for i in range(4)]
                for t_ in rwp + kdp:
                    kb.memset(t_[:], 0.0, [t_])
                sq128 = Ring(kb, "nm", [128, 128], F32, 28)
                ringE = Ring(kb, "nme", [128, 128], F32, 8)
                ringA = Ring(kb, "nma", [128, 128], F32, 4)
                t64 = Ring(kb, "n64", [128, 64], F32, 16)
                tokr = Ring(kb, "ntok", [128, 128], F32, 4)
                otok = Ring(kb, "otok", [128, 256], F32, 2)
                w256 = Ring(kb, "w256", [128, 256], F32, 3)
                r4 = Ring(kb, "r4", [128, 4], F32, 2)
                wTt = [kb.alloc("wTt%d" % i, [128, 128], F32) for i in range(4)]
                for c_ in range(NTT):
                    tsl = slice(c_ * 128, (c_ + 1) * 128)
                    ktok, vtok = [], []
                    for pc in range(2):
                        for (src, lst) in ((kT_, ktok), (vT_, vtok)):
                            tp = ps.get()
                            kb.tr(tp[:, 0:128], src[:, pc, tsl], C("ident"), [src, cst], [tp])
                            tk = tokr.get()
                            kb.copy(tk[:], tp[:, 0:128], [tp], [tk], eng="act")
                            lst.append(tk)
                    ot = otok.get()
                    HS = range(4)
                    idx = [c_ * 4 + h for h in HS]
                    pcs = [h // 2 for h in HS]
                    p0s = [64 * (h % 2) for h in HS]
                    col = lambda t_, h: t_[:, idx[h]:idx[h] + 1]
                    kTh = [kT_[p0s[h]:p0s[h] + 64, pcs[h], tsl] for h in HS]
                    qTh = [qT_[p0s[h]:p0s[h] + 64, pcs[h], tsl] for h in HS]
                    khs = [ktok[pcs[h]][:, p0s[h]:p0s[h] + 64] for h in HS]
                    vhs = [vtok[pcs[h]][:, p0s[h]:p0s[h] + 64] for h in HS]
                    gbc, B_ps, KK_ps, x1, Lm, dtt, QK_ps, aqk, LT_ps, PT, Tt = ([None] * 4 for _ in range(11))
                    for h in HS:
                        gbc[h] = ringE.get()
                        kb.ts(gbc[h][:], C("ones"), col(sc_["g"], h), ALU.mult, [cst, sc_["g"]], [gbc[h]])
                    for h in HS:
                        B_ps[h] = ps.get()
                        kb.mm(B_ps[h][:, 0:128], gbc[h][:], C("tri"), True, True, [gbc[h], cst], [B_ps[h]])
                        kb.mm(B_ps[h][:, 128:256], kTh[h], kTh[h], True, True, [kT_], [B_ps[h]])
                        kb.mm(B_ps[h][:, 256:384], kTh[h], qTh[h], True, True, [kT_, qT_], [B_ps[h]])
                    for h in HS:
                        x1[h] = ringE.get()
                        kb.stt(x1[h][:], B_ps[h][:, 0:128], col(sc_["gc"], h), C("negS"), ALU.subtract, ALU.subtract, [B_ps[h], sc_["gc"], cst], [x1[h]])
                        dtt[h] = ringE.get()
                        kb.stt(dtt[h][:], B_ps[h][:, 0:128], col(sc_["gc"], h), C("negT"), ALU.subtract, ALU.add, [B_ps[h], sc_["gc"], cst], [dtt[h]])
                    for h in HS:
                        kb.act(x1[h][:], x1[h][:], AF.Exp, [x1[h]], [x1[h]], scale=-1.0)
                        kb.act(dtt[h][:], dtt[h][:], AF.Exp, [dtt[h]], [dtt[h]])
                    for h in HS:
                        Lm[h] = sq128.get()
                        kb.stt(Lm[h][:], B_ps[h][:, 128:256], col(sc_["beta"], h), x1[h][:], ALU.mult, ALU.mult, [B_ps[h], sc_["beta"], x1[h]], [Lm[h]])
                        aqk[h] = ringA.get()
                        kb.tt(aqk[h][:], B_ps[h][:, 256:384], dtt[h][:], ALU.mult, [B_ps[h], dtt[h]], [aqk[h]])
                    for h in HS:
                        LT_ps[h] = ps.get()
                        kb.tr(LT_ps[h][:, 0:128], Lm[h][:], C("ident"), [Lm[h], cst], [LT_ps[h]])
                    for h in HS:
                        PT[h] = sq128.get()
                        kb.copy(PT[h][:], LT_ps[h][:, 0:128], [LT_ps[h]], [PT[h]], eng="act")
                        Tt[h] = sq128.get()
                        kb.tt(Tt[h][:], C("ident"), LT_ps[h][:, 0:128], ALU.subtract, [cst, LT_ps[h]], [Tt[h]])
                    Pm = list(Lm)
                    for it in range(6):
                        pq = [None] * 4
                        for h in HS:
                            pq[h] = ps.get()
                            kb.mm(pq[h][:, 0:128], PT[h][:], Pm[h][:], True, True, [PT[h], Pm[h]], [pq[h]])
                            if it < 5:
                                kb.mm(pq[h][:, 128:256], Pm[h][:], PT[h][:], True, True, [PT[h], Pm[h]], [pq[h]])
                        P2 = [None] * 4
                        PT2 = [None] * 4
                        for h in HS:
                            P2[h] = sq128.get()
                            kb.copy(P2[h][:], pq[h][:, 0:128], [pq[h]], [P2[h]], eng="act")
                            if it < 5:
                                PT2[h] = sq128.get()
                                kb.copy(PT2[h][:], pq[h][:, 128:256], [pq[h]], [PT2[h]], eng="act" if h % 2 else "dve")
                        tq = [None] * 4
                        for h in HS:
                            tq[h] = ps.get()
                            kb.mm(tq[h][:, 0:128], P2[h][:], Tt[h][:], True, True, [P2[h], Tt[h]], [tq[h]])
                        for h in HS:
                            Ttn = sq128.get()
                            kb.tt(Ttn[:], Tt[h][:], tq[h][:, 0:128], ALU.add, [Tt[h], tq[h]], [Ttn])
                            Tt[h] = Ttn
                            Pm[h] = P2[h]
                            if it < 5:
                                PT[h] = PT2[h]
                    if stop == "D3":
                        S.barrier(); scC.close(); kb.scope = es
                        return
                    ru, us, vnew, o1 = ([None] * 4 for _ in range(4))
                    for h in HS:
                        p0 = p0s[h]
                        kb.ts(rwp[h][:, p0:p0 + 64], khs[h], col(sc_["beg"], h), ALU.mult, [ktok[pcs[h]], sc_["beg"]], [rwp[h]])
                        ru[h] = t64.get()
                        kb.ts(ru[h][:], vhs[h], col(sc_["beta"], h), ALU.mult, [vtok[pcs[h]], sc_["beta"]], [ru[h]])
                        kb.ts(kdp[h][:, p0:p0 + 64], khs[h], col(sc_["ekd"], h), ALU.mult, [ktok[pcs[h]], sc_["ekd"]], [kdp[h]])
                    wu_ps = [None] * 4
                    for h in HS:
                        wu_ps[h] = ps.get()
                        kb.mm(wu_ps[h][:, 0:128], rwp[h][:], Tt[h][:], True, True, [rwp[h], Tt[h]], [wu_ps[h]])
                        kb.mm(wu_ps[h][:, 128:192], Tt[h][:], ru[h][:], True, True, [Tt[h], ru[h]], [wu_ps[h]])
                    for h in HS:
                        p0 = p0s[h]
                        kb.copy(wTt[h][p0:p0 + 64, :], wu_ps[h][p0:p0 + 64, 0:128], [wu_ps[h]], [wTt[h]], eng="act")
                        us[h] = t64.get()
                        kb.copy(us[h][:], wu_ps[h][:, 128:192], [wu_ps[h]], [us[h]])
                    sq_ps = [None] * 4
                    for h in HS:
                        p0 = p0s[h]
                        sq_ps[h] = ps.get()
                        kb.mm(sq_ps[h][:, 0:64], wTt[h][p0:p0 + 64, :], Sst[h][p0:p0 + 64, :], True, True, [wTt[h], Sst[h]], [sq_ps[h]])
                        kb.mm(sq_ps[h][:, 64:128], qTh[h], Sst[h][p0:p0 + 64, :], True, True, [qT_, Sst[h]], [sq_ps[h]])
                    for h in HS:
                        vnew[h] = t64.get()
                        kb.tt(vnew[h][:], us[h][:], sq_ps[h][:, 0:64], ALU.subtract, [us[h], sq_ps[h]], [vnew[h]])
                        o1[h] = t64.get()
                        kb.ts(o1[h][:], sq_ps[h][:, 64:128], col(sc_["eg"], h), ALU.mult, [sq_ps[h], sc_["eg"]], [o1[h]])
                    ak_ps = [None] * 4
                    for h in HS:
                        ak_ps[h] = ps.get()
                        kb.mm(ak_ps[h][:, 0:64], aqk[h][:], vnew[h][:], True, True, [aqk[h], vnew[h]], [ak_ps[h]])
                        kb.mm(ak_ps[h][:, 64:128], kdp[h][:], vnew[h][:], True, True, [kdp[h], vnew[h]], [ak_ps[h]])
                    for h in HS:
                        p0 = p0s[h]
                        kb.tt(ot[:, h * 64:(h + 1) * 64], o1[h][:], ak_ps[h][:, 0:64], ALU.add, [o1[h], ak_ps[h]], [ot])
                        kb.stt(Sst[h][p0:p0 + 64, :], Sst[h][p0:p0 + 64, :], sc_["dl"][p0:p0 + 64, idx[h]:idx[h] + 1], ak_ps[h][p0:p0 + 64, 64:128],
                               ALU.mult, ALU.add, [Sst[h], sc_["dl"], ak_ps[h]], [Sst[h]])
                    if stop == "D4":
                        S.barrier(); scC.close(); kb.scope = es
                        return
                    g_ps = ps.get()
                    for c in range(8):
                        kb.mm(g_ps[:, 0:256], nT[:, c, tsl], wg[:, c, :], c == 0, c == 7, [nT, wg], [g_ps])
                    sgt = w256.get()
                    kb.act(sgt[:], g_ps[:, 0:256], AF.Silu, [g_ps], [sgt])
                    sq_ = w256.get()
                    kb.act(sq_[:], ot[:], AF.Square, [ot], [sq_])
                    ss = r4.get()
                    S.op("dve", lambda e, ss=ss, sq_=sq_: e.tensor_reduce(out=ss[:], in_=sq_[:].rearrange("p (h d) -> p h d", h=4),
                                                                        axis=mybir.AxisListType.X, op=ALU.add), [sq_], [ss])
                    kb.act(ss[:], ss[:], AF.Sqrt, [ss], [ss], scale=1.0 / 64, bias=EPS)
                    kb.recip(ss[:], ss[:], [ss], [ss])
                    yt = w256.get()
                    for h in range(4):
                        kb.stt(yt[:, h * 64:(h + 1) * 64], ot[:, h * 64:(h + 1) * 64], ss[:, h:h + 1], drow[:, 128:192], ALU.mult, ALU.mult,
                               [ot, ss, drow], [yt])
                    kb.tt(yt[:], yt[:], sgt[:], ALU.mult, [yt, sgt], [yt])
                    for pc in range(2):
                        tp = ps.get()
                        kb.tr(tp[:, 0:128], yt[:, pc * 128:(pc + 1) * 128], C("ident"), [yt, cst], [tp])
                        kb.copy(yTdn[:, pc, tsl], tp[:, 0:128], [tp], [yTdn], eng="act")
                    if stop == "D5" or (stop is not None and stop[0] == "T" and int(stop[1:]) == c_):
                        S.barrier(); scC.close(); kb.scope = es
                        return
                S.barrier()
                scC.close()
                kb.scope = sc
                if dbg is not None and dbg[0] == "dn":
                    for pc in range(2):
                        kb.copy(qkvT[0][:, pc, :], yTdn[:, pc, :], [yTdn], [qkvT[0]])
                        S.dma(dbg_d[pc * 128:(pc + 1) * 128, :], qkvT[0][:, pc, :], reads=[qkvT[0]])
                S.barrier()
                apply_wout(l, [(yTdn, (lambda tb, pc=pc: yTdn[:, pc, tb * 512:(tb + 1) * 512]), 3 * GW + 128 * pc, 128) for pc in range(2)], sub=True)
                kb.scope = sc
                S.barrier()
            kb.scope = es

        def ffn_block(l):
            HC = FFN // 128
            with ExitStack() as sc:
                kb.scope = sc
                w13 = Ring(kb, "w13", [128, 8, 256], BF16, 3)
                w2t = Ring(kb, "w2t", [128, HC, 128], BF16, 2)
                gT = kb.alloc("gT", [128, HC, 1024], BF16)
                sg = Ring(kb, "sg", [128, 512], F32, 3)
                for half in range(2):
                    t0 = half * 1024
                    for hc in range(HC):
                        w = w13.get()
                        S.dma(w[:, :, 0:128], w1_d[l, hc], writes=[w], q="pool")
                        S.dma(w[:, :, 128:256], w3_d[l, hc], writes=[w], q="pool")
                        for tb in range(2):
                            sl = slice(t0 + tb * 512, t0 + (tb + 1) * 512)
                            a = ps.get()
                            b = ps.get()
                            for c in range(8):
                                kb.mm(a[:], w[:, c, 0:128], nT[:, c, sl], c == 0, c == 7, [w, nT], [a])
                            for c in range(8):
                                kb.mm(b[:], w[:, c, 128:256], nT[:, c, sl], c == 0, c == 7, [w, nT], [b])
                            s = sg.get()
                            kb.act(s[:], a[:], AF.Silu, [a], [s])
                            kb.tt(gT[:, hc, tb * 512:(tb + 1) * 512], s[:], b[:], ALU.mult, [s, b], [gT])
                    for oc in range(8):
                        accs = [ps.get(), ps.get()]
                        w = w2t.get()
                        S.dma(w[:], w2_d[l, oc], writes=[w], q="pool")
                        for hc in range(HC):
                            for tb in range(2):
                                kb.mm(accs[tb][:], w[:, hc, :], gT[:, hc, tb * 512:(tb + 1) * 512], hc == 0, hc == HC - 1, [w, gT], [accs[tb]])
                        for tb in range(2):
                            sl = slice(t0 + tb * 512, t0 + (tb + 1) * 512)
                            kb.tt(hT[:, oc, sl], hT[:, oc, sl], accs[tb][:], ALU.add, [hT, accs[tb]], [hT])
                S.barrier()
            kb.scope = es

        for l in range(nlayers):
            stream_norm("attn_norm", l)
            if stop == "N":
                break
            if "mla" in mixers:
                mla(l)
                kb.scope = es
            if "s5" in mixers:
                s5(l)
            if "dil" in mixers:
                dilated(l)
            if "dn" in mixers:
                deltanet(l)
                kb.scope = es
            if ffn:
                stream_norm("ffn_norm", l)
                ffn_block(l)

        for c in range(8):
            S.dma(yT_d[:, c, :], hT[:, c, :], reads=[hT], q=("sp" if c % 2 == 0 else "pool"))
        S.finish("sp")
        S.emit()
    return nc


class _Lay:
    def __init__(self):
        self.off = {}
        self.per = {}
        self.n = 0

    def add(self, name, n, per=None):
        self.off[name] = (self.n, n) if per is None else self.n
        if per is not None:
            self.per[name] = per
        self.n += n


PV = _Lay()
for _nm, _per in (("attn_norm", 8), ("ffn_norm", 8), ("mla_qn", 2), ("mla_kvn", 1), ("mla_qkq", 1), ("mla_qkk", 1), ("dil_qn", 1), ("dil_kn", 1), ("s5_lr", 8), ("s5_li", 8), ("s5_ldt", 8), ("s5_d", 2), ("dn_conv", 24)):
    PV.add(_nm, NL * _per, per=_per)

CS = _Lay()
for _nm, _n in (("ones", 128), ("ident", 128), ("tri01", 128), ("rotm", 96), ("sel65", 64), ("bones", 128), ("trow", 512), ("tri", 128), ("negT", 128), ("negS", 128)):
    CS.add(_nm, _n)


def _consts():
    c = np.zeros((128, CS.n), np.float32)

    def put(name, m):
        o, n = CS.off[name]
        c[:m.shape[0], o:o + m.shape[1]] = m
    put("ones", np.ones((128, 128), np.float32))
    put("ident", np.eye(128, dtype=np.float32))
    k = np.arange(128)[:, None]
    q = np.arange(128)[None, :]
    put("tri01", (q >= k).astype(np.float32))
    rot = np.zeros((96, 96), np.float32)
    for m in range(64, 80):
        rot[m + 16, m] = -1.0
    for m in range(80, 96):
        rot[m - 16, m] = 1.0
    put("rotm", rot)
    sel = np.zeros((65, 64), np.float32)
    sel[64, :] = 1.0
    put("sel65", sel)
    bo = np.zeros((128, 128), np.float32)
    bo[0:64, 0:64] = 1.0
    bo[64:128, 64:128] = 1.0
    put("bones", bo)
    pi_ = np.arange(128)[:, None]
    fi_ = np.arange(128)[None, :]
    put("tri", (pi_ <= fi_).astype(np.float32))
    put("negT", np.where(fi_ >= pi_, 0.0, NEG).astype(np.float32))
    put("negS", np.where(pi_ > fi_, 0.0, NEG).astype(np.float32))
    put("trow", np.tile(np.arange(512, dtype=np.float32)[None, :], (128, 1)))
    return c


def _rope_tables():
    half = 16
    freqs = (10000.0 ** (-np.arange(half, dtype=np.float32) / half)).astype(np.float32)
    pos = np.arange(SEQ, dtype=np.float32)
    ang = pos[None, :] * freqs[:, None]
    tab = np.zeros((96, 2, SEQ), np.float32)
    tab[0:64, 0, :] = 1.0
    tab[64:80, 0, :] = np.cos(ang)
    tab[80:96, 0, :] = np.cos(ang)
    tab[64:80, 1, :] = np.sin(ang)
    tab[80:96, 1, :] = np.sin(ang)
    return tab


def _pvec(inp):
    pvv = np.zeros((128, PV.n), np.float32)

    def put(name, l, arr):
        o = PV.off[name] + l * PV.per[name]
        a = np.asarray(arr, np.float32)
        if a.shape[0] <= 128:
            pvv[:a.shape[0], o] = a
        else:
            k = a.shape[0] // 128
            pvv[:, o:o + k] = a.reshape(k, 128).T
    for l in range(NL):
        put("attn_norm", l, inp["attn_norm"][l])
        put("ffn_norm", l, inp["ffn_norm"][l])
        put("mla_qn", l, inp["mla_q_norm"][l])
        put("mla_kvn", l, inp["mla_kv_norm"][l])
        put("mla_qkq", l, inp["mla_qk_q"][l])
        put("mla_qkk", l, inp["mla_qk_k"][l])
        put("dil_qn", l, np.tile(np.asarray(inp["dil_q_norm"][l], np.float32), 2))
        put("dil_kn", l, np.tile(np.asarray(inp["dil_k_norm"][l], np.float32), 2))
        put("s5_lr", l, np.asarray(inp["s5_lambda_re"][l], np.float32).reshape(-1))
        put("s5_li", l, np.asarray(inp["s5_lambda_im"][l], np.float32).reshape(-1))
        put("s5_ldt", l, np.repeat(np.asarray(inp["s5_log_dt"][l], np.float32), 64))
        put("s5_d", l, inp["s5_d"][l])
        cw = np.asarray(inp["dn_conv"][l], np.float32)
        o_ = PV.off["dn_conv"] + l * 24
        for k_ in range(4):
            pvv[:, o_ + k_ * 6:o_ + k_ * 6 + 6] = cw[k_].reshape(6, 128).T
    return pvv


def _t5_bucket_np(dist):
    exact = 16
    df = np.maximum(dist, 1).astype(np.float32)
    large = exact + (np.log(df / np.float32(exact)) / np.float32(math.log(2048 / exact)) * np.float32(32 - exact)).astype(np.int32)
    large = np.minimum(large, 31)
    return np.where(dist < exact, dist, large)


def _dil_bias(t5):
    t5 = np.asarray(t5, np.float32)
    k = np.arange(128)[:, None]
    q = np.arange(256)[None, :]
    delta = q - k
    valid = (delta >= 0) & (delta <= 128)
    out = np.zeros((128, 12, 256), np.float32)
    for bi, dil in enumerate((1, 4, 16)):
        bucket = _t5_bucket_np(np.clip(delta, 0, 128) * dil)
        for h in range(4):
            out[:, bi * 4 + h, :] = np.where(valid, t5[bucket, h], np.float32(NEG))
    return out


def _s5_blocks(inp):
    bre, bim = np.asarray(inp["s5_b_re"], np.float32), np.asarray(inp["s5_b_im"], np.float32)
    cre, cim = np.asarray(inp["s5_c_re"], np.float32), np.asarray(inp["s5_c_im"], np.float32)
    b = np.zeros((NL, 128, 2, 2, 512), np.float32)
    c = np.zeros((NL, 128, 2, 8, 128), np.float32)
    for g in range(16):
        hh, gl = g // 8, g % 8
        for ri, (bb, cc) in enumerate(((bre, cre), (bim, cim))):
            b[:, gl * 16:(gl + 1) * 16, hh, ri, gl * 64:(gl + 1) * 64] = bb[:, g].transpose(0, 2, 1)
            c[:, (g % 2) * 64:(g % 2) * 64 + 64, ri, g // 2, gl * 16:(gl + 1) * 16] = cc[:, g].transpose(0, 2, 1)
    return b, c


def _dn_rows(inp):
    out = np.zeros((NL, 128, 192), np.float32)
    for l in range(NL):
        out[l, :, 0:64] = np.tile(np.asarray(inp["dn_a_log"][l], np.float32), 16)[None, :]
        out[l, :, 64:128] = np.tile(np.asarray(inp["dn_dt_bias"][l], np.float32), 16)[None, :]
        out[l, :, 128:192] = np.asarray(inp["dn_o_norm"][l], np.float32)[None, :]
    return out


def prep_shared(inp):
    f = lambda a: np.ascontiguousarray(np.asarray(a, np.float32))
    w_ukv = f(inp["mla_w_ukv"])
    w_ukp = np.zeros((NL, 128, 4 * 96), np.float32)
    w_uv = np.zeros((NL, 128, 256), np.float32)
    for h in range(4):
        w_ukp[:, :, h * 96:h * 96 + 64] = w_ukv[:, :, h * 128:h * 128 + 64]
        w_uv[:, :, h * 64:(h + 1) * 64] = w_ukv[:, :, h * 128 + 64:h * 128 + 128]
    w_in = f(inp["w_in"])
    w_krp = np.zeros((NL, D, 96), np.float32)
    w_krp[:, :, 64:96] = w_in[:, :, C_KR:C_KR + 32]
    return {
        "w_in": w_in, "w_out": f(inp["w_out"]), "pvec": _pvec(inp), "cst128": _consts(),
        "w_uq": f(inp["mla_w_uq"]), "w_ukp": w_ukp, "w_uv": w_uv, "w_krp": w_krp, "rope": _rope_tables(), "dil_bias": _dil_bias(inp["t5_bias"]), "dn_row": _dn_rows(inp),
        "s5_b": _s5_blocks(inp)[0], "s5_c": _s5_blocks(inp)[1], "s5_w_glu": f(inp["s5_w_glu"]),
        "ffn_w1": np.ascontiguousarray(f(inp["ffn_w1"]).reshape(NL, 8, 128, FFN // 128, 128).transpose(0, 3, 2, 1, 4)),
        "ffn_w3": np.ascontiguousarray(f(inp["ffn_w3"]).reshape(NL, 8, 128, FFN // 128, 128).transpose(0, 3, 2, 1, 4)),
        "ffn_w2": np.ascontiguousarray(f(inp["ffn_w2"]).reshape(NL, FFN // 128, 128, 8, 128).transpose(0, 3, 2, 1, 4)),
    }


def x_to_core(xb):
    return np.ascontiguousarray(xb.T.reshape(8, 128, SEQ).transpose(1, 0, 2))


def core_to_y(yT):
    return np.ascontiguousarray(yT.transpose(1, 0, 2).reshape(D, SEQ).T)


_NC_CACHE = {}


def kernel(**inputs):
    x = np.asarray(inputs["x"], np.float32)
    shared = prep_shared(inputs)
    if "nc" not in _NC_CACHE:
        _NC_CACHE["nc"] = build()
    nc = _NC_CACHE["nc"]
    in_maps = []
    for b in range(8):
        m = dict(shared)
        m["xT"] = x_to_core(x[b])
        in_maps.append(m)
    res = run_bass_kernel_spmd(nc, in_maps, core_ids=list(range(8)))
    out = np.stack([core_to_y(np.asarray(r["yT"])) for r in res.results], axis=0)
    return out.astype(np.float32)
```
